# Optimizing a Trainium2 kernel written in Bass

```python
import math
import jax, jax.numpy as jnp
from jax import lax
import numpy as np

D_MODEL = 1024
BATCH = 8
SEQ = 2048
DEPTH = 2
DEC_BATCH = 128
DEC_SEQ = 4
PAST_LEN = 2048
PAGE_SIZE = 128

CONV_W = 4
DN_HEADS = 8
DN_DK = 128
DN_DV = 128
DN_WIDTH = DN_HEADS * DN_DK
DN_CHUNK = 64
DIL_GROUPS = ((128, 1), (512, 4), (2048, 16))
DIL_HPG = 4
DIL_HEAD_DIM = 128
DIL_HEADS = DIL_HPG * len(DIL_GROUPS)
DIL_WIDTH = DIL_HEADS * DIL_HEAD_DIM
DIL_OUT = DIL_HPG * DIL_HEAD_DIM
ROT_DIM = DIL_HEAD_DIM // 4
ROPE_THETA = 500000.0
SSM_HEADS = 16
SSM_P = 64
SSM_N = 128
SSM_GROUPS = 4
SSM_HPG = SSM_HEADS // SSM_GROUPS
SSM_INNER = SSM_HEADS * SSM_P
SSM_CONV_DIM = SSM_INNER + 2 * SSM_GROUPS * SSM_N
SSM_CHUNK = 64
MEM_TOKENS = 256
MEM_HEADS = 4
MEM_HEAD_DIM = 128
MEM_WIDTH = MEM_HEADS * MEM_HEAD_DIM
D_FF = 4 * D_MODEL
N_BRANCH = 3
IN_SIZES = (3 * DN_WIDTH, DN_HEADS * DN_DV, DN_HEADS, DN_HEADS, 3 * DIL_WIDTH, SSM_INNER, SSM_CONV_DIM, SSM_HEADS, N_BRANCH * D_MODEL)
N_IN = sum(IN_SIZES)
EPS = 1e-6
NEG_INF = -1e30

kernel_name = 'hybrid_deltanet_dilated_ssd_decoder_step'


def rms(x):
    xf = x.astype(jnp.float32)
    return xf * lax.rsqrt(jnp.mean(xf * xf, axis=-1, keepdims=True) + EPS)


def rmsnorm(x, g):
    return (rms(x) * g.astype(jnp.float32)).astype(x.dtype)


def l2norm(x):
    return x * lax.rsqrt(jnp.sum(x * x, axis=-1, keepdims=True) + EPS)


def split_columns(proj):
    idx = [int(i) for i in np.cumsum(IN_SIZES)[:-1]]
    return jnp.split(proj, idx, axis=-1)


def causal_conv(u, buf, w, bias=None):
    L = u.shape[1]
    full = jnp.concatenate([buf.astype(u.dtype), u], axis=1)
    y = full[:, 0:L] * w[0]
    for j in range(1, CONV_W):
        y = y + full[:, j:j + L] * w[j]
    if bias is not None:
        y = y + bias
    return jax.nn.silu(y), full[:, L:]


def to_chunks(t, c):
    b, L = t.shape[:2]
    pad = (-L) % c
    t = jnp.pad(t, [(0, 0), (0, pad)] + [(0, 0)] * (t.ndim - 2))
    t = t.reshape((b, (L + pad) // c, c) + t.shape[2:])
    return jnp.moveaxis(t, 1, 0)


def from_chunks(t, L):
    t = jnp.moveaxis(t, 0, 1)
    t = t.reshape((t.shape[0], -1) + t.shape[3:])
    return t[:, :L]


def gated_delta_rule(q, k, v, beta, g, s0):
    L = q.shape[1]
    c = min(DN_CHUNK, L)
    tri = jnp.tril(jnp.ones((c, c), bool))
    strict = jnp.tril(jnp.ones((c, c), bool), -1)
    eye = jnp.eye(c, dtype=jnp.float32)

    def step(S, inp):
        qc, kc, vc, bc, gc = inp
        gcum = jnp.cumsum(gc, axis=1)
        gh = jnp.moveaxis(gcum, 1, -1)
        diff = gh[..., :, None] - gh[..., None, :]
        decay = jnp.where(tri, jnp.exp(jnp.where(tri, diff, 0.0)), 0.0)
        kb = kc * bc[..., None]
        a_mat = jnp.einsum('bihd,bjhd->bhij', kb, kc) * jnp.where(strict, decay, 0.0)
        rhs = jnp.concatenate([vc * bc[..., None], kb * jnp.exp(gcum)[..., None]], axis=-1)
        rhs = jnp.moveaxis(rhs, 1, 2)
        sol = lax.linalg.triangular_solve(eye + a_mat, rhs, left_side=True, lower=True)
        u, w = sol[..., :DN_DV], sol[..., DN_DV:]
        v_new = u - jnp.einsum('bhcd,bhde->bhce', w, S)
        q_dec = jnp.moveaxis(qc * jnp.exp(gcum)[..., None], 1, 2)
        scores = jnp.einsum('bihd,bjhd->bhij', qc, kc) * decay
        o = jnp.einsum('bhcd,bhde->bhce', q_dec, S) + jnp.einsum('bhij,bhje->bhie', scores, v_new)
        g_last = gh[..., -1]
        k_dec = jnp.moveaxis(kc, 1, 2) * jnp.exp(g_last[..., None] - gh)[..., None]
        S = S * jnp.exp(g_last)[..., None, None] + jnp.einsum('bhcd,bhce->bhde', k_dec, v_new)
        return S, jnp.moveaxis(o, 1, 2)

    xs = tuple(to_chunks(t, c) for t in (q, k, v, beta, g))
    s_fin, o = lax.scan(step, s0, xs)
    return from_chunks(o, L), s_fin


def ssd_chunked(x, bm, cm, dt, a_neg, h0):
    L = x.shape[1]
    c = min(SSM_CHUNK, L)
    tri = jnp.tril(jnp.ones((c, c), bool))

    def step(h, inp):
        xc, bc, cc, dc = inp
        acum = jnp.cumsum(dc * a_neg, axis=1)
        ah = jnp.moveaxis(acum, 1, -1)
        diff = ah[..., :, None] - ah[..., None, :]
        lmat = jnp.where(tri, jnp.exp(jnp.where(tri, diff, 0.0)), 0.0)
        cb = jnp.einsum('bign,bjgn->bgij', cc, bc)
        wmat = cb[:, :, None] * lmat * jnp.moveaxis(dc, 1, -1)[..., None, :]
        y = jnp.einsum('bghij,bjghp->bighp', wmat, xc)
        y = y + jnp.einsum('bign,bghpn->bighp', cc, h) * jnp.exp(acum)[..., None]
        a_last = ah[..., -1]
        wdec = dc * jnp.exp(jnp.moveaxis(a_last[..., None] - ah, -1, 1))
        h = h * jnp.exp(a_last)[..., None, None] + jnp.einsum('bjgn,bjgh,bjghp->bghpn', bc, wdec, xc)
        return h, y

    xs = tuple(to_chunks(t, c) for t in (x, bm, cm, dt))
    h_fin, y = lax.scan(step, h0, xs)
    return from_chunks(y, L), h_fin


def partial_rope(t, pos):
    half = ROT_DIM // 2
    inv = jnp.power(ROPE_THETA, -jnp.arange(half, dtype=jnp.float32) * 2.0 / ROT_DIM)
    ang = pos[:, None] * inv[None, :]
    cos = jnp.cos(ang)[None, :, None, :]
    sin = jnp.sin(ang)[None, :, None, :]
    t1 = t[..., :half]
    t2 = t[..., half:ROT_DIM]
    return jnp.concatenate([t1 * cos - t2 * sin, t2 * cos + t1 * sin, t[..., ROT_DIM:]], axis=-1)


def dilated_attn_prompt(q, k, v, window, dil):
    b, S, h, e = q.shape
    span = window // dil
    M = S // dil
    nb = -(-M // span)
    mp = nb * span

    def by_residue(t):
        t = t.reshape(b, M, dil, h, e).transpose(0, 2, 1, 3, 4)
        return jnp.pad(t, ((0, 0), (0, 0), (0, mp - M), (0, 0), (0, 0)))

    def band(t):
        tp = jnp.pad(t, ((0, 0), (0, 0), (span, 0), (0, 0), (0, 0)))
        prev = tp[:, :, :mp].reshape(b, dil, nb, span, h, e)
        cur = tp[:, :, span:].reshape(b, dil, nb, span, h, e)
        return jnp.concatenate([prev, cur], axis=3)

    qb = by_residue(q).reshape(b, dil, nb, span, h, e)
    kw = band(by_residue(k))
    vw = band(by_residue(v))
    s = jnp.einsum('brnqhe,brnkhe->brnhqk', qb, kw) * (e ** -0.5)
    qi = jnp.arange(span)[:, None]
    kl = jnp.arange(2 * span)[None, :]
    dist = qi + span - kl
    blk = jnp.arange(nb)[:, None, None]
    valid = (dist >= 0) & (dist <= span) & (blk * span + kl - span >= 0)
    s = jnp.where(valid[:, None], s, NEG_INF)
    lse = jax.nn.logsumexp(s, axis=-1)
    p = jnp.exp(s - lse[..., None])
    o = jnp.einsum('brnhqk,brnkhe->brnqhe', p, vw)

    def back(t):
        t = t.reshape((b, dil, mp) + t.shape[4:])[:, :, :M]
        t = jnp.moveaxis(t, 1, 2)
        return t.reshape((b, S) + t.shape[3:])

    return back(o), back(jnp.moveaxis(lse, -1, 3))


def dilated_attn_sample(q, k, v, kbuf, vbuf, window, dil):
    T = q.shape[1]
    e = q.shape[-1]
    Lb = kbuf.shape[1]
    kall = jnp.concatenate([kbuf.astype(jnp.float32), k], axis=1)
    vall = jnp.concatenate([vbuf.astype(jnp.float32), v], axis=1)
    idx = Lb + jnp.arange(T)[:, None] - dil * jnp.arange(window // dil + 1)[None, :]
    valid = idx >= 0
    idx = jnp.maximum(idx, 0)
    kg = kall[:, idx]
    vg = vall[:, idx]
    s = jnp.einsum('bthe,btjhe->bthj', q, kg) * (e ** -0.5)
    s = jnp.where(valid[:, None, :], s, NEG_INF)
    lse = jax.nn.logsumexp(s, axis=-1)
    p = jnp.exp(s - lse[..., None])
    return jnp.einsum('bthj,btjhe->bthe', p, vg), lse


def trunk_layer(x, pos, mem_k, mem_v, dn_conv, dn_state, ssm_conv, ssm_state, win, p):
    f32 = jnp.float32
    b, L, _ = x.shape
    dt_ = x.dtype
    h = rmsnorm(x, p['norm_mix_pre'])
    (dn_qkv, dn_z, dn_b, dn_a, dil_qkv, ssm_z, ssm_xbc, ssm_dt, gates) = split_columns(h @ p['w_in'])

    qkv, dn_conv_new = causal_conv(dn_qkv, dn_conv, p['dn_conv_w'])
    qkv = qkv.astype(f32).reshape(b, L, 3, DN_HEADS, DN_DK)
    q = l2norm(qkv[:, :, 0]) * (DN_DK ** -0.5)
    k = l2norm(qkv[:, :, 1])
    v = qkv[:, :, 2]
    beta = jax.nn.sigmoid(dn_b.astype(f32))
    g = -jnp.exp(p['dn_a_log'].astype(f32)) * jax.nn.softplus(dn_a.astype(f32) + p['dn_dt_bias'].astype(f32))
    o, dn_state_new = gated_delta_rule(q, k, v, beta, g, dn_state.astype(f32))
    z = dn_z.astype(f32).reshape(b, L, DN_HEADS, DN_DV)
    o_dn = (rmsnorm(o, p['dn_norm']) * jax.nn.silu(z)).reshape(b, L, DN_WIDTH).astype(dt_)

    qkv = dil_qkv.astype(f32).reshape(b, L, 3, DIL_HEADS, DIL_HEAD_DIM)
    q = partial_rope(qkv[:, :, 0], pos)
    k = partial_rope(qkv[:, :, 1], pos)
    v = qkv[:, :, 2]
    outs, lses, win_new = [], [], []
    for gi, (window, dil) in enumerate(DIL_GROUPS):
        hs = slice(gi * DIL_HPG, (gi + 1) * DIL_HPG)
        qg, kg, vg = q[:, :, hs], k[:, :, hs], v[:, :, hs]
        if win is None:
            o, lse = dilated_attn_prompt(qg, kg, vg, window, dil)
            keep = min(window, L)
            win_new += [kg[:, L - keep:], vg[:, L - keep:]]
        else:
            o, lse = dilated_attn_sample(qg, kg, vg, win[2 * gi], win[2 * gi + 1], window, dil)
            win_new += [kg, vg]
        outs.append(o)
        lses.append(lse)
    wts = jax.nn.softmax(jnp.stack(lses), axis=0)
    o_dil = jnp.einsum('gblh,gblhe->blhe', wts, jnp.stack(outs)).reshape(b, L, DIL_OUT).astype(dt_)

    xbc, ssm_conv_new = causal_conv(ssm_xbc, ssm_conv, p['ssm_conv_w'], p['ssm_conv_b'])
    xbc = xbc.astype(f32)
    gn = SSM_GROUPS * SSM_N
    xs = xbc[..., :SSM_INNER].reshape(b, L, SSM_GROUPS, SSM_HPG, SSM_P)
    bm = xbc[..., SSM_INNER:SSM_INNER + gn].reshape(b, L, SSM_GROUPS, SSM_N)
    cm = xbc[..., SSM_INNER + gn:].reshape(b, L, SSM_GROUPS, SSM_N)
    dt = jax.nn.softplus(ssm_dt.astype(f32) + p['ssm_dt_bias'].astype(f32)).reshape(b, L, SSM_GROUPS, SSM_HPG)
    a_neg = -jnp.exp(p['ssm_a_log'].astype(f32)).reshape(SSM_GROUPS, SSM_HPG)
    h0 = ssm_state.astype(f32).reshape(b, SSM_GROUPS, SSM_HPG, SSM_P, SSM_N)
    y, ssm_state_new = ssd_chunked(xs, bm, cm, dt, a_neg, h0)
    y = y + p['ssm_d'].astype(f32).reshape(SSM_GROUPS, SSM_HPG, 1) * xs
    y = y.reshape(b, L, SSM_INNER) * jax.nn.silu(ssm_z.astype(f32))
    y = rms(y.reshape(b, L, SSM_GROUPS, SSM_INNER // SSM_GROUPS)).reshape(b, L, SSM_INNER)
    o_ssm = (y * p['ssm_norm'].astype(f32)).astype(dt_)
    ssm_state_new = ssm_state_new.reshape(b, SSM_HEADS, SSM_P, SSM_N)

    g_dn, g_dil, g_ssm = jnp.split(jax.nn.sigmoid(gates), N_BRANCH, axis=-1)
    merged = g_dn * (o_dn @ p['w_br_dn']) + g_dil * (o_dil @ p['w_br_dil']) + g_ssm * (o_ssm @ p['w_br_ssm'])
    x = x + rmsnorm(merged @ p['w_out'], p['norm_mix_post'])

    h = rmsnorm(x, p['norm_mem_pre'])
    qm = (h @ p['w_mq']).astype(f32).reshape(b, L, MEM_HEADS, MEM_HEAD_DIM)
    s = jnp.einsum('blhe,bmhe->bhlm', qm, mem_k.astype(f32)) * (MEM_HEAD_DIM ** -0.5)
    pm = jax.nn.softmax(s, axis=-1)
    om = jnp.einsum('bhlm,bmhe->blhe', pm, mem_v.astype(f32)).reshape(b, L, MEM_WIDTH).astype(dt_)
    x = x + rmsnorm(om @ p['w_mo'], p['norm_mem_post'])

    h = rmsnorm(x, p['norm_ffn_pre'])
    f = jnp.square(jax.nn.relu(h @ p['w_ff1'])) @ p['w_ff2']
    x = x + rmsnorm(f, p['norm_ffn_post'])
    states = (dn_conv_new, dn_state_new, ssm_conv_new, ssm_state_new) + tuple(win_new)
    return x, states


def setup_inputs(seed: int = 0) -> dict:
    key = jax.random.key(seed)
    keys = iter(jax.random.split(key, 64))

    def nrm(shape, scale=1.0):
        return jax.random.normal(next(keys), shape, jnp.float32) * scale

    def gain(n):
        return 1.0 + nrm((DEPTH, n), 0.05)

    def dt_bias(n):
        dt = jnp.exp(jax.random.uniform(next(keys), (DEPTH, n), jnp.float32, math.log(1e-3), math.log(1e-1)))
        return dt + jnp.log(-jnp.expm1(-dt))

    def a_log(n):
        return jnp.log(jax.random.uniform(next(keys), (DEPTH, n), jnp.float32, 1.0, 16.0))

    len1, len2, len3 = [min(w, PAST_LEN) for w, _ in DIL_GROUPS]
    kvh = (DIL_HPG, DIL_HEAD_DIM)
    memh = (MEM_TOKENS, MEM_HEADS, MEM_HEAD_DIM)
    return {
        'x_prompt': nrm((BATCH, SEQ, D_MODEL)),
        'x_sample': nrm((DEC_BATCH, DEC_SEQ, D_MODEL)),
        'state_dn_conv': nrm((DEPTH, DEC_BATCH, CONV_W - 1, 3 * DN_WIDTH)),
        'state_dn': nrm((DEPTH, DEC_BATCH, DN_HEADS, DN_DK, DN_DV), 0.1),
        'state_ssm_conv': nrm((DEPTH, DEC_BATCH, CONV_W - 1, SSM_CONV_DIM)),
        'state_ssm': nrm((DEPTH, DEC_BATCH, SSM_HEADS, SSM_P, SSM_N), 0.1),
        'cache_win1_k': nrm((DEPTH, DEC_BATCH, len1) + kvh),
        'cache_win1_v': nrm((DEPTH, DEC_BATCH, len1) + kvh),
        'cache_win2_k': nrm((DEPTH, DEC_BATCH, len2) + kvh),
        'cache_win2_v': nrm((DEPTH, DEC_BATCH, len2) + kvh),
        'cache_win3_k': nrm((DEPTH, DEC_BATCH, len3) + kvh),
        'cache_win3_v': nrm((DEPTH, DEC_BATCH, len3) + kvh),
        'cache_mem_k': nrm((DEPTH, DEC_BATCH) + memh),
        'cache_mem_v': nrm((DEPTH, DEC_BATCH) + memh),
        'mem_prompt': nrm((BATCH, MEM_TOKENS, D_MODEL)),
        'norm_mix_pre': gain(D_MODEL),
        'w_in': nrm((DEPTH, D_MODEL, N_IN), D_MODEL ** -0.5),
        'dn_conv_w': nrm((DEPTH, CONV_W, 3 * DN_WIDTH), 0.5),
        'dn_a_log': a_log(DN_HEADS),
        'dn_dt_bias': dt_bias(DN_HEADS),
        'dn_norm': gain(DN_DV),
        'ssm_conv_w': nrm((DEPTH, CONV_W, SSM_CONV_DIM), 0.5),
        'ssm_conv_b': nrm((DEPTH, SSM_CONV_DIM), 0.02),
        'ssm_a_log': a_log(SSM_HEADS),
        'ssm_dt_bias': dt_bias(SSM_HEADS),
        'ssm_d': 1.0 + nrm((DEPTH, SSM_HEADS), 0.1),
        'ssm_norm': gain(SSM_INNER),
        'w_br_dn': nrm((DEPTH, DN_WIDTH, D_MODEL), DN_WIDTH ** -0.5),
        'w_br_dil': nrm((DEPTH, DIL_OUT, D_MODEL), DIL_OUT ** -0.5),
        'w_br_ssm': nrm((DEPTH, SSM_INNER, D_MODEL), SSM_INNER ** -0.5),
        'w_out': nrm((DEPTH, D_MODEL, D_MODEL), D_MODEL ** -0.5),
        'norm_mix_post': gain(D_MODEL),
        'norm_mem_pre': gain(D_MODEL),
        'norm_mem_kv': gain(D_MODEL),
        'w_mq': nrm((DEPTH, D_MODEL, MEM_WIDTH), D_MODEL ** -0.5),
        'w_mkv': nrm((DEPTH, D_MODEL, 2 * MEM_WIDTH), D_MODEL ** -0.5),
        'w_mo': nrm((DEPTH, MEM_WIDTH, D_MODEL), MEM_WIDTH ** -0.5),
        'norm_mem_post': gain(D_MODEL),
        'norm_ffn_pre': gain(D_MODEL),
        'w_ff1': nrm((DEPTH, D_MODEL, D_FF), D_MODEL ** -0.5),
        'w_ff2': nrm((DEPTH, D_FF, D_MODEL), D_FF ** -0.5),
        'norm_ffn_post': gain(D_MODEL),
    }


def reference(x_prompt, x_sample, state_dn_conv, state_dn, state_ssm_conv, state_ssm,
              cache_win1_k, cache_win1_v, cache_win2_k, cache_win2_v, cache_win3_k, cache_win3_v,
              cache_mem_k, cache_mem_v, mem_prompt,
              norm_mix_pre, w_in, dn_conv_w, dn_a_log, dn_dt_bias, dn_norm,
              ssm_conv_w, ssm_conv_b, ssm_a_log, ssm_dt_bias, ssm_d, ssm_norm,
              w_br_dn, w_br_dil, w_br_ssm, w_out, norm_mix_post,
              norm_mem_pre, norm_mem_kv, w_mq, w_mkv, w_mo, norm_mem_post,
              norm_ffn_pre, w_ff1, w_ff2, norm_ffn_post):
    n_p, S = x_prompt.shape[:2]
    T = x_sample.shape[1]
    n_mem = mem_prompt.shape[1]
    pos_p = jnp.arange(S, dtype=jnp.float32)
    pos_s = PAST_LEN + jnp.arange(T, dtype=jnp.float32)
    zero_dn_conv = jnp.zeros((n_p, CONV_W - 1, 3 * DN_WIDTH), x_prompt.dtype)
    zero_dn = jnp.zeros((n_p, DN_HEADS, DN_DK, DN_DV), jnp.float32)
    zero_ssm_conv = jnp.zeros((n_p, CONV_W - 1, SSM_CONV_DIM), x_prompt.dtype)
    zero_ssm = jnp.zeros((n_p, SSM_HEADS, SSM_P, SSM_N), jnp.float32)
    xp, xs = x_prompt, x_sample
    new_p = [[] for _ in range(12)]
    new_s = [[] for _ in range(10)]
    for l in range(DEPTH):
        prm = dict(norm_mix_pre=norm_mix_pre[l], w_in=w_in[l], dn_conv_w=dn_conv_w[l], dn_a_log=dn_a_log[l],
                   dn_dt_bias=dn_dt_bias[l], dn_norm=dn_norm[l], ssm_conv_w=ssm_conv_w[l], ssm_conv_b=ssm_conv_b[l],
                   ssm_a_log=ssm_a_log[l], ssm_dt_bias=ssm_dt_bias[l], ssm_d=ssm_d[l], ssm_norm=ssm_norm[l],
                   w_br_dn=w_br_dn[l], w_br_dil=w_br_dil[l], w_br_ssm=w_br_ssm[l], w_out=w_out[l],
                   norm_mix_post=norm_mix_post[l], norm_mem_pre=norm_mem_pre[l], w_mq=w_mq[l], w_mo=w_mo[l],
                   norm_mem_post=norm_mem_post[l], norm_ffn_pre=norm_ffn_pre[l], w_ff1=w_ff1[l], w_ff2=w_ff2[l],
                   norm_ffn_post=norm_ffn_post[l])
        mkv = (rmsnorm(mem_prompt, norm_mem_kv[l]) @ w_mkv[l]).reshape(n_p, n_mem, 2, MEM_HEADS, MEM_HEAD_DIM)
        mk, mv = mkv[:, :, 0], mkv[:, :, 1]
        xp, st_p = trunk_layer(xp, pos_p, mk, mv, zero_dn_conv, zero_dn, zero_ssm_conv, zero_ssm, None, prm)
        for i, a in enumerate(st_p + (mk, mv)):
            new_p[i].append(a)
        win_l = (cache_win1_k[l], cache_win1_v[l], cache_win2_k[l], cache_win2_v[l], cache_win3_k[l], cache_win3_v[l])
        xs, st_s = trunk_layer(xs, pos_s, cache_mem_k[l], cache_mem_v[l], state_dn_conv[l], state_dn[l],
                               state_ssm_conv[l], state_ssm[l], win_l, prm)
        for i, a in enumerate(st_s):
            new_s[i].append(a)
    (p_dn_conv, p_dn, p_ssm_conv, p_ssm, p_win1_k, p_win1_v, p_win2_k, p_win2_v,
     p_win3_k, p_win3_v, p_mem_k, p_mem_v) = [jnp.stack(a) for a in new_p]
    (s_dn_conv, s_dn, s_ssm_conv, s_ssm, s_win1_k, s_win1_v, s_win2_k, s_win2_v,
     s_win3_k, s_win3_v) = [jnp.stack(a) for a in new_s]
    return (xp, xs, p_dn_conv, p_dn, p_ssm_conv, p_ssm, p_win1_k, p_win1_v, p_win2_k, p_win2_v,
            p_win3_k, p_win3_v, p_mem_k, p_mem_v, s_dn_conv, s_dn, s_ssm_conv, s_ssm,
            s_win1_k, s_win1_v, s_win2_k, s_win2_v, s_win3_k, s_win3_v)
```

```python
import os
import threading
import numpy as np
from contextlib import ExitStack
import concourse.bass as bass
import concourse.mybir as mybir
from concourse.bass_utils import run_bass_kernel_spmd

F32 = mybir.dt.float32
BF16 = mybir.dt.bfloat16
AF = mybir.ActivationFunctionType
ALU = mybir.AluOpType
AX = mybir.AxisListType

D = 1024
NCORES = 8
SEQ = 2048
NSS = 16
TS = 4
NT = 17
TOK = SEQ + NSS * TS
DEPTH = 2
N_IN = 14880
EPS = 1e-6
C_DNQKV = 0
C_DNZ = 3072
C_DNB = 4096
C_DNA = 4104
C_DIL = 4112
C_SSZ = 8720
C_SSX = 9744
C_SSDT = 11792
C_GATE = 11808


def tn(t):
    return 128 if t < 16 else 64


class Sem:
    def __init__(self, h):
        self.h = h


class Buf:
    def __init__(self, t, name=""):
        self.t = t
        self.w = None
        self.r = {}
        self.name = name
        self.is_psum = False

    def __getitem__(self, idx):
        return V(self.t[idx], [self])

    def v(self):
        return V(self.t[:], [self])


class V:
    def __init__(self, ap, bufs):
        self.ap = ap
        self.bufs = bufs

    def __getitem__(self, idx):
        return V(self.ap[idx], self.bufs)

    def rr(self, pat, **kw):
        return V(self.ap.rearrange(pat, **kw), self.bufs)

    def bc(self, shape):
        return V(self.ap.broadcast_to(shape), self.bufs)

    def bitcast(self, dt):
        return V(self.ap.bitcast(dt), self.bufs)

    def unsq(self, d):
        return V(self.ap.unsqueeze(d), self.bufs)


class Slot:
    def __init__(self, sem):
        self.sem = sem
        self.cnt = 0


class Eng:
    def __init__(self, h, sem, is_pe=False):
        self.h = h
        self.sem = sem
        self.cnt = 0
        self.seen = {}
        self.is_pe = is_pe
        self.slots = []
        self.slot_i = 0


def _aps(x):
    return x.ap if isinstance(x, V) else x


class KB:
    def __init__(self, nc, es):
        self.nc = nc
        self.es = es
        mk = lambda n: Sem(es.enter_context(nc.semaphore(n)))
        self.pe = Eng(nc.tensor, mk("s_pe"), is_pe=True)
        self.act = Eng(nc.scalar, mk("s_act"))
        self.dve = Eng(nc.vector, mk("s_dve"))
        self.pool = Eng(nc.gpsimd, mk("s_pool"))
        self.sp = Eng(nc.sync, mk("s_sp"))
        for i in range(24):
            self.sp.slots.append(Slot(mk("d_sp%d" % i)))
        for i in range(16):
            self.pool.slots.append(Slot(mk("d_pl%d" % i)))
        for i in range(8):
            self.act.slots.append(Slot(mk("d_ac%d" % i)))
        self.n_inst = 0
        self.interleaving = False
        self._tls = threading.local()

    def barrier(self):
        engs = [self.pe, self.act, self.dve, self.pool, self.sp]
        for e in engs:
            for o in engs:
                if (o is not e or not e.is_pe) and o.cnt > 0 and e.seen.get(o.sem, 0) < o.cnt:
                    e.h.wait_ge(o.sem.h, o.cnt)
                    e.seen[o.sem] = o.cnt
            for q in (self.sp, self.pool, self.act):
                for s in q.slots:
                    if s.cnt > 0 and e.seen.get(s.sem, 0) < 16 * s.cnt:
                        e.h.wait_ge(s.sem.h, 16 * s.cnt)
                        e.seen[s.sem] = 16 * s.cnt

    def phase(self):
        kb = self

        class _P:
            def __enter__(s):
                s.old = kb.es
                s.st = ExitStack()
                kb.es = s.st
                kb.uid = getattr(kb, "uid", 0) + 1

            def __exit__(s, *a):
                kb.barrier()
                s.st.close()
                kb.es = s.old
        return _P()

    def sb(self, name, shape, dt=F32):
        self.alloc_i = getattr(self, "alloc_i", 0) + 1
        name = "%s_%d" % (name, self.alloc_i)
        return Buf(self.es.enter_context(self.nc.sbuf_tensor(name, list(shape), dt)), name)

    def psum(self, name, shape, dt=F32):
        b = Buf(self.es.enter_context(self.nc.psum_tensor(name, list(shape), dt)), name)
        b.is_psum = True
        return b

    def slot(self):
        return getattr(self._tls, "slot", 0) if self.interleaving else 0

    def interleave(self, fns):
        n = len(fns)
        sems = [threading.Semaphore(0) for _ in range(n)]
        done = threading.Semaphore(0)
        st = {"alive": [True] * n, "exc": None}
        self._il = (sems, st)

        def nxt_alive(i):
            for d in range(1, n + 1):
                j = (i + d) % n
                if st["alive"][j]:
                    return j
            return None

        self._nxt_alive = nxt_alive

        def runner(i):
            sems[i].acquire()
            self._tls.slot = i
            try:
                fns[i]()
            except BaseException as e:
                st["exc"] = e
                st["alive"] = [False] * n
                done.release()
                return
            st["alive"][i] = False
            j = nxt_alive(i)
            if j is None:
                done.release()
            else:
                sems[j].release()
        ths = [threading.Thread(target=runner, args=(i,), daemon=True) for i in range(n)]
        for t in ths:
            t.start()
        self.interleaving = True
        sems[0].release()
        done.acquire()
        self.interleaving = False
        if st["exc"] is not None:
            raise st["exc"]

    def _yield(self):
        if not self.interleaving:
            return
        sems, st = self._il
        i = self._tls.slot
        j = self._nxt_alive(i)
        if j is None or j == i:
            return
        sems[j].release()
        sems[i].acquire()

    def _waits(self, eng, R, W):
        deps = {}

        def add(sm, v):
            if deps.get(sm, 0) < v:
                deps[sm] = v

        for b in R:
            if b.w is not None:
                add(*b.w)
            if b.is_psum:
                for sm, v in b.r.items():
                    if sm is not eng.sem:
                        add(sm, v)
        for b in W:
            if b.w is not None:
                add(*b.w)
            for sm, v in b.r.items():
                add(sm, v)
        for sm, v in deps.items():
            if sm is eng.sem and eng.is_pe:
                continue
            if eng.seen.get(sm, 0) >= v:
                continue
            eng.h.wait_ge(sm.h, v)
            eng.seen[sm] = v

    def emit(self, eng, fn, R, W):
        self._waits(eng, R, W)
        ins = fn()
        ins.then_inc(eng.sem.h, 1)
        eng.cnt += 1
        self.n_inst += 1
        for b in R:
            b.r[eng.sem] = eng.cnt
        for b in W:
            b.w = (eng.sem, eng.cnt)
            b.r = {}
        self._yield()

    def dma(self, q, out, in_):
        slot = q.slots[q.slot_i % len(q.slots)]
        q.slot_i += 1
        if slot.cnt > 0 and q.seen.get(slot.sem, 0) < 16 * slot.cnt:
            q.h.wait_ge(slot.sem.h, 16 * slot.cnt)
            q.seen[slot.sem] = 16 * slot.cnt
        R, W = in_.bufs, out.bufs
        self._waits(q, R, W)
        ins = q.h.dma_start(out=out.ap, in_=in_.ap)
        ins.then_inc(slot.sem.h, 16)
        slot.cnt += 1
        self.n_inst += 1
        v = 16 * slot.cnt
        for b in R:
            b.r[slot.sem] = v
        for b in W:
            b.w = (slot.sem, v)
            b.r = {}
        self._yield()

    def finish(self):
        q = self.sp
        for e in (self.sp, self.pool, self.act):
            for s in e.slots:
                if s.cnt > 0:
                    q.h.wait_ge(s.sem.h, 16 * s.cnt)
        for e in (self.pe, self.act, self.dve, self.pool):
            if e.cnt > 0:
                q.h.wait_ge(e.sem.h, e.cnt)

    def mm(self, out, lhsT, rhs, start=True, stop=True):
        self.emit(self.pe, lambda: self.nc.tensor.matmul(out.ap, lhsT=lhsT.ap, rhs=rhs.ap, start=start, stop=stop),
                  lhsT.bufs + rhs.bufs, out.bufs)

    def tr(self, out, in_, ident):
        self.emit(self.pe, lambda: self.nc.tensor.transpose(out.ap, in_.ap, ident.ap),
                  in_.bufs + ident.bufs, out.bufs)

    def actf(self, out, in_, func, scale=None, bias=None, accum=None):
        kw = {}
        R = list(in_.bufs)
        W = list(out.bufs)
        if scale is not None:
            kw["scale"] = _aps(scale)
            if isinstance(scale, V):
                R += scale.bufs
        if bias is not None:
            kw["bias"] = _aps(bias)
            if isinstance(bias, V):
                R += bias.bufs
        if accum is not None:
            kw["accum_out"] = accum.ap
            W += accum.bufs
        self.emit(self.act, lambda: self.nc.scalar.activation(out=out.ap, in_=in_.ap, func=func, **kw), R, W)

    def _e(self, eng):
        return {"dve": self.dve, "pool": self.pool}[eng]

    def tt(self, out, in0, in1, op, eng="dve"):
        e = self._e(eng)
        self.emit(e, lambda: e.h.tensor_tensor(out=out.ap, in0=in0.ap, in1=in1.ap, op=op),
                  in0.bufs + in1.bufs, out.bufs)

    def ts(self, out, in0, s1, s2=None, op0=ALU.mult, op1=None, eng="dve", accum=None):
        e = self._e(eng)
        R = list(in0.bufs)
        W = list(out.bufs)
        for s in (s1, s2):
            if isinstance(s, V):
                R += s.bufs
        kw = {}
        if op1 is not None:
            kw["op1"] = op1
        if accum is not None:
            kw["accum_out"] = accum.ap
            W += accum.bufs
        self.emit(e, lambda: e.h.tensor_scalar(out=out.ap, in0=in0.ap, scalar1=_aps(s1), scalar2=_aps(s2), op0=op0, **kw), R, W)

    def stt(self, out, in0, scalar, in1, op0, op1):
        e = self.dve
        R = in0.bufs + in1.bufs + (scalar.bufs if isinstance(scalar, V) else [])
        self.emit(e, lambda: e.h.scalar_tensor_tensor(out=out.ap, in0=in0.ap, scalar=_aps(scalar), in1=in1.ap, op0=op0, op1=op1),
                  R, out.bufs)

    def cp(self, out, in_, eng="dve"):
        if eng == "act":
            return self.actf(out, in_, AF.Copy)
        e = self._e(eng)
        self.emit(e, lambda: e.h.tensor_copy(out=out.ap, in_=in_.ap), in_.bufs, out.bufs)

    def recip(self, out, in_):
        self.emit(self.dve, lambda: self.nc.vector.reciprocal(out=out.ap, in_=in_.ap), in_.bufs, out.bufs)

    def memset(self, out, c, eng="pool"):
        e = self._e(eng)
        self.emit(e, lambda: e.h.memset(out.ap, c), [], out.bufs)


class Ring:
    def __init__(self, bufs):
        self.bufs = bufs
        self.i = 0

    def next(self):
        b = self.bufs[self.i % len(self.bufs)]
        self.i += 1
        return b


IN_NAMES = []
FLAGS = set(os.environ.get('KFLAGS', '').split(','))

DIL_W = (128, 512, 2048)
DIL_D = (1, 4, 16)
SC = 128.0 ** -0.5


def host_consts():
    c = {}
    c["c_ident"] = np.eye(128, dtype=np.float32)
    j = np.arange(128)[:, None]
    i = np.arange(128)[None, :]
    m = np.zeros((9, 128, 128), np.float32)
    for g, d in enumerate(DIL_D):
        res = ((i - j) % d) == 0
        m[g * 3 + 0] = res & (i >= j)
        m[g * 3 + 1] = res
        m[g * 3 + 2] = res & (i <= j)
    c["c_dilmask"] = np.ascontiguousarray(m.transpose(1, 0, 2))
    half = 16
    inv = np.power(np.float32(500000.0), -np.arange(half, dtype=np.float32) * np.float32(2.0) / np.float32(32))
    pos = np.concatenate([np.arange(SEQ, dtype=np.float32), np.tile(2048 + np.arange(TS, dtype=np.float32), NSS)])
    ang = (pos[:, None] * inv[None, :]).astype(np.float32)
    cos = np.cos(ang).astype(np.float32).T
    sin = np.sin(ang).astype(np.float32).T
    c["c_rope"] = np.ascontiguousarray(np.stack([np.concatenate([cos, cos]), np.concatenate([-sin, sin])]))
    p = np.arange(128)[:, None]
    t = np.arange(4)[None, :]
    sm = np.zeros((128, 9, 4), np.float32)
    sm[:, 0, :] = (p >= t)
    for b in range(4):
        sm[:, 1 + b, :] = ((p % 4) == t)
        sm[:, 5 + b, :] = (b == t)
    c["c_smask"] = sm
    ks, kt = np.arange(64)[:, None] // 4, np.arange(64)[:, None] % 4
    qs, qt = np.arange(64)[None, :] // 4, np.arange(64)[None, :] % 4
    nm = np.zeros((64, 3, 64), np.float32)
    nm[:, 0, :] = (ks == qs) & (kt <= qt)
    nm[:, 1, :] = (ks == qs) & (kt == qt)
    nm[:, 2, :] = (ks == qs) & (kt == qt)
    c["c_nmask"] = nm
    kk = np.arange(128)[:, None]
    ii = np.arange(128)[None, :]
    tri = np.zeros((128, 2, 128), np.float32)
    tri[:, 0, :] = (kk <= ii)
    same = ((kk // 4) == (ii // 4)) & (kk < 64) & (ii < 64)
    tri[:, 1, :] = same & (kk <= ii)
    c["c_tri"] = tri
    sg = np.ones((128, 2, 128), np.float32)
    sg[:, 1, :] = same
    c["c_segm"] = sg
    stri = np.zeros((128, 2, 128), np.float32)
    stri[:, 0, :] = (kk < ii)
    stri[:, 1, :] = same & (kk < ii)
    c["c_stri"] = stri
    bm = np.zeros((128, 4, 128), np.float32)
    bm[:, 0, :] = (kk // 16) == (ii // 16)
    for mi, b in enumerate((16, 32, 64)):
        bm[:, mi + 1, :] = ((kk // (2 * b)) == (ii // (2 * b))) & ((kk // b) != (ii // b))
    c["c_blk"] = bm
    sc = np.zeros((128, NSS, 64), np.float32)
    for s_ in range(NSS):
        sc[:, s_, 4 * s_:4 * s_ + 4] = 1.0
    c["c_segcol"] = sc
    rm = np.zeros((64, NSS), np.float32)
    for s_ in range(NSS):
        rm[4 * s_:4 * s_ + 4, s_] = 1.0
    c["c_rowmask"] = rm
    return c


def build(debug=False):
    nc = bass.Bass("TRN2", target_bir_lowering=False)
    es = ExitStack()
    k = KB(nc, es)

    def din(name, shape, dt=F32):
        if name not in IN_NAMES:
            IN_NAMES.append(name)
        return Buf(nc.dram_tensor(name, list(shape), dt, kind="ExternalInput"), name)

    def dout(name, shape, dt=F32):
        return Buf(nc.dram_tensor(name, list(shape), dt, kind="ExternalOutput"), name)

    x_p = din("x_p", [SEQ, D])
    x_s = din("x_s", [NSS * TS, D])
    mem_p = din("mem_p", [256, D])
    W = {}
    for name, shape in [("norm_mix_pre", [DEPTH, D]), ("w_in", [DEPTH, D, N_IN]), ("w_br_dn", [DEPTH, 1024, D]),
                        ("w_br_dil", [DEPTH, 512, D]), ("w_br_ssm", [DEPTH, 1024, D]), ("w_out", [DEPTH, D, D]),
                        ("norm_mix_post", [DEPTH, D]), ("norm_mem_pre", [DEPTH, D]), ("norm_mem_kv", [DEPTH, D]),
                        ("w_mq", [DEPTH, D, 512]), ("w_mkv", [DEPTH, D, 1024]), ("w_mo", [DEPTH, 512, D]),
                        ("norm_mem_post", [DEPTH, D]), ("norm_ffn_pre", [DEPTH, D]), ("w_ff1", [DEPTH, D, 4096]),
                        ("w_ff2", [DEPTH, 4096, D]), ("norm_ffn_post", [DEPTH, D])]:
        W[name] = din(name, shape)
    cache_mem_k = din("cache_mem_k", [DEPTH, NSS, 256, 512])
    cache_mem_v = din("cache_mem_v", [DEPTH, NSS, 256, 512])
    cwin_k = [din("cache_win%d_k" % (g + 1), [DEPTH, NSS, DIL_W[g], 512]) for g in range(3)]
    cwin_v = [din("cache_win%d_v" % (g + 1), [DEPTH, NSS, DIL_W[g], 512]) for g in range(3)]
    c_ident = din("c_ident", [128, 128])
    c_dilmask = din("c_dilmask", [128, 9, 128])
    c_rope = din("c_rope", [2, 32, TOK])
    c_smask = din("c_smask", [128, 9, 4])
    c_nmask = din("c_nmask", [64, 3, 64])
    c_tri = din("c_tri", [128, 2, 128])
    c_segm = din("c_segm", [128, 2, 128])
    c_stri = din("c_stri", [128, 2, 128])
    c_blk = din("c_blk", [128, 4, 128])
    state_dn_conv = din("state_dn_conv", [DEPTH, NSS, 3, 3072])
    state_dn = din("state_dn", [DEPTH, NSS, 8, 128, 128])
    dn_conv_w = din("dn_conv_w", [DEPTH, 4, 3072])
    dn_a_log = din("dn_a_log", [DEPTH, 8])
    dn_dt_bias = din("dn_dt_bias", [DEPTH, 8])
    dn_norm = din("dn_norm", [DEPTH, 128])
    c_segcol = din("c_segcol", [128, NSS, 64])
    c_rowmask = din("c_rowmask", [64, NSS])
    state_ssm_conv = din("state_ssm_conv", [DEPTH, NSS, 3, 2048])
    state_ssm = din("state_ssm", [DEPTH, NSS, 16, 64, 128])
    ssm_conv_w = din("ssm_conv_w", [DEPTH, 4, 2048])
    ssm_conv_b = din("ssm_conv_b", [DEPTH, 2048])
    ssm_a_log = din("ssm_a_log", [DEPTH, 16])
    ssm_dt_bias = din("ssm_dt_bias", [DEPTH, 16])
    ssm_d = din("ssm_d", [DEPTH, 16])
    ssm_norm = din("ssm_norm", [DEPTH, 1024])

    y_p = dout("y_p", [SEQ, D])
    y_s = dout("y_s", [NSS * TS, D])
    p_mem_k = dout("p_mem_k", [DEPTH, 256, 512])
    p_mem_v = dout("p_mem_v", [DEPTH, 256, 512])
    pwin_k = [dout("p_win%d_k" % (g + 1), [DEPTH, DIL_W[g], 512]) for g in range(3)]
    pwin_v = [dout("p_win%d_v" % (g + 1), [DEPTH, DIL_W[g], 512]) for g in range(3)]
    swin_k = [dout("s_win%d_k" % (g + 1), [DEPTH, NSS * TS, 512]) for g in range(3)]
    swin_v = [dout("s_win%d_v" % (g + 1), [DEPTH, NSS * TS, 512]) for g in range(3)]
    p_ssm_conv = dout("p_ssm_conv", [DEPTH, 3, 2048])
    p_dn_conv = dout("p_dn_conv", [DEPTH, 3, 3072])
    p_dn = dout("p_dn", [DEPTH, 8, 128, 128])
    s_dn_conv = dout("s_dn_conv", [DEPTH, NSS, 3, 3072])
    s_dn = dout("s_dn", [DEPTH, NSS, 8, 128, 128])
    p_ssm = dout("p_ssm", [DEPTH, 16, 64, 128])
    s_ssm_conv = dout("s_ssm_conv", [DEPTH, NSS, 3, 2048])
    s_ssm = dout("s_ssm", [DEPTH, NSS, 16, 64, 128])

    X = [None] * NT
    hT_t = es.enter_context(nc.sbuf_tensor("hT", [128, 8, TOK], BF16))
    hTb = [Buf(hT_t, "hT%d" % t) for t in range(NT)]

    def hT(kc, t0, t1):
        a = t0 * 128
        b = min(t1 * 128, TOK)
        return V(hT_t[:, kc, a:b], hTb[t0:t1])

    ident_f = k.sb("ident_f", [128, 128])
    ident_b = k.sb("ident_b", [128, 128], BF16)
    gpre = k.sb("gpre", [128, 4, DEPTH, 8])
    gpost = Ring([k.sb("gpost%d" % i, [128, D]) for i in range(1)])
    small = Ring([k.sb("small%d" % i, [128, 8]) for i in range(12)])
    xn_r = Ring([k.sb("xn%d" % i, [128, D], BF16) for i in range(2)])
    junk_r = Ring([k.sb("junk%d" % i, [128, D], BF16) for i in range(1)])
    ysb_box = [None]

    class _YB:
        def next(self):
            return ysb_box[0].next()
    ysb_r = _YB()

    def alloc_ysb(n):
        ysb_box[0] = Ring([k.sb("ysb%d" % i, [128, D]) for i in range(n)])
    junkf = k.sb("junkf", [128, 576])
    PS = [k.psum("ps%d" % i, [128, 512]) for i in range(8)]
    class SlotRing:
        def __init__(self, full, parts):
            self.full = Ring(full)
            self.parts = [Ring(p) for p in parts]

        def next(self):
            if k.interleaving:
                return self.parts[k.slot()].next()
            return self.full.next()
    psr = SlotRing(PS[4:8], [PS[4:6], PS[6:8]])
    accr = SlotRing(PS[0:2], [PS[0:2], PS[2:4]])
    wblk_box = [None]

    class _WB:
        def next(self):
            return wblk_box[0].next()
    wblk = _WB()

    def alloc_wblk(n=3):
        wblk_box[0] = Ring([k.sb("wblk%d" % i, [128, 8, 512], BF16) for i in range(n)])

    k.dma(k.sp, ident_f.v(), c_ident.v())
    k.cp(ident_b.v(), ident_f.v())
    for i, nm in enumerate(["norm_mix_pre", "norm_mem_pre", "norm_ffn_pre", "norm_mem_kv"]):
        with nc.allow_non_contiguous_dma(reason="tiny gain vectors"):
            k.dma(k.sp, gpre[:, i, :, :], W[nm].v().rr("l (c p) -> p l c", p=128))

    def xrows(bp, bs, t):
        if t < 16:
            return bp[t * 128:(t + 1) * 128, :]
        return bs.v()

    def rstd_of(src, n, scale_inv):
        sm = small.next()
        jk = junk_r.next()
        F = src.ap.shape[-1]
        k.actf(jk[0:n, 0:F], src, AF.Square, accum=sm[0:n, 0:1])
        k.ts(sm[0:n, 1:2], sm[0:n, 0:1], scale_inv, EPS, op0=ALU.mult, op1=ALU.add)
        k.actf(sm[0:n, 2:3], sm[0:n, 1:2], AF.Sqrt)
        k.recip(sm[0:n, 3:4], sm[0:n, 2:3])
        return sm[0:n, 3:4]

    def make_hT(gi, l, xget=None):
        for t in range(NT):
            n = tn(t)
            xt = X[t] if xget is None else xget(t)
            r = rstd_of(xt[0:n, :], n, 1.0 / D)
            xn = xn_r.next()
            k.actf(xn[0:n, :], xt[0:n, :], AF.Copy, scale=r)
            ps = psr.next()
            psb = ps.v().bitcast(BF16)
            for kc in range(8):
                k.tr(psb[:, kc * 128:kc * 128 + n], xn[0:n, kc * 128:(kc + 1) * 128], ident_b[0:n, 0:n])
            a = t * 128
            k.tt(V(hT_t[:, :, a:a + n], [hTb[t]]),
                 psb.rr("p (c j) -> p c j", c=8)[:, :, 0:n],
                 gpre[:, gi, l, :].unsq(2).bc([128, 8, n]), ALU.mult)

    def load_w(dst, src):
        k.dma(k.pool, dst, src)

    def wsrc(name, l, r0, r1, c0, c1):
        return V(W[name].t[l, r0:r1, c0:c1].rearrange("(c p) n -> p c n", p=128), [W[name]])

    def out_proj_post(l, wname, gname, lhs_fn, nkc, stream=None):
        g = gpost.next()
        k.dma(k.sp, g.v(), V(W[gname].t[l, :].partition_broadcast(128), [W[gname]]))
        wbs = []
        for half in range(2):
            wb = wblk.next()
            load_w(wb[:, 0:nkc, :], wsrc(wname, l, 0, nkc * 128, half * 512, (half + 1) * 512))
            wbs.append(wb)
        for t in range(NT):
            n = tn(t)
            ysb = ysb_r.next()
            for half in range(2):
                ps = psr.next()
                for kc in range(nkc):
                    k.mm(ps[0:n, :], lhs_fn(t, kc), wbs[half][:, kc, :], start=(kc == 0), stop=(kc == nkc - 1))
                k.cp(ysb[0:n, half * 512:(half + 1) * 512], ps[0:n, :], eng="act")
            r = rstd_of(ysb[0:n, :], n, 1.0 / D)
            k.stt(ysb[0:n, :], ysb[0:n, :], r, g[0:n, :], ALU.mult, ALU.mult)
            if stream is None:
                k.tt(X[t][0:n, :], X[t][0:n, :], ysb[0:n, :], ALU.add, eng="pool")
            else:
                xt = ysb_r.next()
                k.dma(k.sp, xt[0:n, :], xrows(stream[0], stream[1], t))
                k.tt(xt[0:n, :], xt[0:n, :], ysb[0:n, :], ALU.add, eng="pool")
                k.dma(k.sp, xrows(y_p, y_s, t), xt[0:n, :])

    def mixer(l):
      srcp, srcs = (x_p, x_s) if l == 0 else (y_p, y_s)
      with k.phase():
        obT = k.sb("obT", [128, 8, TOK], BF16)
        mbox = [None]

        def xget(t):
            xt = ysb_r.next()
            k.dma(k.sp, xt[0:tn(t), :], xrows(srcp, srcs, t))
            return xt
        with k.phase():
            alloc_ysb(2)
            make_hT(0, l, xget)

        def branch_proj(wname, nkc, gidx, first):
          if mbox[0] is None:
              mbox[0] = k.sb("mergedT", [128, 8, TOK], BF16)
          mergedT = mbox[0]
          with k.phase():
            alloc_wblk(3)
            sig_r = Ring([k.sb("sig%d" % i, [128, 512]) for i in range(2)])
            for cg in range(2):
                    wg = wblk.next()
                    c0 = C_GATE + gidx * 1024 + cg * 512
                    load_w(wg.v(), wsrc("w_in", l, 0, 1024, c0, c0 + 512))
                    wb = wblk.next()
                    load_w(wb[:, 0:nkc, :], wsrc(wname, l, 0, nkc * 128, cg * 512, (cg + 1) * 512))
                    for j in range(4):
                        dc = cg * 4 + j
                        for tb in range(5):
                            t0, t1 = tb * 4, min(tb * 4 + 4, NT)
                            ntok = min(512, TOK - tb * 512)
                            a = tb * 512
                            psg = psr.next()
                            for kc in range(8):
                                k.mm(psg[:, 0:ntok], wg[:, kc, j * 128:(j + 1) * 128], hT(kc, t0, t1), start=(kc == 0), stop=(kc == 7))
                            sg = sig_r.next()
                            k.actf(sg[:, 0:ntok], psg[:, 0:ntok], AF.Sigmoid)
                            psp = psr.next()
                            for kc in range(nkc):
                                k.mm(psp[:, 0:ntok], wb[:, kc, j * 128:(j + 1) * 128], obT[:, kc, a:a + ntok], start=(kc == 0), stop=(kc == nkc - 1))
                            if first:
                                k.tt(mergedT[:, dc, a:a + ntok], psp[:, 0:ntok], sg[:, 0:ntok], ALU.mult)
                            else:
                                k.tt(sg[:, 0:ntok], psp[:, 0:ntok], sg[:, 0:ntok], ALU.mult)
                                k.tt(mergedT[:, dc, a:a + ntok], mergedT[:, dc, a:a + ntok], sg[:, 0:ntok], ALU.add, eng="pool")

        def dil_branch():
          with k.phase():
            KT = k.sb("KT", [128, 3, TOK], BF16)
            VA = k.sb("VA", [128, NT, 3, 129], BF16)
            k.memset(VA.v(), 1.0)
            QT_r = Ring([k.sb("QT%d" % i, [128, 3, 512], BF16) for i in range(2)])
            Wd = k.sb("Wd", [128, 8, 1344], BF16)
            rope_r = Ring([k.sb("rope%d" % i, [32, 2, 512]) for i in range(1)])
            dmask = k.sb("dmask", [128, 9, 128], BF16)
            k.dma(k.pool, dmask.v(), c_dilmask.v())
            smask = k.sb("smask", [128, 9, 4], BF16)
            k.dma(k.pool, smask.v(), c_smask.v())
            nmask = k.sb("nmask", [64, 3, 64], BF16)
            k.dma(k.pool, nmask.v(), c_nmask.v())
            ktmp_r = Ring([k.sb("ktmp%d" % i, [128, 512]) for i in range(1)])
            rt_r = Ring([k.sb("rt%d" % i, [32, 2, 512]) for i in range(1)])
            E_r = Ring([k.sb("E%d" % i, [128, 128], BF16) for i in range(2)])
            E4_r = Ring([k.sb("E4_%d" % i, [128, 4, 128], BF16) for i in range(3)])
            o_r = Ring([k.sb("o%d" % i, [128, 128], BF16) for i in range(2)])
            stg_r = Ring([k.sb("stg%d" % i, [128, 384]) for i in range(2)])
            Kc_r = Ring([k.sb("Kc%d" % i, [128, 9, 128]) for i in range(2)])
            Vc_r = Ring([k.sb("Vc%d" % i, [128, 9, 128]) for i in range(1)])
            Vcb_r = Ring([k.sb("Vcb%d" % i, [128, 9, 129], BF16) for i in range(1)])
            for b in Vcb_r.bufs:
                k.memset(b.v(), 1.0)
            KcT_r = Ring([k.sb("KcT%d" % i, [128, 9, 128], BF16) for i in range(2)])
            PTz_r = Ring([k.sb("PTz%d" % i, [128, 9, 64], BF16) for i in range(2)])
            for b in PTz_r.bufs:
                k.memset(b.v(), 0.0)
            win = W["w_in"]
            flip = [0]

            def rope_rows(dst32, ps, ps2, rope, ntok):
                r = rt_r.next()
                k.tt(r[:, 0, 0:ntok], ps[0:32, 0:ntok], rope[:, 0, 0:ntok], ALU.mult)
                k.tt(r[:, 1, 0:ntok], ps2[0:32, 0:ntok], rope[:, 1, 0:ntok], ALU.mult)
                k.tt(dst32, r[:, 0, 0:ntok], r[:, 1, 0:ntok], ALU.add, eng="pool")

            for hh in range(4):
                for g in range(3):
                    head = g * 4 + hh
                    for sec in range(3):
                        c0 = C_DIL + sec * 1536 + head * 128
                        load_w(Wd[:, :, (sec * 3 + g) * 128:(sec * 3 + g + 1) * 128], V(win.t[l, :, c0:c0 + 128].rearrange("(c p) n -> p c n", p=128), [win]))
                        if sec < 2:
                            o = 1152 + (sec * 3 + g) * 32
                            load_w(Wd[:, :, o:o + 16], V(win.t[l, :, c0 + 16:c0 + 32].rearrange("(c p) n -> p c n", p=128), [win]))
                            load_w(Wd[:, :, o + 16:o + 32], V(win.t[l, :, c0:c0 + 16].rearrange("(c p) n -> p c n", p=128), [win]))
                for tb in range(5):
                    t0, t1 = tb * 4, min(tb * 4 + 4, NT)
                    ntok = min(512, TOK - tb * 512)
                    a = tb * 512
                    QT = QT_r.next()
                    rope = rope_r.next()
                    k.dma(k.sp, rope[:, :, 0:ntok], V(c_rope.t[:, :, a:a + ntok].rearrange("c p n -> p c n"), [c_rope]))
                    for g in range(3):
                        for sec in range(2):
                            ps = psr.next()
                            c = (sec * 3 + g) * 128
                            for kc in range(8):
                                k.mm(ps[:, 0:ntok], Wd[:, kc, c:c + 128], hT(kc, t0, t1), start=(kc == 0), stop=(kc == 7))
                            ps2 = psr.next()
                            o = 1152 + (sec * 3 + g) * 32
                            for kc in range(8):
                                k.mm(ps2[0:32, 0:ntok], Wd[:, kc, o:o + 32], hT(kc, t0, t1), start=(kc == 0), stop=(kc == 7))
                            if sec == 0:
                                k.cp(QT[32:64, g, 0:ntok], ps[32:64, 0:ntok], eng="act")
                                k.cp(QT[64:128, g, 0:ntok], ps[64:128, 0:ntok], eng="act")
                                rope_rows(QT[0:32, g, 0:ntok], ps, ps2, rope, ntok)
                            else:
                                kt = ktmp_r.next()
                                k.cp(kt[32:64, 0:ntok], ps[32:64, 0:ntok], eng="act")
                                k.cp(kt[64:128, 0:ntok], ps[64:128, 0:ntok], eng="act")
                                rope_rows(kt[0:32, 0:ntok], ps, ps2, rope, ntok)
                                k.cp(KT[:, g, a:a + ntok], kt[:, 0:ntok], eng="pool")
                                for t in range(t0, t1):
                                    n = tn(t)
                                    if t == 16:
                                        dst = swin_k[g][l, :, hh * 128:(hh + 1) * 128]
                                    else:
                                        first_t = 16 - DIL_W[g] // 128
                                        if t < first_t:
                                            continue
                                        r0 = (t - first_t) * 128
                                        dst = pwin_k[g][l, r0:r0 + 128, hh * 128:(hh + 1) * 128]
                                    pst = psr.next()
                                    k.tr(pst[0:n, 0:128], kt[:, (t - t0) * 128:(t - t0) * 128 + n], ident_f.v())
                                    st = stg_r.next()
                                    k.cp(st[0:n, 0:128], pst[0:n, 0:128])
                                    k.dma(k.sp, dst, st[0:n, 0:128])
                    for t in range(t0, t1):
                        n = tn(t)
                        ps = psr.next()
                        for kc in range(8):
                            k.mm(ps[0:n, 0:384], hT(kc, t, t + 1), Wd[:, kc, 768:1152], start=(kc == 0), stop=(kc == 7))
                        k.cp(VA[0:n, t, :, 0:128], ps[0:n, 0:384].rr("p (g e) -> p g e", g=3), eng="act")
                        st = stg_r.next()
                        k.cp(st[0:n, :], ps[0:n, 0:384])
                        for g in range(3):
                            if t == 16:
                                dst = swin_v[g][l, :, hh * 128:(hh + 1) * 128]
                            else:
                                first_t = 16 - DIL_W[g] // 128
                                if t < first_t:
                                    continue
                                r0 = (t - first_t) * 128
                                dst = pwin_v[g][l, r0:r0 + 128, hh * 128:(hh + 1) * 128]
                            k.dma(k.sp, dst, st[0:n, g * 128:(g + 1) * 128])
                    if tb < 4:
                        for qt in range(t0, t1):
                            acc = accr.next()
                            blocks = []
                            for g in range(3):
                                nd = DIL_W[g] // 128
                                for delta in range(0, min(nd, qt) + 1):
                                    kind = 0 if delta == 0 else (2 if delta == nd else 1)
                                    blocks.append((g, qt - delta, kind))
                            chunks = [blocks[i:i + 4] for i in range(0, len(blocks), 4)]
                            nblk = len(blocks)

                            def stage1(ch):
                                ps2 = psr.next()
                                for c, (g, kb, kind) in enumerate(ch):
                                    k.mm(ps2[:, c * 128:(c + 1) * 128], KT[:, g, kb * 128:(kb + 1) * 128], QT[:, g, (qt - t0) * 128:(qt - t0 + 1) * 128])
                                E4 = E4_r.next()
                                ncb = len(ch)
                                k.actf(E4[:, 0:ncb, :], ps2[:, 0:ncb * 128].rr("p (c j) -> p c j", c=ncb), AF.Exp, scale=SC)
                                c = 0
                                while c < ncb:
                                    mi = ch[c][0] * 3 + ch[c][2]
                                    e = c
                                    while e + 1 < ncb and ch[e + 1][0] * 3 + ch[e + 1][2] == mi:
                                        e += 1
                                    flip[0] ^= 1
                                    k.tt(E4[:, c:e + 1, :], E4[:, c:e + 1, :], dmask[:, mi, :].unsq(1).bc([128, e + 1 - c, 128]), ALU.mult,
                                         eng=("dve" if flip[0] else "pool"))
                                    c = e + 1
                                return E4

                            def stage2(ch, E4, base):
                                for c, (g, kb, kind) in enumerate(ch):
                                    k.mm(acc[:, 0:129], E4[:, c, :], VA[:, kb, g, :], start=(base + c == 0), stop=(base + c == nblk - 1))

                            Es = [None] * len(chunks)
                            Es[0] = stage1(chunks[0])
                            for ci in range(len(chunks)):
                                if ci + 1 < len(chunks):
                                    Es[ci + 1] = stage1(chunks[ci + 1])
                                stage2(chunks[ci], Es[ci], ci * 4)
                            sm = small.next()
                            k.recip(sm[:, 0:1], acc[:, 128:129])
                            o = o_r.next()
                            k.ts(o.v(), acc[:, 0:128], sm[:, 0:1], None, op0=ALU.mult)
                            pst = psr.next()
                            psb = pst.v().bitcast(BF16)
                            k.tr(psb[:, 0:128], o.v(), ident_b.v())
                            k.cp(obT[:, hh, qt * 128:(qt + 1) * 128], psb[:, 0:128], eng="act")
                    else:
                        acc = accr.next()
                        for g in range(3):
                            ps2 = psr.next()
                            k.mm(ps2[0:64, 0:64], KT[:, g, SEQ:TOK], QT[:, g, 0:64])
                            E = E_r.next()
                            k.actf(E[0:64, 0:64], ps2[0:64, 0:64], AF.Exp, scale=SC)
                            k.tt(E[0:64, 0:64], E[0:64, 0:64], nmask[:, g, :], ALU.mult)
                            k.mm(acc[0:64, 0:129], E[0:64, 0:64], VA[0:64, 16, g, :], start=(g == 0), stop=False)
                        for s in range(NSS):
                            Kc = Kc_r.next()
                            Vc = Vc_r.next()
                            hs = slice(hh * 128, (hh + 1) * 128)
                            for (src, dstb) in ((cwin_k, Kc), (cwin_v, Vc)):
                                k.dma(k.sp, dstb[:, 0, :], V(src[0].t[l, s, :, hs], [src[0]]))
                                k.dma(k.sp, dstb[:, 1:5, :], V(src[1].t[l, s, :, hs].rearrange("(b p) n -> p b n", p=128), [src[1]]))
                                k.dma(k.sp, dstb[:, 5:9, :], V(src[2].t[l, s, :, hs].rearrange("(m x) n -> m x n", x=16)[:, 0:4, :], [src[2]]))
                            Vcb = Vcb_r.next()
                            k.cp(Vcb[:, :, 0:128], Vc.v(), eng="pool")
                            KcT = KcT_r.next()
                            for grp in range(3):
                                nb = 4 if grp < 2 else 1
                                pst = psr.next()
                                for j in range(nb):
                                    k.tr(pst[:, j * 128:(j + 1) * 128], Kc[:, grp * 4 + j, :], ident_f.v())
                                k.cp(KcT[:, grp * 4:grp * 4 + nb, :], pst[:, 0:nb * 128].rr("p (c j) -> p c j", c=nb), eng="act")
                            ps2 = psr.next()
                            for b in range(9):
                                g = 0 if b == 0 else (1 if b < 5 else 2)
                                k.mm(ps2[:, b * 4:(b + 1) * 4], KcT[:, b, :], QT[:, g, s * 4:(s + 1) * 4])
                            PTz = PTz_r.next()
                            k.actf(PTz[:, :, s * 4:(s + 1) * 4], ps2[:, 0:36].rr("p (c j) -> p c j", c=9), AF.Exp, scale=SC)
                            k.tt(PTz[:, :, s * 4:(s + 1) * 4], PTz[:, :, s * 4:(s + 1) * 4], smask.v(), ALU.mult)
                            for b in range(9):
                                k.mm(acc[0:64, 0:129], PTz[:, b, :], Vcb[:, b, :], start=False, stop=(s == NSS - 1 and b == 8))
                            k.memset(PTz[:, :, s * 4:(s + 1) * 4], 0.0)
                        sm = small.next()
                        k.recip(sm[0:64, 0:1], acc[0:64, 128:129])
                        o = o_r.next()
                        k.ts(o[0:64, :], acc[0:64, 0:128], sm[0:64, 0:1], None, op0=ALU.mult)
                        pst = psr.next()
                        psb = pst.v().bitcast(BF16)
                        k.tr(psb[:, 0:64], o[0:64, :], ident_b[0:64, 0:64])
                        k.cp(obT[:, hh, SEQ:TOK], psb[:, 0:64], eng="act")

        def ssd_branch():
          with k.phase():
            win = W["w_in"]
            Wdt = k.sb("Wdt", [128, 8, 16], BF16)
            sc_dt = k.sb("sc_dt", [128, NT, 16])
            sc_ac = k.sb("sc_ac", [128, NT, 16])
            sc_al = k.sb("sc_al", [128, NT, 16])
            sc_ed = k.sb("sc_ed", [128, NT, 16])
            dtb = k.sb("dtb", [128, 16])
            Aneg = k.sb("Aneg", [128, 16])
            Dsk = k.sb("Dsk", [128, 16])
            gB = k.sb("gB", [128, 1024])
            tri = k.sb("tri", [128, 2, 128])
            segm = k.sb("segm", [128, 2, 128])
            ones_f = k.sb("ones_f", [128, 128])
            segcol = k.sb("segcol", [128, NSS, 64], BF16)
            rowmask = k.sb("rowmask", [64, NSS], BF16)
            k.dma(k.sp, tri.v(), c_tri.v())
            k.dma(k.sp, segm.v(), c_segm.v())
            k.dma(k.pool, segcol.v(), c_segcol.v())
            k.dma(k.pool, rowmask.v(), c_rowmask.v())
            k.memset(ones_f.v(), 1.0)
            k.dma(k.sp, dtb.v(), V(ssm_dt_bias.t[l, :].partition_broadcast(128), [ssm_dt_bias]))
            k.dma(k.sp, Aneg.v(), V(ssm_a_log.t[l, :].partition_broadcast(128), [ssm_a_log]))
            k.actf(Aneg.v(), Aneg.v(), AF.Exp)
            k.ts(Aneg.v(), Aneg.v(), -1.0, None, op0=ALU.mult)
            k.dma(k.sp, Dsk.v(), V(ssm_d.t[l, :].partition_broadcast(128), [ssm_d]))
            k.dma(k.sp, gB.v(), V(ssm_norm.t[l, :].partition_broadcast(128), [ssm_norm]))
            load_w(Wdt.v(), V(win.t[l, :, C_SSDT:C_SSDT + 16].rearrange("(c p) n -> p c n", p=128), [win]))
            tmp16 = Ring([k.sb("tmp16_%d" % i, [128, 16]) for i in range(3)])
            for t in range(NT):
                n = tn(t)
                v = 0 if t < 16 else 1
                ps = psr.next()
                for kc in range(8):
                    k.mm(ps[0:n, 0:16], hT(kc, t, t + 1), Wdt[:, kc, :], start=(kc == 0), stop=(kc == 7))
                x1 = tmp16.next()
                k.tt(x1[0:n, :], ps[0:n, 0:16], dtb[0:n, :], ALU.add)
                k.actf(x1[0:n, :], x1[0:n, :], AF.Exp)
                k.actf(sc_dt[0:n, t, :], x1[0:n, :], AF.Ln, bias=1.0)
                a = tmp16.next()
                k.tt(a[0:n, :], sc_dt[0:n, t, :], Aneg[0:n, :], ALU.mult)
                ps2 = psr.next()
                k.mm(ps2[0:n, 0:16], tri[0:n, v, 0:n], a[0:n, :])
                k.cp(sc_ac[0:n, t, :], ps2[0:n, 0:16])
                ps3 = psr.next()
                k.mm(ps3[0:n, 0:16], segm[0:n, v, 0:n], a[0:n, :])
                k.cp(sc_al[0:n, t, :], ps3[0:n, 0:16], eng="act")
                e = tmp16.next()
                k.tt(e[0:n, :], sc_al[0:n, t, :], sc_ac[0:n, t, :], ALU.subtract)
                k.actf(sc_ed[0:n, t, :], e[0:n, :], AF.Exp)


            def group_prog(gi):
                cTb = k.sb("cTs", [128, 4, 512], BF16)
                pre_r = Ring([k.sb("pre%d" % i, [128, 515]) for i in range(1)])
                halo = k.sb("halo", [128, 4, 3])
                pre_s = k.sb("pre_s", [128, 4, NSS, 7])
                Wg = k.sb("Wg", [128, 8, 512], BF16)
                Wz = k.sb("Wz", [128, 8, 256], BF16)
                cw = k.sb("cw", [128, 4, 4])
                cb = k.sb("cb", [128, 4])
                cacc = k.sb("caccs", [128, 576])
                rd_r = Ring([k.sb("rd%d" % i, [128, 4, 128]) for i in range(1)])
                u_r = Ring([k.sb("u%d" % i, [128, 4, 128]) for i in range(2)])
                eR_r = Ring([k.sb("eR%d" % i, [128, 4, 128]) for i in range(2)])
                WT_r = Ring([k.sb("WT%d" % i, [128, 4, 128], BF16) for i in range(2)])
                CE_r = Ring([k.sb("CE%d" % i, [128, 4, 128], BF16) for i in range(2)])
                xt_r = Ring([k.sb("xtok%d" % i, [128, 4, 64], BF16) for i in range(2)])
                xd_r = Ring([k.sb("xdt%d" % i, [128, 4, 64], BF16) for i in range(2)])
                xe_r = Ring([k.sb("xdec%d" % i, [128, 4, 64], BF16) for i in range(2)])
                bt_r = Ring([k.sb("btok%d" % i, [128, 128], BF16) for i in range(2)])
                yy_r = Ring([k.sb("yy%d" % i, [128, 256]) for i in range(2)])
                zs_r = Ring([k.sb("zs%d" % i, [128, 256]) for i in range(1)])
                ob_r = Ring([k.sb("ob%d" % i, [128, 256], BF16) for i in range(2)])
                eal_r = Ring([k.sb("eal%d" % i, [128, 4]) for i in range(2)])
                hst = k.sb("hst", [128, 4, 64])
                hstb = k.sb("hstb", [128, 4, 64], BF16)
                hs_r = Ring([k.sb("hs%d" % i, [128, 4, 64]) for i in range(2)])
                hsb_r = Ring([k.sb("hsb%d" % i, [128, 4, 64], BF16) for i in range(2)])
                CEm_r = Ring([k.sb("CEm%d" % i, [128, 4, 64], BF16) for i in range(2)])
                Bm_r = Ring([k.sb("Bm%d" % i, [64, 128], BF16) for i in range(2)])
                stl_r = Ring([k.sb("stl%d" % i, [128, 2, 128]) for i in range(1)])
                sto_r = Ring([k.sb("sto%d" % i, [128, 128]) for i in range(2)])
                cx = C_SSX + gi * 256
                load_w(Wg[:, :, 0:256], V(win.t[l, :, cx:cx + 256].rearrange("(c p) n -> p c n", p=128), [win]))
                cbb = C_SSX + 1024 + gi * 128
                load_w(Wg[:, :, 256:384], V(win.t[l, :, cbb:cbb + 128].rearrange("(c p) n -> p c n", p=128), [win]))
                ccc = C_SSX + 1536 + gi * 128
                load_w(Wg[:, :, 384:512], V(win.t[l, :, ccc:ccc + 128].rearrange("(c p) n -> p c n", p=128), [win]))
                cz = C_SSZ + gi * 256
                load_w(Wz.v(), V(win.t[l, :, cz:cz + 256].rearrange("(c p) n -> p c n", p=128), [win]))
                ch0 = [gi * 256, gi * 256 + 128, 1024 + gi * 128, 1536 + gi * 128]
                with nc.allow_non_contiguous_dma(reason="tiny conv params"):
                    for b in range(4):
                        k.dma(k.sp, cw[:, b, :], V(ssm_conv_w.t[l, :, ch0[b]:ch0[b] + 128].rearrange("j p -> p j"), [ssm_conv_w]))
                        k.dma(k.sp, cb[:, b:b + 1], V(ssm_conv_b.t[l, ch0[b]:ch0[b] + 128].rearrange("(p o) -> p o", o=1), [ssm_conv_b]))
                stc = sto_r.next()
                for b in range(4):
                    stc = sto_r.next()
                    k.dma(k.sp, stc[0:48, :], V(state_ssm_conv.t[l, :, :, ch0[b]:ch0[b] + 128].rearrange("s j n -> (s j) n"), [state_ssm_conv]))
                    pst = psr.next()
                    k.tr(pst[:, 0:48], stc[0:48, :], ident_f[0:48, 0:48])
                    k.cp(pre_s[:, b, :, 0:3], pst[:, 0:48].rr("p (s j) -> p s j", j=3))
                k.memset(hst.v(), 0.0)
                k.memset(hstb.v(), 0.0)
                prev_pre = None
                for tb in range(5):
                    t0, t1 = tb * 4, min(tb * 4 + 4, NT)
                    ntok = min(512, TOK - tb * 512)
                    a0 = tb * 512
                    if tb == 0:
                        k.memset(halo.v(), 0.0)
                    for b in range(4):
                        ps = psr.next()
                        for kc in range(8):
                            k.mm(ps[:, 0:ntok], Wg[:, kc, b * 128:(b + 1) * 128], hT(kc, t0, t1), start=(kc == 0), stop=(kc == 7))
                        if tb < 4:
                            pre = pre_r.next()
                            k.cp(pre[:, 0:3], halo[:, b, :], eng="pool")
                            k.cp(pre[:, 3:515], ps.v(), eng="act")
                            k.cp(halo[:, b, :], pre[:, 512:515], eng="pool")
                            src = lambda j, pre=pre: pre[:, j:j + 512]
                            acc = cacc[:, 0:512]
                            dst = cTb[:, b, 0:512]
                        else:
                            k.cp(pre_s[:, b, :, 3:7], ps[:, 0:64].rr("p (s j) -> p s j", j=4), eng="act")
                            src = lambda j: pre_s[:, b, :, j:j + 4]
                            acc = cacc[:, 0:64].rr("p (s j) -> p s j", j=4)
                            dst = cTb[:, b, 0:64].rr("p (s j) -> p s j", j=4)
                        k.ts(acc, src(0), cw[:, b, 0:1], cb[:, b:b + 1], op0=ALU.mult, op1=ALU.add)
                        for j in range(1, 4):
                            k.stt(acc, src(j), cw[:, b, j:j + 1], acc, ALU.mult, ALU.add)
                        k.actf(dst, acc, AF.Silu)
                        if tb == 3 or tb == 4:
                            pst = psr.next()
                            sto = sto_r.next()
                            if tb == 3:
                                k.tr(pst[0:3, 0:128], pre[:, 512:515], ident_f.v())
                                k.cp(sto[0:3, :], pst[0:3, 0:128])
                                k.dma(k.sp, p_ssm_conv[l, :, ch0[b]:ch0[b] + 128], sto[0:3, :])
                            else:
                                cst = cacc[:, 512:560]
                                k.cp(cst.rr("p (s j) -> p s j", j=3), pre_s[:, b, :, 4:7], eng="pool")
                                k.tr(pst[0:48, 0:128], cst, ident_f.v())
                                k.cp(sto[0:48, :], pst[0:48, 0:128])
                                k.dma(k.sp, V(s_ssm_conv.t[l, :, :, ch0[b]:ch0[b] + 128].rearrange("s j n -> (s j) n"), [s_ssm_conv]), sto[0:48, :])
                    for t in range(t0, t1):
                        n = tn(t)
                        v = 0 if t < 16 else 1
                        ta = t * 128
                        lo = (t - t0) * 128
                        hs4 = slice(gi * 4, gi * 4 + 4)
                        ac = sc_ac[0:n, t, hs4]
                        rd = rd_r.next()
                        k.tt(rd[0:n, :, 0:n], ident_f[0:n, 0:n].unsq(1).bc([n, 4, n]), ac.unsq(2).bc([n, 4, n]), ALU.mult, eng="pool")
                        Rp = psr.next()
                        k.mm(Rp[:, 0:4 * n], ones_f[0:n, :], rd[0:n, :, 0:n])
                        R3 = Rp[:, 0:4 * n].rr("p (h f) -> p h f", h=4)
                        u = u_r.next()
                        k.tt(u[0:n, :, 0:n], R3[0:n], ac.unsq(2).bc([n, 4, n]), ALU.subtract)
                        k.ts(u[0:n, :, 0:n], u[0:n, :, 0:n], 0.0, None, op0=ALU.min, eng="pool")
                        k.actf(u[0:n, :, 0:n], u[0:n, :, 0:n], AF.Exp)
                        k.tt(u[0:n, :, 0:n], u[0:n, :, 0:n], tri[0:n, v, 0:n].unsq(1).bc([n, 4, n]), ALU.mult, eng="pool")
                        eR = eR_r.next()
                        k.actf(eR[:, :, 0:n], R3, AF.Exp)
                        eal = eal_r.next()
                        if t < 16:
                            k.cp(eal.v(), eR[:, :, n - 1])
                        cbp = psr.next()
                        k.mm(cbp[0:n, 0:n], cTb[:, 2, lo:lo + n], cTb[:, 3, lo:lo + n])
                        WT = WT_r.next()
                        k.tt(WT[0:n, :, 0:n], cbp[0:n, 0:n].unsq(1).bc([n, 4, n]), u[0:n, :, 0:n], ALU.mult)
                        CE = CE_r.next()
                        k.tt(CE[:, :, 0:n], cTb[:, 3, lo:lo + n].unsq(1).bc([128, 4, n]), eR[:, :, 0:n], ALU.mult, eng="pool")
                        xp = psr.next()
                        xpb = xp.v().bitcast(BF16)
                        for b in range(2):
                            k.tr(xpb[0:n, b * 128:(b + 1) * 128], cTb[:, b, lo:lo + n], ident_b.v())
                        xtok = xt_r.next()
                        k.cp(xtok[0:n], xpb[0:n, 0:256].rr("p (h q) -> p h q", h=4), eng="act")
                        xdt = xd_r.next()
                        k.tt(xdt[0:n], xtok[0:n], sc_dt[0:n, t, hs4].unsq(2).bc([n, 4, 64]), ALU.mult, eng="pool")
                        xdec = xe_r.next()
                        k.tt(xdec[0:n], xdt[0:n], sc_ed[0:n, t, hs4].unsq(2).bc([n, 4, 64]), ALU.mult, eng="pool")
                        bp = psr.next()
                        bpb = bp.v().bitcast(BF16)
                        k.tr(bpb[0:n, 0:128], cTb[:, 2, lo:lo + n], ident_b.v())
                        btok = bt_r.next()
                        k.cp(btok[0:n, :], bpb[0:n, 0:128], eng="act")
                        yp = accr.next()
                        if t < 16:
                            for h in range(4):
                                k.mm(yp[0:n, h * 64:(h + 1) * 64], WT[0:n, h, 0:n], xdt[0:n, h, :], start=(h == 0), stop=False)
                                k.mm(yp[0:n, h * 64:(h + 1) * 64], CE[:, h, 0:n], hstb[:, h, :], start=False, stop=(h == 3))
                        else:
                            for h in range(4):
                                k.mm(yp[0:n, h * 64:(h + 1) * 64], WT[0:n, h, 0:n], xdt[0:n, h, :], start=(h == 0), stop=False)
                            for s in range(NSS):
                                stl = stl_r.next()
                                k.dma(k.sp, stl.v(), V(state_ssm.t[l, s, gi * 4:gi * 4 + 4].rearrange("(a h) p n -> (h p) a n", h=2), [state_ssm]))
                                pst = psr.next()
                                for a2 in range(2):
                                    k.tr(pst[:, a2 * 128:(a2 + 1) * 128], stl[:, a2, :], ident_f.v())
                                hs = hs_r.next()
                                hsb = hsb_r.next()
                                k.cp(hs.v(), pst[:, 0:256].rr("p (h q) -> p h q", h=4))
                                k.cp(hsb.v(), hs.v(), eng="act")
                                CEm = CEm_r.next()
                                k.tt(CEm.v(), CE[:, :, 0:64], segcol[:, s, :].unsq(1).bc([128, 4, 64]), ALU.mult, eng="pool")
                                for h in range(4):
                                    k.mm(yp[0:n, h * 64:(h + 1) * 64], CEm[:, h, :], hsb[:, h, :], start=False, stop=(h == 3 and s == NSS - 1))
                                Bm = Bm_r.next()
                                k.ts(Bm.v(), btok[0:64, :], rowmask[:, s:s + 1], None, op0=ALU.mult, eng="pool")
                                hp = psr.next()
                                k.mm(hp[:, 0:256], Bm.v(), xdec[0:64].rr("p h q -> p (h q)"))
                                eals = eal_r.next()
                                k.cp(eals.v(), eR[:, :, 4 * s + 3])
                                k.tt(hs.v(), hs.v(), eals.v().unsq(2).bc([128, 4, 64]), ALU.mult, eng="pool")
                                k.tt(hs.v(), hs.v(), hp[:, 0:256].rr("p (h q) -> p h q", h=4), ALU.add)
                                for a2 in range(2):
                                    pst2 = psr.next()
                                    k.tr(pst2[:, 0:128], hs[:, a2 * 2:a2 * 2 + 2, :].rr("p h q -> p (h q)"), ident_f.v())
                                    sto = sto_r.next()
                                    k.cp(sto.v(), pst2[:, 0:128], eng="act")
                                    k.dma(k.sp, V(s_ssm.t[l, s, gi * 4 + a2 * 2:gi * 4 + a2 * 2 + 2].rearrange("h p n -> (h p) n"), [s_ssm]), sto.v())
                        if t < 16:
                            hp = psr.next()
                            k.mm(hp[:, 0:256], btok[0:n, :], xdec[0:n].rr("p h q -> p (h q)"))
                            k.tt(hst.v(), hst.v(), eal.v().unsq(2).bc([128, 4, 64]), ALU.mult, eng="pool")
                            k.tt(hst.v(), hst.v(), hp[:, 0:256].rr("p (h q) -> p h q", h=4), ALU.add)
                            k.cp(hstb.v(), hst.v(), eng="act")
                            if t == 15:
                                for a2 in range(2):
                                    pst = psr.next()
                                    k.tr(pst[:, 0:128], hst[:, a2 * 2:a2 * 2 + 2, :].rr("p h q -> p (h q)"), ident_f.v())
                                    sto = sto_r.next()
                                    k.cp(sto.v(), pst[:, 0:128])
                                    k.dma(k.sp, V(p_ssm.t[l, gi * 4 + a2 * 2:gi * 4 + a2 * 2 + 2].rearrange("h p n -> (h p) n"), [p_ssm]), sto.v())
                        yy = yy_r.next()
                        k.tt(yy[0:n].rr("p (h q) -> p h q", h=4), xtok[0:n], Dsk[0:n, hs4].unsq(2).bc([n, 4, 64]), ALU.mult, eng="pool")
                        k.tt(yy[0:n], yy[0:n], yp[0:n, 0:256], ALU.add)
                        zp = psr.next()
                        for kc in range(8):
                            k.mm(zp[0:n, 0:256], hT(kc, t, t + 1), Wz[:, kc, :], start=(kc == 0), stop=(kc == 7))
                        zs = zs_r.next()
                        k.actf(zs[0:n], zp[0:n, 0:256], AF.Silu)
                        k.tt(yy[0:n], yy[0:n], zs[0:n], ALU.mult, eng="pool")
                        r = rstd_of(yy[0:n], n, 1.0 / 256)
                        ob = ob_r.next()
                        k.stt(ob[0:n], yy[0:n], r, gB[0:n, gi * 256:(gi + 1) * 256], ALU.mult, ALU.mult)
                        op_ = psr.next()
                        opb = op_.v().bitcast(BF16)
                        for b in range(2):
                            k.tr(opb[:, b * 128:b * 128 + n], ob[0:n, b * 128:(b + 1) * 128], ident_b[0:n, 0:n])
                        k.cp(obT[:, gi * 2:gi * 2 + 2, ta:ta + n], opb[:, 0:256].rr("p (c j) -> p c j", c=2)[:, :, 0:n], eng="act")

            for gp in range(2):
                with k.phase():
                    if 'noil' in FLAGS:
                        group_prog(2 * gp)
                        group_prog(2 * gp + 1)
                    else:
                        k.interleave([lambda g=2 * gp: group_prog(g), lambda g=2 * gp + 1: group_prog(g)])

        def dn_branch():
          with k.phase():
            win = W["w_in"]
            Wba = k.sb("Wba", [128, 8, 16], BF16)
            sc_b = k.sb("sc_b", [128, NT, 8])
            sc_nb = k.sb("sc_nb", [128, NT, 8])
            sc_gc = k.sb("sc_gc", [128, NT, 8])
            sc_eg = k.sb("sc_eg", [128, NT, 8])
            sc_ed = k.sb("sc_edd", [128, NT, 8])
            dtb = k.sb("dtbd", [128, 8])
            Aneg = k.sb("Anegd", [128, 8])
            gB = k.sb("gBd", [128, 128])
            tri = k.sb("trid", [128, 2, 128])
            stri = k.sb("strid", [128, 2, 128])
            segm = k.sb("segmd", [128, 2, 128])
            ones_f = k.sb("ones_fd", [128, 128])
            segcol = k.sb("segcold", [128, NSS, 64], BF16)
            rowmask = k.sb("rowmaskd", [64, NSS], BF16)
            k.dma(k.sp, tri.v(), c_tri.v())
            k.dma(k.sp, stri.v(), c_stri.v())
            k.dma(k.sp, segm.v(), c_segm.v())
            k.dma(k.pool, segcol.v(), c_segcol.v())
            k.dma(k.pool, rowmask.v(), c_rowmask.v())
            k.memset(ones_f.v(), 1.0)
            k.dma(k.sp, dtb.v(), V(dn_dt_bias.t[l, :].partition_broadcast(128), [dn_dt_bias]))
            k.dma(k.sp, Aneg.v(), V(dn_a_log.t[l, :].partition_broadcast(128), [dn_a_log]))
            k.actf(Aneg.v(), Aneg.v(), AF.Exp)
            k.ts(Aneg.v(), Aneg.v(), -1.0, None, op0=ALU.mult)
            k.dma(k.sp, gB.v(), V(dn_norm.t[l, :].partition_broadcast(128), [dn_norm]))
            load_w(Wba.v(), V(win.t[l, :, C_DNB:C_DNB + 16].rearrange("(c p) n -> p c n", p=128), [win]))
            tmp8 = Ring([k.sb("tmp8_%d" % i, [128, 8]) for i in range(4)])
            for t in range(NT):
                n = tn(t)
                v = 0 if t < 16 else 1
                ps = psr.next()
                for kc in range(8):
                    k.mm(ps[0:n, 0:16], hT(kc, t, t + 1), Wba[:, kc, :], start=(kc == 0), stop=(kc == 7))
                k.actf(sc_b[0:n, t, :], ps[0:n, 0:8], AF.Sigmoid)
                k.ts(sc_nb[0:n, t, :], sc_b[0:n, t, :], -1.0, None, op0=ALU.mult, eng="pool")
                x1 = tmp8.next()
                k.tt(x1[0:n, :], ps[0:n, 8:16], dtb[0:n, :], ALU.add)
                k.actf(x1[0:n, :], x1[0:n, :], AF.Exp)
                k.actf(x1[0:n, :], x1[0:n, :], AF.Ln, bias=1.0)
                g = tmp8.next()
                k.tt(g[0:n, :], x1[0:n, :], Aneg[0:n, :], ALU.mult)
                ps2 = psr.next()
                k.mm(ps2[0:n, 0:8], tri[0:n, v, 0:n], g[0:n, :])
                k.cp(sc_gc[0:n, t, :], ps2[0:n, 0:8])
                k.actf(sc_eg[0:n, t, :], ps2[0:n, 0:8], AF.Exp)
                ps3 = psr.next()
                k.mm(ps3[0:n, 0:8], segm[0:n, v, 0:n], g[0:n, :])
                e = tmp8.next()
                k.tt(e[0:n, :], ps3[0:n, 0:8], sc_gc[0:n, t, :], ALU.subtract)
                k.actf(sc_ed[0:n, t, :], e[0:n, :], AF.Exp)
            blk = k.sb("blk", [128, 4, 128])
            k.dma(k.sp, blk.v(), c_blk.v())


            def head_prog(h):
                cTb = k.sb("cTb", [128, 3, 512], BF16)
                pre_r = Ring([k.sb("pred%d" % i, [128, 515]) for i in range(1)])
                halo = k.sb("halod", [128, 3, 3])
                pre_s = k.sb("pre_sd", [128, 3, NSS, 7])
                Wh = k.sb("Wh", [128, 8, 384], BF16)
                Wz = k.sb("Wzd", [128, 8, 128], BF16)
                cw = k.sb("cwd", [128, 3, 4])
                cacc = k.sb("cacc", [128, 576])
                f32r = Ring([k.sb("f32_%d" % i, [128, 128]) for i in range(12)])
                eRd_r = Ring([k.sb("eRd%d" % i, [128, 128]) for i in range(2)])
                Nd_r = Ring([k.sb("Nd%d" % i, [128, 128]) for i in range(2)])
                Pd_r = Ring([k.sb("Pd%d" % i, [128, 128]) for i in range(2)])
                o1_r = Ring([k.sb("o1d%d" % i, [128, 128]) for i in range(2)])
                zsd_r = Ring([k.sb("zsd%d" % i, [128, 128]) for i in range(2)])
                b16s = Ring([k.sb("b16s_%d" % i, [128, 128], BF16) for i in range(4)])
                b16r = Ring([k.sb("b16_%d" % i, [128, 128], BF16) for i in range(16)])
                smr = Ring([k.sb("smd%d" % i, [128, 8]) for i in range(6)])
                S = k.sb("S", [128, 128])
                Sb = k.sb("Sb", [128, 128], BF16)
                Sb_all = k.sb("Sb_all", [128, NSS, 128], BF16)
                Ss_r = Ring([k.sb("Ss%d" % i, [128, 128]) for i in range(2)])
                sto_r = Ring([k.sb("stod%d" % i, [128, 128]) for i in range(2)])
                for b in range(3):
                    c0 = C_DNQKV + b * 1024 + h * 128
                    load_w(Wh[:, :, b * 128:(b + 1) * 128], V(win.t[l, :, c0:c0 + 128].rearrange("(c p) n -> p c n", p=128), [win]))
                cz = C_DNZ + h * 128
                load_w(Wz.v(), V(win.t[l, :, cz:cz + 128].rearrange("(c p) n -> p c n", p=128), [win]))
                ch0 = [b * 1024 + h * 128 for b in range(3)]
                with nc.allow_non_contiguous_dma(reason="tiny conv params"):
                    for b in range(3):
                        k.dma(k.sp, cw[:, b, :], V(dn_conv_w.t[l, :, ch0[b]:ch0[b] + 128].rearrange("j p -> p j"), [dn_conv_w]))
                for b in range(3):
                    stc = sto_r.next()
                    k.dma(k.sp, stc[0:48, :], V(state_dn_conv.t[l, :, :, ch0[b]:ch0[b] + 128].rearrange("s j n -> (s j) n"), [state_dn_conv]))
                    pst = psr.next()
                    k.tr(pst[:, 0:48], stc[0:48, :], ident_f[0:48, 0:48])
                    k.cp(pre_s[:, b, :, 0:3], pst[:, 0:48].rr("p (s j) -> p s j", j=3))
                k.dma(k.pool, Sb_all.v(), V(state_dn.t[l, :, h].rearrange("s k v -> k s v"), [state_dn]))
                k.memset(S.v(), 0.0)
                k.memset(Sb.v(), 0.0)
                for tb in range(5):
                    t0, t1 = tb * 4, min(tb * 4 + 4, NT)
                    ntok = min(512, TOK - tb * 512)
                    a0 = tb * 512
                    if tb == 0:
                        k.memset(halo.v(), 0.0)
                    for b in range(3):
                        ps = psr.next()
                        for kc in range(8):
                            k.mm(ps[:, 0:ntok], Wh[:, kc, b * 128:(b + 1) * 128], hT(kc, t0, t1), start=(kc == 0), stop=(kc == 7))
                        if tb < 4:
                            pre = pre_r.next()
                            k.cp(pre[:, 0:3], halo[:, b, :], eng="pool")
                            k.cp(pre[:, 3:515], ps.v(), eng="act")
                            k.cp(halo[:, b, :], pre[:, 512:515], eng="pool")
                            src = lambda j, pre=pre: pre[:, j:j + 512]
                            acc = cacc[:, 0:512]
                            dst = cTb[:, b, 0:512]
                        else:
                            k.cp(pre_s[:, b, :, 3:7], ps[:, 0:64].rr("p (s j) -> p s j", j=4), eng="act")
                            src = lambda j, b=b: pre_s[:, b, :, j:j + 4]
                            acc = cacc[:, 0:64].rr("p (s j) -> p s j", j=4)
                            dst = cTb[:, b, 0:64].rr("p (s j) -> p s j", j=4)
                        k.ts(acc, src(0), cw[:, b, 0:1], None, op0=ALU.mult)
                        for j in range(1, 4):
                            k.stt(acc, src(j), cw[:, b, j:j + 1], acc, ALU.mult, ALU.add)
                        k.actf(dst, acc, AF.Silu)
                        if tb == 3 or tb == 4:
                            pst = psr.next()
                            sto = sto_r.next()
                            if tb == 3:
                                k.tr(pst[0:3, 0:128], pre[:, 512:515], ident_f.v())
                                k.cp(sto[0:3, :], pst[0:3, 0:128])
                                k.dma(k.sp, p_dn_conv[l, :, ch0[b]:ch0[b] + 128], sto[0:3, :])
                            else:
                                cst = cacc[:, 512:560]
                                k.cp(cst.rr("p (s j) -> p s j", j=3), pre_s[:, b, :, 4:7], eng="pool")
                                k.tr(pst[0:48, 0:128], cst, ident_f.v())
                                k.cp(sto[0:48, :], pst[0:48, 0:128])
                                k.dma(k.sp, V(s_dn_conv.t[l, :, :, ch0[b]:ch0[b] + 128].rearrange("s j n -> (s j) n"), [s_dn_conv]), sto[0:48, :])
                    for t in range(t0, t1):
                        n = tn(t)
                        v = 0 if t < 16 else 1
                        ta = t * 128
                        samp = (t == 16)
                        beta = sc_b[0:n, t, h:h + 1]
                        nbeta = sc_nb[0:n, t, h:h + 1]
                        gc = sc_gc[0:n, t, h:h + 1]
                        tp = psr.next()
                        tpb = tp.v().bitcast(BF16)
                        for b in range(3):
                            k.tr(tpb[0:n, b * 128:(b + 1) * 128], cTb[:, b, (t - t0) * 128:(t - t0) * 128 + n], ident_b.v())
                        sm = smr.next()
                        jk = junk_r.next()
                        k.actf(jk[0:n, 0:128], tpb[0:n, 0:128], AF.Square, accum=sm[0:n, 0:1])
                        k.actf(jk[0:n, 128:256], tpb[0:n, 128:256], AF.Square, accum=sm[0:n, 1:2])
                        k.ts(sm[0:n, 2:4], sm[0:n, 0:2], EPS, None, op0=ALU.add)
                        k.actf(sm[0:n, 2:4], sm[0:n, 2:4], AF.Sqrt)
                        k.recip(sm[0:n, 4:6], sm[0:n, 2:4])
                        sc2 = smr.next()
                        k.ts(sc2[0:n, 0:1], sm[0:n, 4:5], SC, None, op0=ALU.mult, eng="pool")
                        k.tt(sc2[0:n, 1:2], sm[0:n, 5:6], sc_eg[0:n, t, h:h + 1], ALU.mult, eng="pool")
                        k.tt(sc2[0:n, 2:3], sm[0:n, 5:6], sc_ed[0:n, t, h:h + 1], ALU.mult, eng="pool")
                        qn, kn, ke, kd, vv = [b16r.next() for _ in range(5)]
                        k.actf(qn[0:n], tpb[0:n, 0:128], AF.Copy, scale=sc2[0:n, 0:1])
                        k.ts(kn[0:n], tpb[0:n, 128:256], sm[0:n, 5:6], None, op0=ALU.mult)
                        k.actf(ke[0:n], tpb[0:n, 128:256], AF.Copy, scale=sc2[0:n, 1:2])
                        k.ts(kd[0:n], tpb[0:n, 128:256], sc2[0:n, 2:3], None, op0=ALU.mult)
                        k.cp(vv[0:n], tpb[0:n, 256:384], eng="act")
                        tp2 = psr.next()
                        tp2b = tp2.v().bitcast(BF16)
                        k.tr(tp2b[:, 0:n], kn[0:n], ident_b[0:n, 0:n])
                        k.tr(tp2b[:, 128:128 + n], qn[0:n], ident_b[0:n, 0:n])
                        knT, qnT = b16r.next(), b16r.next()
                        k.cp(knT[:, 0:n], tp2b[:, 0:n])
                        k.cp(qnT[:, 0:n], tp2b[:, 128:128 + n], eng="act")
                        rd = f32r.next()
                        k.ts(rd[0:n, 0:n], ident_f[0:n, 0:n], gc, None, op0=ALU.mult, eng="pool")
                        Rp = psr.next()
                        k.mm(Rp[:, 0:n], ones_f[0:n, :], rd[0:n, 0:n])
                        DTm = f32r.next()
                        k.ts(DTm[0:n, 0:n], Rp[0:n, 0:n], gc, 0.0, op0=ALU.subtract, op1=ALU.min)
                        k.actf(DTm[0:n, 0:n], DTm[0:n, 0:n], AF.Exp)
                        DTs = f32r.next()
                        k.tt(DTs[0:n, 0:n], DTm[0:n, 0:n], stri[0:n, v, 0:n], ALU.mult, eng="pool")
                        k.tt(DTm[0:n, 0:n], DTm[0:n, 0:n], tri[0:n, v, 0:n], ALU.mult, eng="pool")
                        eR = eRd_r.next()
                        k.actf(eR[:, 0:n], Rp[:, 0:n], AF.Exp)
                        Gp = psr.next()
                        k.mm(Gp[0:n, 0:n], knT[:, 0:n], knT[:, 0:n])
                        P = Pd_r.next()
                        k.stt(P[0:n, 0:n], Gp[0:n, 0:n], nbeta, DTs[0:n, 0:n], ALU.mult, ALU.mult)
                        STp = psr.next()
                        k.mm(STp[0:n, 0:n], knT[:, 0:n], qnT[:, 0:n])
                        STm = b16r.next()
                        k.tt(STm[0:n, 0:n], STp[0:n, 0:n], DTm[0:n, 0:n], ALU.mult)
                        Np = psr.next()
                        k.tr(Np[0:n, 0:n], P[0:n, 0:n], ident_f[0:n, 0:n])
                        N = Nd_r.next()
                        k.cp(N[0:n, 0:n], Np[0:n, 0:n], eng="act")
                        Tt = f32r.next()
                        k.tt(Tt[0:n, 0:n], P[0:n, 0:n], ident_f[0:n, 0:n], ALU.add, eng="pool")
                        if samp:
                            p1 = psr.next()
                            k.mm(p1[0:n, 0:n], P[0:n, 0:n], N[0:n, 0:n])
                            N2 = f32r.next()
                            k.cp(N2[0:n, 0:n], p1[0:n, 0:n], eng="act")
                            p3 = psr.next()
                            k.mm(p3[0:n, 0:n], N2[0:n, 0:n], Tt[0:n, 0:n])
                            Tt2 = f32r.next()
                            k.tt(Tt2[0:n, 0:n], Tt[0:n, 0:n], p3[0:n, 0:n], ALU.add)
                            Tt = Tt2
                        else:
                            Pc, Nc = f32r.next(), f32r.next()
                            k.tt(Pc.v(), P.v(), blk[:, 0, :], ALU.mult, eng="pool")
                            k.tt(Nc.v(), N.v(), blk[:, 0, :], ALU.mult, eng="pool")
                            k.tt(Tt.v(), Pc.v(), ident_f.v(), ALU.add, eng="pool")
                            for lev in range(3):
                                p1 = psr.next()
                                k.mm(p1[:, 0:128], Pc.v(), Nc.v())
                                N2 = f32r.next()
                                k.cp(N2.v(), p1[:, 0:128], eng="act")
                                if lev < 2:
                                    p2 = psr.next()
                                    k.mm(p2[:, 0:128], Nc.v(), Pc.v())
                                    P2 = f32r.next()
                                    k.cp(P2.v(), p2[:, 0:128])
                                    Pc = P2
                                Nc = N2
                                p3 = psr.next()
                                k.mm(p3[:, 0:128], Nc.v(), Tt.v())
                                Tt2 = f32r.next()
                                k.tt(Tt2.v(), Tt.v(), p3[:, 0:128], ALU.add)
                                Tt = Tt2
                            for mi in range(1, 4):
                                tdp = psr.next()
                                k.tr(tdp[:, 0:128], Tt.v(), ident_f.v())
                                Td = f32r.next()
                                k.cp(Td.v(), tdp[:, 0:128], eng="act")
                                Noff = f32r.next()
                                k.tt(Noff.v(), N.v(), blk[:, mi, :], ALU.mult, eng="pool")
                                z1p = psr.next()
                                k.mm(z1p[:, 0:128], Noff.v(), Tt.v())
                                Z1 = f32r.next()
                                k.cp(Z1.v(), z1p[:, 0:128])
                                y1p = psr.next()
                                k.mm(y1p[:, 0:128], Td.v(), Z1.v())
                                Tt2 = f32r.next()
                                k.tt(Tt2.v(), Tt.v(), y1p[:, 0:128], ALU.add)
                                Tt = Tt2
                        TtT = b16r.next()
                        k.cp(TtT[0:n, 0:n], Tt[0:n, 0:n], eng="act")
                        acc = accr.next()
                        k.mm(acc[0:n, 0:128], TtT[0:n, 0:n], vv[0:n], start=True, stop=False)
                        wp = psr.next()
                        k.mm(wp[:, 0:n], ke[0:n], TtT[0:n, 0:n])
                        nwT = b16r.next()
                        k.actf(nwT[:, 0:n], wp[:, 0:n], AF.Copy, scale=-1.0)
                        o1p = accr.next()
                        if not samp:
                            k.mm(acc[0:n, 0:128], nwT[:, 0:n], Sb.v(), start=False, stop=True)
                            k.mm(o1p[0:n, 0:128], qnT[:, 0:n], Sb.v())
                        else:
                            for s in range(NSS):
                                wm = b16s.next()
                                k.tt(wm[:, 0:64], nwT[:, 0:64], segcol[:, s, :], ALU.mult, eng="pool")
                                k.mm(acc[0:n, 0:128], wm[:, 0:64], Sb_all[:, s, :], start=False, stop=(s == NSS - 1))
                                qm = b16s.next()
                                k.tt(qm[:, 0:64], qnT[:, 0:64], segcol[:, s, :], ALU.mult, eng="pool")
                                k.mm(o1p[0:n, 0:128], qm[:, 0:64], Sb_all[:, s, :], start=(s == 0), stop=(s == NSS - 1))
                        vnew = b16r.next()
                        k.ts(vnew[0:n], acc[0:n, 0:128], beta, None, op0=ALU.mult)
                        o1 = o1_r.next()
                        k.actf(o1[0:n], o1p[0:n, 0:128], AF.Copy, scale=sc_eg[0:n, t, h:h + 1])
                        o2p = psr.next()
                        k.mm(o2p[0:n, 0:128], STm[0:n, 0:n], vnew[0:n])
                        k.tt(o1[0:n], o1[0:n], o2p[0:n, 0:128], ALU.add)
                        if not samp:
                            Sp = psr.next()
                            k.mm(Sp[:, 0:128], kd[0:n], vnew[0:n])
                            k.stt(S.v(), S.v(), eR[:, n - 1:n], Sp[:, 0:128], ALU.mult, ALU.add)
                            k.cp(Sb.v(), S.v(), eng="act")
                            if t == 15:
                                k.dma(k.sp, p_dn[l, h], S.v())
                        else:
                            for s in range(NSS):
                                km = b16s.next()
                                k.ts(km[0:64], kd[0:64], rowmask[:, s:s + 1], None, op0=ALU.mult, eng="pool")
                                Sp = psr.next()
                                k.mm(Sp[:, 0:128], km[0:64], vnew[0:64])
                                Ss = Ss_r.next()
                                k.dma(k.sp, Ss.v(), V(state_dn.t[l, s, h], [state_dn]))
                                k.stt(Ss.v(), Ss.v(), eR[:, 4 * s + 3:4 * s + 4], Sp[:, 0:128], ALU.mult, ALU.add)
                                k.dma(k.sp, V(s_dn.t[l, s, h], [s_dn]), Ss.v())
                        zp = psr.next()
                        for kc in range(8):
                            k.mm(zp[0:n, 0:128], hT(kc, t, t + 1), Wz[:, kc, :], start=(kc == 0), stop=(kc == 7))
                        zs = zsd_r.next()
                        k.actf(zs[0:n], zp[0:n, 0:128], AF.Silu)
                        r = rstd_of(o1[0:n], n, 1.0 / 128)
                        k.stt(o1[0:n], o1[0:n], r, gB[0:n], ALU.mult, ALU.mult)
                        ob = b16r.next()
                        k.tt(ob[0:n], o1[0:n], zs[0:n], ALU.mult, eng="pool")
                        op_ = psr.next()
                        opb = op_.v().bitcast(BF16)
                        k.tr(opb[:, 0:n], ob[0:n], ident_b[0:n, 0:n])
                        k.cp(obT[:, h, ta:ta + n], opb[:, 0:n], eng="act")

            for hp in range(4):
                with k.phase():
                    if 'noil' in FLAGS:
                        head_prog(2 * hp)
                        head_prog(2 * hp + 1)
                    else:
                        k.interleave([lambda h=2 * hp: head_prog(h), lambda h=2 * hp + 1: head_prog(h)])

        first = True
        if 'nossm' not in FLAGS:
            ssd_branch()
            branch_proj("w_br_ssm", 8, 2, first)
            first = False
        if 'nodil' not in FLAGS:
            dil_branch()
            branch_proj("w_br_dil", 4, 1, first)
            first = False
        if 'nodn' not in FLAGS:
            dn_branch()
            branch_proj("w_br_dn", 8, 0, first)
            first = False
        if first:
            mbox[0] = k.sb("mergedT", [128, 8, TOK], BF16)
            k.memset(mbox[0].v(), 0.0)
        with k.phase():
            alloc_wblk(2)
            alloc_ysb(4)
            out_proj_post(l, "w_out", "norm_mix_post", lambda t, kc: mbox[0][:, kc, t * 128:t * 128 + tn(t)], 8, stream=(srcp, srcs))

    def cross_phase(l):
      with k.phase():
          alloc_wblk(3)
          alloc_ysb(3)
          Xm = [ysb_r.next() for i in range(2)]
          for i in range(2):
              k.dma(k.sp, Xm[i].v(), mem_p[i * 128:(i + 1) * 128, :])
          memT = k.sb("memT", [128, 8, 256], BF16)
          KmT = k.sb("KmT", [128, 4, 256], BF16)
          Vm = k.sb("Vm", [128, 2, 4, 129], BF16)
          k.memset(Vm.v(), 1.0)
          qT_r = Ring([k.sb("qT%d" % i, [128, 512], BF16) for i in range(2)])
          PT_r = Ring([k.sb("PT%d" % i, [128, 2, 512], BF16) for i in range(2)])
          om_r = Ring([k.sb("om%d" % i, [128, 512], BF16) for i in range(4)])
          stage_r = Ring([k.sb("stage%d" % i, [128, 2, 512]) for i in range(2)])
          KsT_r = Ring([k.sb("KsT%d" % i, [128, 8, 128], BF16) for i in range(2)])
          Vs_r = Ring([k.sb("Vs%d" % i, [128, 2, 4, 129], BF16) for i in range(2)])
          for b in Vs_r.bufs:
              k.memset(b.v(), 1.0)
          PTz_r = Ring([k.sb("PTz%d" % i, [128, 8, 64], BF16) for i in range(2)])
          for b in PTz_r.bufs:
              k.memset(b.v(), 0.0)
          qTs = k.sb("qTs", [128, 4, 64], BF16)

          def memkv(l):
              wm = W["w_mkv"].t
              for mt in range(2):
                  r = rstd_of(Xm[mt].v(), 128, 1.0 / D)
                  xn = xn_r.next()
                  k.actf(xn.v(), Xm[mt].v(), AF.Copy, scale=r)
                  ps = psr.next()
                  psb = ps.v().bitcast(BF16)
                  for kc in range(8):
                      k.tr(psb[:, kc * 128:(kc + 1) * 128], xn[:, kc * 128:(kc + 1) * 128], ident_b.v())
                  k.tt(memT[:, :, mt * 128:(mt + 1) * 128], psb.rr("p (c j) -> p c j", c=8),
                       gpre[:, 3, l, :].unsq(2).bc([128, 8, 128]), ALU.mult)
              for which in range(2):
                  wb = wblk.next()
                  load_w(wb.v(), V(wm[l, :, which * 512:(which + 1) * 512].rearrange("(c p) n -> p c n", p=128), [W["w_mkv"]]))
                  st = stage_r.next()
                  for mt in range(2):
                      ps = psr.next()
                      for kc in range(8):
                          k.mm(ps.v(), memT[:, kc, mt * 128:(mt + 1) * 128], wb[:, kc, :], start=(kc == 0), stop=(kc == 7))
                      k.cp(st[:, mt, :], ps.v(), eng="act")
                      if which == 1:
                          k.cp(Vm[:, mt, :, 0:128], ps.v().rr("p (h e) -> p h e", h=4))
                  dst = (p_mem_k if which == 0 else p_mem_v)
                  k.dma(k.sp, V(dst.t[l].rearrange("(m p) n -> p m n", p=128), [dst]), st.v())
                  if which == 0:
                      for hd in range(4):
                          ps = psr.next()
                          for kc in range(8):
                              k.mm(ps[:, 0:256], wb[:, kc, hd * 128:(hd + 1) * 128], memT[:, kc, :], start=(kc == 0), stop=(kc == 7))
                          k.cp(KmT[:, hd, :], ps[:, 0:256], eng="act")

          omT_all = k.sb("omT_all", [128, 4, TOK], BF16)

          def om_to_T(om, t, n):
              ps = psr.next()
              psb = ps.v().bitcast(BF16)
              for hd in range(4):
                  k.tr(psb[:, hd * 128:hd * 128 + n], om[0:n, hd * 128:(hd + 1) * 128], ident_b[0:n, 0:n])
              k.cp(omT_all[:, :, t * 128:t * 128 + n], psb.rr("p (c j) -> p c j", c=8)[:, 0:4, 0:n])

          def cross_attn(l):
              make_hT(1, l)
              wq = wblk.next()
              load_w(wq.v(), V(W["w_mq"].t[l].rearrange("(c p) n -> p c n", p=128), [W["w_mq"]]))
              sc = 128.0 ** -0.5
              for tb in range(5):
                  t0, t1 = tb * 4, min(tb * 4 + 4, NT)
                  ntok = min(512, TOK - tb * 512)
                  oms = [om_r.next() for _ in range(t0, t1)] if tb < 4 else []
                  for hd in range(4):
                      ps = psr.next()
                      for kc in range(8):
                          k.mm(ps[:, 0:ntok], wq[:, kc, hd * 128:(hd + 1) * 128], hT(kc, t0, t1), start=(kc == 0), stop=(kc == 7))
                      if tb == 4:
                          k.actf(qTs[:, hd, :], ps[:, 0:64], AF.Copy, scale=sc)
                          continue
                      qT = qT_r.next()
                      k.actf(qT.v(), ps.v(), AF.Copy, scale=sc)
                      PT = PT_r.next()
                      for mt in range(2):
                          ps2 = psr.next()
                          k.mm(ps2.v(), KmT[:, hd, mt * 128:(mt + 1) * 128], qT.v())
                          k.actf(PT[:, mt, :], ps2.v(), AF.Exp)
                      for ti in range(4):
                          ps3 = psr.next()
                          for mt in range(2):
                              k.mm(ps3[:, 0:129], PT[:, mt, ti * 128:(ti + 1) * 128], Vm[:, mt, hd, :], start=(mt == 0), stop=(mt == 1))
                          sm = small.next()
                          k.recip(sm[:, 0:1], ps3[:, 128:129])
                          k.ts(oms[ti][:, hd * 128:(hd + 1) * 128], ps3[:, 0:128], sm[:, 0:1], None, op0=ALU.mult)
                  if tb < 4:
                      for ti in range(4):
                          om_to_T(oms[ti], t0 + ti, 128)
              accs = PS[0:4]
              for s in range(NSS if 'nosamp' not in FLAGS else 0):
                  stK = stage_r.next()
                  k.dma(k.sp, stK.v(), V(cache_mem_k.t[l, s].rearrange("(m p) n -> p m n", p=128), [cache_mem_k]))
                  stV = stage_r.next()
                  k.dma(k.sp, stV.v(), V(cache_mem_v.t[l, s].rearrange("(m p) n -> p m n", p=128), [cache_mem_v]))
                  KsT = KsT_r.next()
                  for half in range(2):
                      ps = psr.next()
                      for j in range(4):
                          hd = half * 2 + j // 2
                          mt = j % 2
                          k.tr(ps[:, j * 128:(j + 1) * 128], stK[:, mt, hd * 128:(hd + 1) * 128], ident_f.v())
                      k.cp(KsT[:, half * 4:(half + 1) * 4, :], ps.v().rr("p (c j) -> p c j", c=4), eng="act")
                  Vs = Vs_r.next()
                  k.cp(Vs[:, :, :, 0:128], stV.v().rr("p m (h e) -> p m h e", h=4), eng="pool")
                  ps = psr.next()
                  for hd in range(4):
                      for mt in range(2):
                          c = hd * 2 + mt
                          k.mm(ps[:, c * 4:(c + 1) * 4], KsT[:, c, :], qTs[:, hd, s * 4:(s + 1) * 4])
                  PTz = PTz_r.next()
                  k.actf(PTz[:, :, s * 4:(s + 1) * 4], ps[:, 0:32].rr("p (c j) -> p c j", c=8), AF.Exp)
                  for hd in range(4):
                      for mt in range(2):
                          k.mm(accs[hd][0:64, 0:129], PTz[:, hd * 2 + mt, :], Vs[:, mt, hd, :],
                               start=(s == 0 and mt == 0), stop=(s == NSS - 1 and mt == 1))
                  k.memset(PTz[:, :, s * 4:(s + 1) * 4], 0.0)
              om = om_r.next()
              for hd in range(4):
                  sm = small.next()
                  k.recip(sm[0:64, 0:1], accs[hd][0:64, 128:129])
                  k.ts(om[0:64, hd * 128:(hd + 1) * 128], accs[hd][0:64, 0:128], sm[0:64, 0:1], None, op0=ALU.mult)
              om_to_T(om, 16, 64)
              out_proj_post(l, "w_mo", "norm_mem_post", lambda t, kc: omT_all[:, kc, t * 128:t * 128 + tn(t)], 4)
          memkv(l)
          if 'noca' not in FLAGS:
              cross_attn(l)

    def mlp(l):
      with k.phase():
          alloc_wblk(3)
          alloc_ysb(4)
          fT = k.sb("fT", [128, 32, 512], BF16)
          rtmp = Ring([k.sb("rtmp%d" % i, [128, 512]) for i in range(2)])
          make_hT(2, l)
          w1 = W["w_ff1"].t
          w2 = W["w_ff2"].t
          g = gpost.next()
          k.dma(k.sp, g.v(), V(W["norm_ffn_post"].t[l, :].partition_broadcast(128), [W["norm_ffn_post"]]))
          for tb in range(5):
              t0, t1 = tb * 4, min(tb * 4 + 4, NT)
              ntok = min(512, TOK - tb * 512)
              for cg in range(8):
                  wb = wblk.next()
                  load_w(wb.v(), V(w1[l, :, cg * 512:(cg + 1) * 512].rearrange("(c p) n -> p c n", p=128), [W["w_ff1"]]))
                  for j in range(4):
                      fc = cg * 4 + j
                      ps = psr.next()
                      for kc in range(8):
                          k.mm(ps[:, 0:ntok], wb[:, kc, j * 128:(j + 1) * 128], hT(kc, t0, t1), start=(kc == 0), stop=(kc == 7))
                      rt = rtmp.next()
                      k.actf(rt[:, 0:ntok], ps[:, 0:ntok], AF.Relu)
                      k.tt(fT[:, fc, 0:ntok], rt[:, 0:ntok], rt[:, 0:ntok], ALU.mult, eng="pool")
              ysbs = [ysb_r.next() for _ in range(t0, t1)]
              for half in range(2):
                  accs = PS[0:t1 - t0]
                  for fg in range(4):
                      wb = wblk.next()
                      load_w(wb.v(), V(w2[l, fg * 1024:(fg + 1) * 1024, half * 512:(half + 1) * 512].rearrange("(c p) n -> p c n", p=128), [W["w_ff2"]]))
                      for ti in range(t1 - t0):
                          n = tn(t0 + ti)
                          for j in range(8):
                              fc = fg * 8 + j
                              k.mm(accs[ti][0:n, :], fT[:, fc, ti * 128:ti * 128 + n], wb[:, j, :], start=(fc == 0), stop=(fc == 31))
                  for ti in range(t1 - t0):
                      n = tn(t0 + ti)
                      k.cp(ysbs[ti][0:n, half * 512:(half + 1) * 512], accs[ti][0:n, :], eng="act")
              for ti in range(t1 - t0):
                  t = t0 + ti
                  n = tn(t)
                  ysb = ysbs[ti]
                  r = rstd_of(ysb[0:n, :], n, 1.0 / D)
                  k.stt(ysb[0:n, :], ysb[0:n, :], r, g[0:n, :], ALU.mult, ALU.mult)
                  k.tt(X[t][0:n, :], X[t][0:n, :], ysb[0:n, :], ALU.add, eng="pool")


    for l in range(DEPTH if 'l1' not in FLAGS else 1):
        mixed = 'nomix' not in FLAGS
        if mixed:
            mixer(l)
        with k.phase():
            sp_, ss_ = (y_p, y_s) if (mixed or l > 0) else (x_p, x_s)
            for t in range(NT):
                X[t] = k.sb("X%d" % t, [128, D])
                k.dma(k.sp, X[t][0:tn(t), :], xrows(sp_, ss_, t))
            if 'nocross' not in FLAGS:
                cross_phase(l)
            if 'nomlp' not in FLAGS:
                mlp(l)
            for t in range(NT):
                k.dma(k.sp, xrows(y_p, y_s, t), X[t][0:tn(t), :])
    k.finish()
    es.close()
    return nc, k


_CACHE = {}


def core_inputs(inp, c, consts):
    f = lambda a: np.ascontiguousarray(np.asarray(a, dtype=np.float32))
    s0, s1 = c * NSS, (c + 1) * NSS
    m = {"x_p": f(inp["x_prompt"][c]), "x_s": f(inp["x_sample"][s0:s1]).reshape(NSS * TS, D),
         "mem_p": f(inp["mem_prompt"][c]),
         "cache_mem_k": f(inp["cache_mem_k"][:, s0:s1]).reshape(DEPTH, NSS, 256, 512),
         "cache_mem_v": f(inp["cache_mem_v"][:, s0:s1]).reshape(DEPTH, NSS, 256, 512)}
    for g in range(3):
        for kv in "kv":
            nm = "cache_win%d_%s" % (g + 1, kv)
            m[nm] = f(inp[nm][:, s0:s1]).reshape(DEPTH, NSS, DIL_W[g], 512)
    m["state_ssm_conv"] = f(inp["state_ssm_conv"][:, s0:s1])
    m["state_ssm"] = f(inp["state_ssm"][:, s0:s1])
    m["state_dn_conv"] = f(inp["state_dn_conv"][:, s0:s1])
    m["state_dn"] = f(inp["state_dn"][:, s0:s1])
    m.update(consts)
    return m


WNAMES = ["norm_mix_pre", "w_in", "w_br_dn", "w_br_dil", "w_br_ssm", "w_out", "norm_mix_post", "norm_mem_pre",
          "norm_mem_kv", "w_mq", "w_mkv", "w_mo", "norm_mem_post", "norm_ffn_pre", "w_ff1", "w_ff2", "norm_ffn_post",
          "ssm_conv_w", "ssm_conv_b", "ssm_a_log", "ssm_dt_bias", "ssm_d", "ssm_norm",
          "dn_conv_w", "dn_a_log", "dn_dt_bias", "dn_norm"]


def assemble(R, ncores):
    z = lambda *s: np.zeros(s, np.float32)

    def per_batch(name, shape):
        if name not in R[0]:
            return z(DEPTH, ncores, *shape)
        return np.stack([R[c][name].reshape((DEPTH,) + tuple(shape)) for c in range(ncores)], axis=1)

    def per_seq(name, shape):
        if name not in R[0]:
            return z(DEPTH, ncores * NSS, *shape)
        return np.concatenate([R[c][name].reshape((DEPTH, NSS) + tuple(shape)) for c in range(ncores)], axis=1)

    yp = np.stack([R[c]["y_p"] for c in range(ncores)])
    ys = np.concatenate([R[c]["y_s"].reshape(NSS, TS, D) for c in range(ncores)])
    outs = [yp, ys,
            per_batch("p_dn_conv", (3, 3072)), per_batch("p_dn", (8, 128, 128)),
            per_batch("p_ssm_conv", (3, 2048)), per_batch("p_ssm", (16, 64, 128)),
            per_batch("p_win1_k", (128, 4, 128)), per_batch("p_win1_v", (128, 4, 128)),
            per_batch("p_win2_k", (512, 4, 128)), per_batch("p_win2_v", (512, 4, 128)),
            per_batch("p_win3_k", (2048, 4, 128)), per_batch("p_win3_v", (2048, 4, 128)),
            per_batch("p_mem_k", (256, 4, 128)), per_batch("p_mem_v", (256, 4, 128)),
            per_seq("s_dn_conv", (3, 3072)), per_seq("s_dn", (8, 128, 128)),
            per_seq("s_ssm_conv", (3, 2048)), per_seq("s_ssm", (16, 64, 128)),
            per_seq("s_win1_k", (4, 4, 128)), per_seq("s_win1_v", (4, 4, 128)),
            per_seq("s_win2_k", (4, 4, 128)), per_seq("s_win2_v", (4, 4, 128)),
            per_seq("s_win3_k", (4, 4, 128)), per_seq("s_win3_v", (4, 4, 128))]
    return tuple(outs)


def kernel(**inp):
    if "nc" not in _CACHE:
        _CACHE["nc"] = build()
    nc, kb = _CACHE["nc"]
    f = lambda a: np.ascontiguousarray(np.asarray(a, dtype=np.float32))
    consts = host_consts()
    wts = {n: f(inp[n]) for n in WNAMES}
    in_maps = []
    for c in range(NCORES):
        m = core_inputs(inp, c, consts)
        m.update(wts)
        in_maps.append(m)
    res = run_bass_kernel_spmd(nc, in_maps, core_ids=list(range(NCORES)))
    return assemble(res.results, NCORES)
```

```python
import os
import threading
import numpy as np
from contextlib import ExitStack
import concourse.bass as bass
import concourse.mybir as mybir
from concourse.bass_utils import run_bass_kernel_spmd

F32 = mybir.dt.float32
BF16 = mybir.dt.bfloat16
AF = mybir.ActivationFunctionType
ALU = mybir.AluOpType
AX = mybir.AxisListType

D = 1024
NCORES = 8
SEQ = 2048
NSS = 16
TS = 4
NT = 17
TOK = SEQ + NSS * TS
DEPTH = 2
N_IN = 14880
EPS = 1e-6
C_DNQKV = 0
C_DNZ = 3072
C_DNB = 4096
C_DNA = 4104
C_DIL = 4112
C_SSZ = 8720
C_SSX = 9744
C_SSDT = 11792
C_GATE = 11808


def tn(t):
    return 128 if t < 16 else 64


class Sem:
    def __init__(self, h):
        self.h = h


class Buf:
    def __init__(self, t, name=""):
        self.t = t
        self.w = None
        self.r = {}
        self.name = name
        self.is_psum = False

    def __getitem__(self, idx):
        return V(self.t[idx], [self])

    def v(self):
        return V(self.t[:], [self])


class V:
    def __init__(self, ap, bufs):
        self.ap = ap
        self.bufs = bufs

    def __getitem__(self, idx):
        return V(self.ap[idx], self.bufs)

    def rr(self, pat, **kw):
        return V(self.ap.rearrange(pat, **kw), self.bufs)

    def bc(self, shape):
        return V(self.ap.broadcast_to(shape), self.bufs)

    def bitcast(self, dt):
        return V(self.ap.bitcast(dt), self.bufs)

    def unsq(self, d):
        return V(self.ap.unsqueeze(d), self.bufs)


class Slot:
    def __init__(self, sem):
        self.sem = sem
        self.cnt = 0


class Eng:
    def __init__(self, h, sem, is_pe=False):
        self.h = h
        self.sem = sem
        self.cnt = 0
        self.seen = {}
        self.is_pe = is_pe
        self.slots = []
        self.slot_i = 0


def _aps(x):
    return x.ap if isinstance(x, V) else x


class KB:
    def __init__(self, nc, es):
        self.nc = nc
        self.es = es
        mk = lambda n: Sem(es.enter_context(nc.semaphore(n)))
        self.pe = Eng(nc.tensor, mk("s_pe"), is_pe=True)
        self.act = Eng(nc.scalar, mk("s_act"))
        self.dve = Eng(nc.vector, mk("s_dve"))
        self.pool = Eng(nc.gpsimd, mk("s_pool"))
        self.sp = Eng(nc.sync, mk("s_sp"))
        for i in range(24):
            self.sp.slots.append(Slot(mk("d_sp%d" % i)))
        for i in range(16):
            self.pool.slots.append(Slot(mk("d_pl%d" % i)))
        for i in range(8):
            self.act.slots.append(Slot(mk("d_ac%d" % i)))
        self.n_inst = 0
        self.interleaving = False
        self._tls = threading.local()

    def barrier(self):
        engs = [self.pe, self.act, self.dve, self.pool, self.sp]
        for e in engs:
            for o in engs:
                if (o is not e or not e.is_pe) and o.cnt > 0 and e.seen.get(o.sem, 0) < o.cnt:
                    e.h.wait_ge(o.sem.h, o.cnt)
                    e.seen[o.sem] = o.cnt
            for q in (self.sp, self.pool, self.act):
                for s in q.slots:
                    if s.cnt > 0 and e.seen.get(s.sem, 0) < 16 * s.cnt:
                        e.h.wait_ge(s.sem.h, 16 * s.cnt)
                        e.seen[s.sem] = 16 * s.cnt

    def phase(self):
        kb = self

        class _P:
            def __enter__(s):
                s.old = kb.es
                s.st = ExitStack()
                kb.es = s.st
                kb.uid = getattr(kb, "uid", 0) + 1

            def __exit__(s, *a):
                kb.barrier()
                s.st.close()
                kb.es = s.old
        return _P()

    def sb(self, name, shape, dt=F32):
        self.alloc_i = getattr(self, "alloc_i", 0) + 1
        name = "%s_%d" % (name, self.alloc_i)
        return Buf(self.es.enter_context(self.nc.sbuf_tensor(name, list(shape), dt)), name)

    def psum(self, name, shape, dt=F32):
        b = Buf(self.es.enter_context(self.nc.psum_tensor(name, list(shape), dt)), name)
        b.is_psum = True
        return b

    def slot(self):
        return getattr(self._tls, "slot", 0) if self.interleaving else 0

    def interleave(self, fns):
        n = len(fns)
        sems = [threading.Semaphore(0) for _ in range(n)]
        done = threading.Semaphore(0)
        st = {"alive": [True] * n, "exc": None}
        self._il = (sems, st)

        def nxt_alive(i):
            for d in range(1, n + 1):
                j = (i + d) % n
                if st["alive"][j]:
                    return j
            return None

        self._nxt_alive = nxt_alive

        def runner(i):
            sems[i].acquire()
            self._tls.slot = i
            try:
                fns[i]()
            except BaseException as e:
                st["exc"] = e
                st["alive"] = [False] * n
                done.release()
                return
            st["alive"][i] = False
            j = nxt_alive(i)
            if j is None:
                done.release()
            else:
                sems[j].release()
        ths = [threading.Thread(target=runner, args=(i,), daemon=True) for i in range(n)]
        for t in ths:
            t.start()
        self.interleaving = True
        sems[0].release()
        done.acquire()
        self.interleaving = False
        if st["exc"] is not None:
            raise st["exc"]

    def _yield(self):
        if not self.interleaving:
            return
        sems, st = self._il
        i = self._tls.slot
        j = self._nxt_alive(i)
        if j is None or j == i:
            return
        sems[j].release()
        sems[i].acquire()

    def _waits(self, eng, R, W):
        deps = {}

        def add(sm, v):
            if deps.get(sm, 0) < v:
                deps[sm] = v

        for b in R:
            if b.w is not None:
                add(*b.w)
            if b.is_psum:
                for sm, v in b.r.items():
                    if sm is not eng.sem:
                        add(sm, v)
        for b in W:
            if b.w is not None:
                add(*b.w)
            for sm, v in b.r.items():
                add(sm, v)
        for sm, v in deps.items():
            if sm is eng.sem and eng.is_pe:
                continue
            if eng.seen.get(sm, 0) >= v:
                continue
            eng.h.wait_ge(sm.h, v)
            eng.seen[sm] = v

    def emit(self, eng, fn, R, W):
        self._waits(eng, R, W)
        ins = fn()
        ins.then_inc(eng.sem.h, 1)
        eng.cnt += 1
        self.n_inst += 1
        for b in R:
            b.r[eng.sem] = eng.cnt
        for b in W:
            b.w = (eng.sem, eng.cnt)
            b.r = {}
        self._yield()

    def dma(self, q, out, in_):
        slot = q.slots[q.slot_i % len(q.slots)]
        q.slot_i += 1
        if slot.cnt > 0 and q.seen.get(slot.sem, 0) < 16 * slot.cnt:
            q.h.wait_ge(slot.sem.h, 16 * slot.cnt)
            q.seen[slot.sem] = 16 * slot.cnt
        R, W = in_.bufs, out.bufs
        self._waits(q, R, W)
        ins = q.h.dma_start(out=out.ap, in_=in_.ap)
        ins.then_inc(slot.sem.h, 16)
        slot.cnt += 1
        self.n_inst += 1
        v = 16 * slot.cnt
        for b in R:
            b.r[slot.sem] = v
        for b in W:
            b.w = (slot.sem, v)
            b.r = {}
        self._yield()

    def finish(self):
        q = self.sp
        for e in (self.sp, self.pool, self.act):
            for s in e.slots:
                if s.cnt > 0:
                    q.h.wait_ge(s.sem.h, 16 * s.cnt)
        for e in (self.pe, self.act, self.dve, self.pool):
            if e.cnt > 0:
                q.h.wait_ge(e.sem.h, e.cnt)

    def mm(self, out, lhsT, rhs, start=True, stop=True):
        self.emit(self.pe, lambda: self.nc.tensor.matmul(out.ap, lhsT=lhsT.ap, rhs=rhs.ap, start=start, stop=stop),
                  lhsT.bufs + rhs.bufs, out.bufs)

    def tr(self, out, in_, ident):
        self.emit(self.pe, lambda: self.nc.tensor.transpose(out.ap, in_.ap, ident.ap),
                  in_.bufs + ident.bufs, out.bufs)

    def actf(self, out, in_, func, scale=None, bias=None, accum=None):
        kw = {}
        R = list(in_.bufs)
        W = list(out.bufs)
        if scale is not None:
            kw["scale"] = _aps(scale)
            if isinstance(scale, V):
                R += scale.bufs
        if bias is not None:
            kw["bias"] = _aps(bias)
            if isinstance(bias, V):
                R += bias.bufs
        if accum is not None:
            kw["accum_out"] = accum.ap
            W += accum.bufs
        self.emit(self.act, lambda: self.nc.scalar.activation(out=out.ap, in_=in_.ap, func=func, **kw), R, W)

    def _e(self, eng):
        return {"dve": self.dve, "pool": self.pool}[eng]

    def tt(self, out, in0, in1, op, eng="dve"):
        e = self._e(eng)
        self.emit(e, lambda: e.h.tensor_tensor(out=out.ap, in0=in0.ap, in1=in1.ap, op=op),
                  in0.bufs + in1.bufs, out.bufs)

    def ts(self, out, in0, s1, s2=None, op0=ALU.mult, op1=None, eng="dve", accum=None):
        e = self._e(eng)
        R = list(in0.bufs)
        W = list(out.bufs)
        for s in (s1, s2):
            if isinstance(s, V):
                R += s.bufs
        kw = {}
        if op1 is not None:
            kw["op1"] = op1
        if accum is not None:
            kw["accum_out"] = accum.ap
            W += accum.bufs
        self.emit(e, lambda: e.h.tensor_scalar(out=out.ap, in0=in0.ap, scalar1=_aps(s1), scalar2=_aps(s2), op0=op0, **kw), R, W)

    def stt(self, out, in0, scalar, in1, op0, op1):
        e = self.dve
        R = in0.bufs + in1.bufs + (scalar.bufs if isinstance(scalar, V) else [])
        self.emit(e, lambda: e.h.scalar_tensor_tensor(out=out.ap, in0=in0.ap, scalar=_aps(scalar), in1=in1.ap, op0=op0, op1=op1),
                  R, out.bufs)

    def cp(self, out, in_, eng="dve"):
        if eng == "act":
            return self.actf(out, in_, AF.Copy)
        e = self._e(eng)
        self.emit(e, lambda: e.h.tensor_copy(out=out.ap, in_=in_.ap), in_.bufs, out.bufs)

    def recip(self, out, in_):
        self.emit(self.dve, lambda: self.nc.vector.reciprocal(out=out.ap, in_=in_.ap), in_.bufs, out.bufs)

    def memset(self, out, c, eng="pool"):
        e = self._e(eng)
        self.emit(e, lambda: e.h.memset(out.ap, c), [], out.bufs)


class Ring:
    def __init__(self, bufs):
        self.bufs = bufs
        self.i = 0

    def next(self):
        b = self.bufs[self.i % len(self.bufs)]
        self.i += 1
        return b


IN_NAMES = []
FLAGS = set(os.environ.get('KFLAGS', '').split(','))

DIL_W = (128, 512, 2048)
DIL_D = (1, 4, 16)
SC = 128.0 ** -0.5


def host_consts():
    c = {}
    c["c_ident"] = np.eye(128, dtype=np.float32)
    j = np.arange(128)[:, None]
    i = np.arange(128)[None, :]
    m = np.zeros((9, 128, 128), np.float32)
    for g, d in enumerate(DIL_D):
        res = ((i - j) % d) == 0
        m[g * 3 + 0] = res & (i >= j)
        m[g * 3 + 1] = res
        m[g * 3 + 2] = res & (i <= j)
    c["c_dilmask"] = np.ascontiguousarray(m.transpose(1, 0, 2))
    half = 16
    inv = np.power(np.float32(500000.0), -np.arange(half, dtype=np.float32) * np.float32(2.0) / np.float32(32))
    pos = np.concatenate([np.arange(SEQ, dtype=np.float32), np.tile(2048 + np.arange(TS, dtype=np.float32), NSS)])
    ang = (pos[:, None] * inv[None, :]).astype(np.float32)
    cos = np.cos(ang).astype(np.float32).T
    sin = np.sin(ang).astype(np.float32).T
    c["c_rope"] = np.ascontiguousarray(np.stack([np.concatenate([cos, cos]), np.concatenate([-sin, sin])]))
    p = np.arange(128)[:, None]
    t = np.arange(4)[None, :]
    sm = np.zeros((128, 9, 4), np.float32)
    sm[:, 0, :] = (p >= t)
    for b in range(4):
        sm[:, 1 + b, :] = ((p % 4) == t)
        sm[:, 5 + b, :] = (b == t)
    c["c_smask"] = sm
    ks, kt = np.arange(64)[:, None] // 4, np.arange(64)[:, None] % 4
    qs, qt = np.arange(64)[None, :] // 4, np.arange(64)[None, :] % 4
    nm = np.zeros((64, 3, 64), np.float32)
    nm[:, 0, :] = (ks == qs) & (kt <= qt)
    nm[:, 1, :] = (ks == qs) & (kt == qt)
    nm[:, 2, :] = (ks == qs) & (kt == qt)
    c["c_nmask"] = nm
    kk = np.arange(128)[:, None]
    ii = np.arange(128)[None, :]
    tri = np.zeros((128, 2, 128), np.float32)
    tri[:, 0, :] = (kk <= ii)
    same = ((kk // 4) == (ii // 4)) & (kk < 64) & (ii < 64)
    tri[:, 1, :] = same & (kk <= ii)
    c["c_tri"] = tri
    sg = np.ones((128, 2, 128), np.float32)
    sg[:, 1, :] = same
    c["c_segm"] = sg
    stri = np.zeros((128, 2, 128), np.float32)
    stri[:, 0, :] = (kk < ii)
    stri[:, 1, :] = same & (kk < ii)
    c["c_stri"] = stri
    bm = np.zeros((128, 4, 128), np.float32)
    bm[:, 0, :] = (kk // 16) == (ii // 16)
    for mi, b in enumerate((16, 32, 64)):
        bm[:, mi + 1, :] = ((kk // (2 * b)) == (ii // (2 * b))) & ((kk // b) != (ii // b))
    c["c_blk"] = bm
    sc = np.zeros((128, NSS, 64), np.float32)
    for s_ in range(NSS):
        sc[:, s_, 4 * s_:4 * s_ + 4] = 1.0
    c["c_segcol"] = sc
    rm = np.zeros((64, NSS), np.float32)
    for s_ in range(NSS):
        rm[4 * s_:4 * s_ + 4, s_] = 1.0
    c["c_rowmask"] = rm
    return c


def build(debug=False):
    nc = bass.Bass("TRN2", target_bir_lowering=False)
    es = ExitStack()
    k = KB(nc, es)

    def din(name, shape, dt=F32):
        if name not in IN_NAMES:
            IN_NAMES.append(name)
        return Buf(nc.dram_tensor(name, list(shape), dt, kind="ExternalInput"), name)

    def dout(name, shape, dt=F32):
        return Buf(nc.dram_tensor(name, list(shape), dt, kind="ExternalOutput"), name)

    x_p = din("x_p", [SEQ, D])
    x_s = din("x_s", [NSS * TS, D])
    mem_p = din("mem_p", [256, D])
    W = {}
    for name, shape in [("norm_mix_pre", [DEPTH, D]), ("w_in", [DEPTH, D, N_IN]), ("w_br_dn", [DEPTH, 1024, D]),
                        ("w_br_dil", [DEPTH, 512, D]), ("w_br_ssm", [DEPTH, 1024, D]), ("w_out", [DEPTH, D, D]),
                        ("norm_mix_post", [DEPTH, D]), ("norm_mem_pre", [DEPTH, D]), ("norm_mem_kv", [DEPTH, D]),
                        ("w_mq", [DEPTH, D, 512]), ("w_mkv", [DEPTH, D, 1024]), ("w_mo", [DEPTH, 512, D]),
                        ("norm_mem_post", [DEPTH, D]), ("norm_ffn_pre", [DEPTH, D]), ("w_ff1", [DEPTH, D, 4096]),
                        ("w_ff2", [DEPTH, 4096, D]), ("norm_ffn_post", [DEPTH, D])]:
        W[name] = din(name, shape)
    cache_mem_k = din("cache_mem_k", [DEPTH, NSS, 256, 512])
    cache_mem_v = din("cache_mem_v", [DEPTH, NSS, 256, 512])
    cwin_k = [din("cache_win%d_k" % (g + 1), [DEPTH, NSS, DIL_W[g], 512]) for g in range(3)]
    cwin_v = [din("cache_win%d_v" % (g + 1), [DEPTH, NSS, DIL_W[g], 512]) for g in range(3)]
    c_ident = din("c_ident", [128, 128])
    c_dilmask = din("c_dilmask", [128, 9, 128])
    c_rope = din("c_rope", [2, 32, TOK])
    c_smask = din("c_smask", [128, 9, 4])
    c_nmask = din("c_nmask", [64, 3, 64])
    c_tri = din("c_tri", [128, 2, 128])
    c_segm = din("c_segm", [128, 2, 128])
    c_stri = din("c_stri", [128, 2, 128])
    c_blk = din("c_blk", [128, 4, 128])
    state_dn_conv = din("state_dn_conv", [DEPTH, NSS, 3, 3072])
    state_dn = din("state_dn", [DEPTH, NSS, 8, 128, 128])
    dn_conv_w = din("dn_conv_w", [DEPTH, 4, 3072])
    dn_a_log = din("dn_a_log", [DEPTH, 8])
    dn_dt_bias = din("dn_dt_bias", [DEPTH, 8])
    dn_norm = din("dn_norm", [DEPTH, 128])
    c_segcol = din("c_segcol", [128, NSS, 64])
    c_rowmask = din("c_rowmask", [64, NSS])
    state_ssm_conv = din("state_ssm_conv", [DEPTH, NSS, 3, 2048])
    state_ssm = din("state_ssm", [DEPTH, NSS, 16, 64, 128])
    ssm_conv_w = din("ssm_conv_w", [DEPTH, 4, 2048])
    ssm_conv_b = din("ssm_conv_b", [DEPTH, 2048])
    ssm_a_log = din("ssm_a_log", [DEPTH, 16])
    ssm_dt_bias = din("ssm_dt_bias", [DEPTH, 16])
    ssm_d = din("ssm_d", [DEPTH, 16])
    ssm_norm = din("ssm_norm", [DEPTH, 1024])

    y_p = dout("y_p", [SEQ, D])
    y_s = dout("y_s", [NSS * TS, D])
    p_mem_k = dout("p_mem_k", [DEPTH, 256, 512])
    p_mem_v = dout("p_mem_v", [DEPTH, 256, 512])
    pwin_k = [dout("p_win%d_k" % (g + 1), [DEPTH, DIL_W[g], 512]) for g in range(3)]
    pwin_v = [dout("p_win%d_v" % (g + 1), [DEPTH, DIL_W[g], 512]) for g in range(3)]
    swin_k = [dout("s_win%d_k" % (g + 1), [DEPTH, NSS * TS, 512]) for g in range(3)]
    swin_v = [dout("s_win%d_v" % (g + 1), [DEPTH, NSS * TS, 512]) for g in range(3)]
    p_ssm_conv = dout("p_ssm_conv", [DEPTH, 3, 2048])
    p_dn_conv = dout("p_dn_conv", [DEPTH, 3, 3072])
    p_dn = dout("p_dn", [DEPTH, 8, 128, 128])
    s_dn_conv = dout("s_dn_conv", [DEPTH, NSS, 3, 3072])
    s_dn = dout("s_dn", [DEPTH, NSS, 8, 128, 128])
    p_ssm = dout("p_ssm", [DEPTH, 16, 64, 128])
    s_ssm_conv = dout("s_ssm_conv", [DEPTH, NSS, 3, 2048])
    s_ssm = dout("s_ssm", [DEPTH, NSS, 16, 64, 128])

    X = [None] * NT
    hT_t = es.enter_context(nc.sbuf_tensor("hT", [128, 8, TOK], BF16))
    hTb = [Buf(hT_t, "hT%d" % t) for t in range(NT)]

    def hT(kc, t0, t1):
        a = t0 * 128
        b = min(t1 * 128, TOK)
        return V(hT_t[:, kc, a:b], hTb[t0:t1])

    ident_f = k.sb("ident_f", [128, 128])
    ident_b = k.sb("ident_b", [128, 128], BF16)
    gpre = k.sb("gpre", [128, 4, DEPTH, 8])
    gpost = Ring([k.sb("gpost%d" % i, [128, D]) for i in range(1)])
    small = Ring([k.sb("small%d" % i, [128, 8]) for i in range(12)])
    xn_r = Ring([k.sb("xn%d" % i, [128, D], BF16) for i in range(2)])
    junk_r = Ring([k.sb("junk%d" % i, [128, D], BF16) for i in range(1)])
    ysb_box = [None]

    class _YB:
        def next(self):
            return ysb_box[0].next()
    ysb_r = _YB()

    def alloc_ysb(n):
        ysb_box[0] = Ring([k.sb("ysb%d" % i, [128, D]) for i in range(n)])
    junkf = k.sb("junkf", [128, 576])
    PS = [k.psum("ps%d" % i, [128, 512]) for i in range(8)]
    class SlotRing:
        def __init__(self, full, parts):
            self.full = Ring(full)
            self.parts = [Ring(p) for p in parts]

        def next(self):
            if k.interleaving:
                return self.parts[k.slot()].next()
            return self.full.next()
    psr = SlotRing(PS[4:8], [PS[4:6], PS[6:8]])
    accr = SlotRing(PS[0:2], [PS[0:2], PS[2:4]])
    wblk_box = [None]

    class _WB:
        def next(self):
            return wblk_box[0].next()
    wblk = _WB()

    def alloc_wblk(n=3):
        wblk_box[0] = Ring([k.sb("wblk%d" % i, [128, 8, 512], BF16) for i in range(n)])

    k.dma(k.sp, ident_f.v(), c_ident.v())
    k.cp(ident_b.v(), ident_f.v())
    for i, nm in enumerate(["norm_mix_pre", "norm_mem_pre", "norm_ffn_pre", "norm_mem_kv"]):
        with nc.allow_non_contiguous_dma(reason="tiny gain vectors"):
            k.dma(k.sp, gpre[:, i, :, :], W[nm].v().rr("l (c p) -> p l c", p=128))

    def xrows(bp, bs, t):
        if t < 16:
            return bp[t * 128:(t + 1) * 128, :]
        return bs.v()

    def rstd_of(src, n, scale_inv):
        sm = small.next()
        jk = junk_r.next()
        F = src.ap.shape[-1]
        k.actf(jk[0:n, 0:F], src, AF.Square, accum=sm[0:n, 0:1])
        k.ts(sm[0:n, 1:2], sm[0:n, 0:1], scale_inv, EPS, op0=ALU.mult, op1=ALU.add)
        k.actf(sm[0:n, 2:3], sm[0:n, 1:2], AF.Sqrt)
        k.recip(sm[0:n, 3:4], sm[0:n, 2:3])
        return sm[0:n, 3:4]

    def make_hT(gi, l, xget=None):
        for t in range(NT):
            n = tn(t)
            xt = X[t] if xget is None else xget(t)
            r = rstd_of(xt[0:n, :], n, 1.0 / D)
            xn = xn_r.next()
            k.actf(xn[0:n, :], xt[0:n, :], AF.Copy, scale=r)
            ps = psr.next()
            psb = ps.v().bitcast(BF16)
            for kc in range(8):
                k.tr(psb[:, kc * 128:kc * 128 + n], xn[0:n, kc * 128:(kc + 1) * 128], ident_b[0:n, 0:n])
            a = t * 128
            k.tt(V(hT_t[:, :, a:a + n], [hTb[t]]),
                 psb.rr("p (c j) -> p c j", c=8)[:, :, 0:n],
                 gpre[:, gi, l, :].unsq(2).bc([128, 8, n]), ALU.mult)

    def load_w(dst, src):
        k.dma(k.pool, dst, src)

    def wsrc(name, l, r0, r1, c0, c1):
        return V(W[name].t[l, r0:r1, c0:c1].rearrange("(c p) n -> p c n", p=128), [W[name]])

    def out_proj_post(l, wname, gname, lhs_fn, nkc, stream=None):
        g = gpost.next()
        k.dma(k.sp, g.v(), V(W[gname].t[l, :].partition_broadcast(128), [W[gname]]))
        wbs = []
        for half in range(2):
            wb = wblk.next()
            load_w(wb[:, 0:nkc, :], wsrc(wname, l, 0, nkc * 128, half * 512, (half + 1) * 512))
            wbs.append(wb)
        for t in range(NT):
            n = tn(t)
            ysb = ysb_r.next()
            for half in range(2):
                ps = psr.next()
                for kc in range(nkc):
                    k.mm(ps[0:n, :], lhs_fn(t, kc), wbs[half][:, kc, :], start=(kc == 0), stop=(kc == nkc - 1))
                k.cp(ysb[0:n, half * 512:(half + 1) * 512], ps[0:n, :], eng="act")
            r = rstd_of(ysb[0:n, :], n, 1.0 / D)
            k.stt(ysb[0:n, :], ysb[0:n, :], r, g[0:n, :], ALU.mult, ALU.mult)
            if stream is None:
                k.tt(X[t][0:n, :], X[t][0:n, :], ysb[0:n, :], ALU.add, eng="pool")
            else:
                xt = ysb_r.next()
                k.dma(k.sp, xt[0:n, :], xrows(stream[0], stream[1], t))
                k.tt(xt[0:n, :], xt[0:n, :], ysb[0:n, :], ALU.add, eng="pool")
                k.dma(k.sp, xrows(y_p, y_s, t), xt[0:n, :])

    def mixer(l):
      srcp, srcs = (x_p, x_s) if l == 0 else (y_p, y_s)
      with k.phase():
        obT = k.sb("obT", [128, 8, TOK], BF16)
        mbox = [None]

        def xget(t):
            xt = ysb_r.next()
            k.dma(k.sp, xt[0:tn(t), :], xrows(srcp, srcs, t))
            return xt
        with k.phase():
            alloc_ysb(2)
            make_hT(0, l, xget)

        def branch_proj(wname, nkc, gidx, first):
          if mbox[0] is None:
              mbox[0] = k.sb("mergedT", [128, 8, TOK], BF16)
          mergedT = mbox[0]
          with k.phase():
            alloc_wblk(3)
            sig_r = Ring([k.sb("sig%d" % i, [128, 512]) for i in range(2)])
            for cg in range(2):
                    wg = wblk.next()
                    c0 = C_GATE + gidx * 1024 + cg * 512
                    load_w(wg.v(), wsrc("w_in", l, 0, 1024, c0, c0 + 512))
                    wb = wblk.next()
                    load_w(wb[:, 0:nkc, :], wsrc(wname, l, 0, nkc * 128, cg * 512, (cg + 1) * 512))
                    for j in range(4):
                        dc = cg * 4 + j
                        for tb in range(5):
                            t0, t1 = tb * 4, min(tb * 4 + 4, NT)
                            ntok = min(512, TOK - tb * 512)
                            a = tb * 512
                            psg = psr.next()
                            for kc in range(8):
                                k.mm(psg[:, 0:ntok], wg[:, kc, j * 128:(j + 1) * 128], hT(kc, t0, t1), start=(kc == 0), stop=(kc == 7))
                            sg = sig_r.next()
                            k.actf(sg[:, 0:ntok], psg[:, 0:ntok], AF.Sigmoid)
                            psp = psr.next()
                            for kc in range(nkc):
                                k.mm(psp[:, 0:ntok], wb[:, kc, j * 128:(j + 1) * 128], obT[:, kc, a:a + ntok], start=(kc == 0), stop=(kc == nkc - 1))
                            if first:
                                k.tt(mergedT[:, dc, a:a + ntok], psp[:, 0:ntok], sg[:, 0:ntok], ALU.mult)
                            else:
                                k.tt(sg[:, 0:ntok], psp[:, 0:ntok], sg[:, 0:ntok], ALU.mult)
                                k.tt(mergedT[:, dc, a:a + ntok], mergedT[:, dc, a:a + ntok], sg[:, 0:ntok], ALU.add, eng="pool")

        def dil_branch():
          with k.phase():
            KT = k.sb("KT", [128, 3, TOK], BF16)
            VA = k.sb("VA", [128, NT, 3, 129], BF16)
            k.memset(VA.v(), 1.0)
            QT_r = Ring([k.sb("QT%d" % i, [128, 3, 512], BF16) for i in range(2)])
            Wd = k.sb("Wd", [128, 8, 1344], BF16)
            rope_r = Ring([k.sb("rope%d" % i, [32, 2, 512]) for i in range(2)])
            dmask = k.sb("dmask", [128, 9, 128], BF16)
            k.dma(k.pool, dmask.v(), c_dilmask.v())
            smask = k.sb("smask", [128, 9, 4], BF16)
            k.dma(k.pool, smask.v(), c_smask.v())
            nmask = k.sb("nmask", [64, 3, 64], BF16)
            k.dma(k.pool, nmask.v(), c_nmask.v())
            ktmp_r = Ring([k.sb("ktmp%d" % i, [128, 512]) for i in range(2)])
            rt_r = Ring([k.sb("rt%d" % i, [32, 2, 512]) for i in range(1)])
            E_r = Ring([k.sb("E%d" % i, [128, 128], BF16) for i in range(2)])
            E4_r = Ring([k.sb("E4_%d" % i, [128, 4, 128], BF16) for i in range(3)])
            o_r = Ring([k.sb("o%d" % i, [128, 128], BF16) for i in range(2)])
            stg_r = Ring([k.sb("stg%d" % i, [128, 384]) for i in range(2)])
            Kc_r = Ring([k.sb("Kc%d" % i, [128, 9, 128]) for i in range(1)])
            Vc_r = Ring([k.sb("Vc%d" % i, [128, 9, 128]) for i in range(1)])
            Vcb_r = Ring([k.sb("Vcb%d" % i, [128, 9, 129], BF16) for i in range(1)])
            for b in Vcb_r.bufs:
                k.memset(b.v(), 1.0)
            KcT_r = Ring([k.sb("KcT%d" % i, [128, 9, 128], BF16) for i in range(1)])
            PTz_r = Ring([k.sb("PTz%d" % i, [128, 9, 64], BF16) for i in range(2)])
            for b in PTz_r.bufs:
                k.memset(b.v(), 0.0)
            win = W["w_in"]
            flip = [0]

            def rope_rows(dst32, ps, ps2, rope, ntok):
                r = rt_r.next()
                k.tt(r[:, 0, 0:ntok], ps[0:32, 0:ntok], rope[:, 0, 0:ntok], ALU.mult)
                k.tt(r[:, 1, 0:ntok], ps2[0:32, 0:ntok], rope[:, 1, 0:ntok], ALU.mult)
                k.tt(dst32, r[:, 0, 0:ntok], r[:, 1, 0:ntok], ALU.add, eng="pool")

            for hh in range(4):
                for g in range(3):
                    head = g * 4 + hh
                    for sec in range(3):
                        c0 = C_DIL + sec * 1536 + head * 128
                        load_w(Wd[:, :, (sec * 3 + g) * 128:(sec * 3 + g + 1) * 128], V(win.t[l, :, c0:c0 + 128].rearrange("(c p) n -> p c n", p=128), [win]))
                        if sec < 2:
                            o = 1152 + (sec * 3 + g) * 32
                            load_w(Wd[:, :, o:o + 16], V(win.t[l, :, c0 + 16:c0 + 32].rearrange("(c p) n -> p c n", p=128), [win]))
                            load_w(Wd[:, :, o + 16:o + 32], V(win.t[l, :, c0:c0 + 16].rearrange("(c p) n -> p c n", p=128), [win]))
                for tb in range(5):
                    t0, t1 = tb * 4, min(tb * 4 + 4, NT)
                    ntok = min(512, TOK - tb * 512)
                    a = tb * 512
                    QT = QT_r.next()
                    rope = rope_r.next()
                    k.dma(k.sp, rope[:, :, 0:ntok], V(c_rope.t[:, :, a:a + ntok].rearrange("c p n -> p c n"), [c_rope]))
                    for g in range(3):
                        for sec in range(2):
                            ps = psr.next()
                            c = (sec * 3 + g) * 128
                            for kc in range(8):
                                k.mm(ps[:, 0:ntok], Wd[:, kc, c:c + 128], hT(kc, t0, t1), start=(kc == 0), stop=(kc == 7))
                            ps2 = psr.next()
                            o = 1152 + (sec * 3 + g) * 32
                            for kc in range(8):
                                k.mm(ps2[0:32, 0:ntok], Wd[:, kc, o:o + 32], hT(kc, t0, t1), start=(kc == 0), stop=(kc == 7))
                            if sec == 0:
                                k.cp(QT[32:64, g, 0:ntok], ps[32:64, 0:ntok], eng="act")
                                k.cp(QT[64:128, g, 0:ntok], ps[64:128, 0:ntok], eng="act")
                                rope_rows(QT[0:32, g, 0:ntok], ps, ps2, rope, ntok)
                            else:
                                kt = ktmp_r.next()
                                k.cp(kt[32:64, 0:ntok], ps[32:64, 0:ntok], eng="act")
                                k.cp(kt[64:128, 0:ntok], ps[64:128, 0:ntok], eng="act")
                                rope_rows(kt[0:32, 0:ntok], ps, ps2, rope, ntok)
                                k.cp(KT[:, g, a:a + ntok], kt[:, 0:ntok], eng="pool")
                                for t in range(t0, t1):
                                    n = tn(t)
                                    if t == 16:
                                        dst = swin_k[g][l, :, hh * 128:(hh + 1) * 128]
                                    else:
                                        first_t = 16 - DIL_W[g] // 128
                                        if t < first_t:
                                            continue
                                        r0 = (t - first_t) * 128
                                        dst = pwin_k[g][l, r0:r0 + 128, hh * 128:(hh + 1) * 128]
                                    pst = psr.next()
                                    k.tr(pst[0:n, 0:128], kt[:, (t - t0) * 128:(t - t0) * 128 + n], ident_f.v())
                                    st = stg_r.next()
                                    k.cp(st[0:n, 0:128], pst[0:n, 0:128])
                                    k.dma(k.sp, dst, st[0:n, 0:128])
                    for t in range(t0, t1):
                        n = tn(t)
                        ps = psr.next()
                        for kc in range(8):
                            k.mm(ps[0:n, 0:384], hT(kc, t, t + 1), Wd[:, kc, 768:1152], start=(kc == 0), stop=(kc == 7))
                        k.cp(VA[0:n, t, :, 0:128], ps[0:n, 0:384].rr("p (g e) -> p g e", g=3), eng="act")
                        st = stg_r.next()
                        k.cp(st[0:n, :], ps[0:n, 0:384])
                        for g in range(3):
                            if t == 16:
                                dst = swin_v[g][l, :, hh * 128:(hh + 1) * 128]
                            else:
                                first_t = 16 - DIL_W[g] // 128
                                if t < first_t:
                                    continue
                                r0 = (t - first_t) * 128
                                dst = pwin_v[g][l, r0:r0 + 128, hh * 128:(hh + 1) * 128]
                            k.dma(k.sp, dst, st[0:n, g * 128:(g + 1) * 128])
                    if tb < 4:
                        for qt in range(t0, t1):
                            acc = accr.next()
                            blocks = []
                            for g in range(3):
                                nd = DIL_W[g] // 128
                                for delta in range(0, min(nd, qt) + 1):
                                    kind = 0 if delta == 0 else (2 if delta == nd else 1)
                                    blocks.append((g, qt - delta, kind))
                            chunks = [blocks[i:i + 4] for i in range(0, len(blocks), 4)]
                            nblk = len(blocks)

                            def stage1(ch):
                                ps2 = psr.next()
                                for c, (g, kb, kind) in enumerate(ch):
                                    k.mm(ps2[:, c * 128:(c + 1) * 128], KT[:, g, kb * 128:(kb + 1) * 128], QT[:, g, (qt - t0) * 128:(qt - t0 + 1) * 128])
                                E4 = E4_r.next()
                                ncb = len(ch)
                                k.actf(E4[:, 0:ncb, :], ps2[:, 0:ncb * 128].rr("p (c j) -> p c j", c=ncb), AF.Exp, scale=SC)
                                c = 0
                                while c < ncb:
                                    mi = ch[c][0] * 3 + ch[c][2]
                                    e = c
                                    while e + 1 < ncb and ch[e + 1][0] * 3 + ch[e + 1][2] == mi:
                                        e += 1
                                    flip[0] ^= 1
                                    k.tt(E4[:, c:e + 1, :], E4[:, c:e + 1, :], dmask[:, mi, :].unsq(1).bc([128, e + 1 - c, 128]), ALU.mult,
                                         eng=("dve" if flip[0] else "pool"))
                                    c = e + 1
                                return E4

                            def stage2(ch, E4, base):
                                for c, (g, kb, kind) in enumerate(ch):
                                    k.mm(acc[:, 0:129], E4[:, c, :], VA[:, kb, g, :], start=(base + c == 0), stop=(base + c == nblk - 1))

                            Es = [None] * len(chunks)
                            Es[0] = stage1(chunks[0])
                            for ci in range(len(chunks)):
                                if ci + 1 < len(chunks):
                                    Es[ci + 1] = stage1(chunks[ci + 1])
                                stage2(chunks[ci], Es[ci], ci * 4)
                            sm = small.next()
                            k.recip(sm[:, 0:1], acc[:, 128:129])
                            o = o_r.next()
                            k.ts(o.v(), acc[:, 0:128], sm[:, 0:1], None, op0=ALU.mult)
                            pst = psr.next()
                            psb = pst.v().bitcast(BF16)
                            k.tr(psb[:, 0:128], o.v(), ident_b.v())
                            k.cp(obT[:, hh, qt * 128:(qt + 1) * 128], psb[:, 0:128], eng="act")
                    else:
                        acc = accr.next()
                        for g in range(3):
                            ps2 = psr.next()
                            k.mm(ps2[0:64, 0:64], KT[:, g, SEQ:TOK], QT[:, g, 0:64])
                            E = E_r.next()
                            k.actf(E[0:64, 0:64], ps2[0:64, 0:64], AF.Exp, scale=SC)
                            k.tt(E[0:64, 0:64], E[0:64, 0:64], nmask[:, g, :], ALU.mult)
                            k.mm(acc[0:64, 0:129], E[0:64, 0:64], VA[0:64, 16, g, :], start=(g == 0), stop=False)
                        for s in range(NSS):
                            Kc = Kc_r.next()
                            Vc = Vc_r.next()
                            hs = slice(hh * 128, (hh + 1) * 128)
                            for (src, dstb) in ((cwin_k, Kc), (cwin_v, Vc)):
                                k.dma(k.sp, dstb[:, 0, :], V(src[0].t[l, s, :, hs], [src[0]]))
                                k.dma(k.sp, dstb[:, 1:5, :], V(src[1].t[l, s, :, hs].rearrange("(b p) n -> p b n", p=128), [src[1]]))
                                k.dma(k.sp, dstb[:, 5:9, :], V(src[2].t[l, s, :, hs].rearrange("(m x) n -> m x n", x=16)[:, 0:4, :], [src[2]]))
                            Vcb = Vcb_r.next()
                            k.cp(Vcb[:, :, 0:128], Vc.v(), eng="pool")
                            KcT = KcT_r.next()
                            for grp in range(3):
                                nb = 4 if grp < 2 else 1
                                pst = psr.next()
                                for j in range(nb):
                                    k.tr(pst[:, j * 128:(j + 1) * 128], Kc[:, grp * 4 + j, :], ident_f.v())
                                k.cp(KcT[:, grp * 4:grp * 4 + nb, :], pst[:, 0:nb * 128].rr("p (c j) -> p c j", c=nb), eng="act")
                            ps2 = psr.next()
                            for b in range(9):
                                g = 0 if b == 0 else (1 if b < 5 else 2)
                                k.mm(ps2[:, b * 4:(b + 1) * 4], KcT[:, b, :], QT[:, g, s * 4:(s + 1) * 4])
                            PTz = PTz_r.next()
                            k.actf(PTz[:, :, s * 4:(s + 1) * 4], ps2[:, 0:36].rr("p (c j) -> p c j", c=9), AF.Exp, scale=SC)
                            k.tt(PTz[:, :, s * 4:(s + 1) * 4], PTz[:, :, s * 4:(s + 1) * 4], smask.v(), ALU.mult)
                            for b in range(9):
                                k.mm(acc[0:64, 0:129], PTz[:, b, :], Vcb[:, b, :], start=False, stop=(s == NSS - 1 and b == 8))
                            k.memset(PTz[:, :, s * 4:(s + 1) * 4], 0.0)
                        sm = small.next()
                        k.recip(sm[0:64, 0:1], acc[0:64, 128:129])
                        o = o_r.next()
                        k.ts(o[0:64, :], acc[0:64, 0:128], sm[0:64, 0:1], None, op0=ALU.mult)
                        pst = psr.next()
                        psb = pst.v().bitcast(BF16)
                        k.tr(psb[:, 0:64], o[0:64, :], ident_b[0:64, 0:64])
                        k.cp(obT[:, hh, SEQ:TOK], psb[:, 0:64], eng="act")

        def ssd_branch():
          with k.phase():
            win = W["w_in"]
            Wdt = k.sb("Wdt", [128, 8, 16], BF16)
            sc_dt = k.sb("sc_dt", [128, NT, 16])
            sc_ac = k.sb("sc_ac", [128, NT, 16])
            sc_al = k.sb("sc_al", [128, NT, 16])
            sc_ed = k.sb("sc_ed", [128, NT, 16])
            dtb = k.sb("dtb", [128, 16])
            Aneg = k.sb("Aneg", [128, 16])
            Dsk = k.sb("Dsk", [128, 16])
            gB = k.sb("gB", [128, 1024])
            tri = k.sb("tri", [128, 2, 128])
            segm = k.sb("segm", [128, 2, 128])
            ones_f = k.sb("ones_f", [128, 128])
            segcol = k.sb("segcol", [128, NSS, 64], BF16)
            rowmask = k.sb("rowmask", [64, NSS], BF16)
            k.dma(k.sp, tri.v(), c_tri.v())
            k.dma(k.sp, segm.v(), c_segm.v())
            k.dma(k.pool, segcol.v(), c_segcol.v())
            k.dma(k.pool, rowmask.v(), c_rowmask.v())
            k.memset(ones_f.v(), 1.0)
            k.dma(k.sp, dtb.v(), V(ssm_dt_bias.t[l, :].partition_broadcast(128), [ssm_dt_bias]))
            k.dma(k.sp, Aneg.v(), V(ssm_a_log.t[l, :].partition_broadcast(128), [ssm_a_log]))
            k.actf(Aneg.v(), Aneg.v(), AF.Exp)
            k.ts(Aneg.v(), Aneg.v(), -1.0, None, op0=ALU.mult)
            k.dma(k.sp, Dsk.v(), V(ssm_d.t[l, :].partition_broadcast(128), [ssm_d]))
            k.dma(k.sp, gB.v(), V(ssm_norm.t[l, :].partition_broadcast(128), [ssm_norm]))
            load_w(Wdt.v(), V(win.t[l, :, C_SSDT:C_SSDT + 16].rearrange("(c p) n -> p c n", p=128), [win]))
            tmp16 = Ring([k.sb("tmp16_%d" % i, [128, 16]) for i in range(3)])
            for t in range(NT):
                n = tn(t)
                v = 0 if t < 16 else 1
                ps = psr.next()
                for kc in range(8):
                    k.mm(ps[0:n, 0:16], hT(kc, t, t + 1), Wdt[:, kc, :], start=(kc == 0), stop=(kc == 7))
                x1 = tmp16.next()
                k.tt(x1[0:n, :], ps[0:n, 0:16], dtb[0:n, :], ALU.add)
                k.actf(x1[0:n, :], x1[0:n, :], AF.Exp)
                k.actf(sc_dt[0:n, t, :], x1[0:n, :], AF.Ln, bias=1.0)
                a = tmp16.next()
                k.tt(a[0:n, :], sc_dt[0:n, t, :], Aneg[0:n, :], ALU.mult)
                ps2 = psr.next()
                k.mm(ps2[0:n, 0:16], tri[0:n, v, 0:n], a[0:n, :])
                k.cp(sc_ac[0:n, t, :], ps2[0:n, 0:16])
                ps3 = psr.next()
                k.mm(ps3[0:n, 0:16], segm[0:n, v, 0:n], a[0:n, :])
                k.cp(sc_al[0:n, t, :], ps3[0:n, 0:16], eng="act")
                e = tmp16.next()
                k.tt(e[0:n, :], sc_al[0:n, t, :], sc_ac[0:n, t, :], ALU.subtract)
                k.actf(sc_ed[0:n, t, :], e[0:n, :], AF.Exp)


            def group_prog(gi):
                cTb = k.sb("cTs", [128, 4, 512], BF16)
                pre_r = Ring([k.sb("pre%d" % i, [128, 515]) for i in range(1)])
                halo = k.sb("halo", [128, 4, 3])
                pre_s = k.sb("pre_s", [128, 4, NSS, 7])
                Wg = k.sb("Wg", [128, 8, 512], BF16)
                Wz = k.sb("Wz", [128, 8, 256], BF16)
                cw = k.sb("cw", [128, 4, 4])
                cb = k.sb("cb", [128, 4])
                cacc = k.sb("caccs", [128, 576])
                rd_r = Ring([k.sb("rd%d" % i, [128, 4, 128]) for i in range(1)])
                u_r = Ring([k.sb("u%d" % i, [128, 4, 128]) for i in range(2)])
                eR_r = Ring([k.sb("eR%d" % i, [128, 4, 128]) for i in range(2)])
                WT_r = Ring([k.sb("WT%d" % i, [128, 4, 128], BF16) for i in range(2)])
                CE_r = Ring([k.sb("CE%d" % i, [128, 4, 128], BF16) for i in range(2)])
                xt_r = Ring([k.sb("xtok%d" % i, [128, 4, 64], BF16) for i in range(2)])
                xd_r = Ring([k.sb("xdt%d" % i, [128, 4, 64], BF16) for i in range(2)])
                xe_r = Ring([k.sb("xdec%d" % i, [128, 4, 64], BF16) for i in range(2)])
                bt_r = Ring([k.sb("btok%d" % i, [128, 128], BF16) for i in range(2)])
                yy_r = Ring([k.sb("yy%d" % i, [128, 256]) for i in range(2)])
                zs_r = Ring([k.sb("zs%d" % i, [128, 256]) for i in range(1)])
                ob_r = Ring([k.sb("ob%d" % i, [128, 256], BF16) for i in range(2)])
                eal_r = Ring([k.sb("eal%d" % i, [128, 4]) for i in range(2)])
                hst = k.sb("hst", [128, 4, 64])
                hstb = k.sb("hstb", [128, 4, 64], BF16)
                hs_r = Ring([k.sb("hs%d" % i, [128, 4, 64]) for i in range(2)])
                hsb_r = Ring([k.sb("hsb%d" % i, [128, 4, 64], BF16) for i in range(2)])
                CEm_r = Ring([k.sb("CEm%d" % i, [128, 4, 64], BF16) for i in range(2)])
                Bm_r = Ring([k.sb("Bm%d" % i, [64, 128], BF16) for i in range(2)])
                stl_r = Ring([k.sb("stl%d" % i, [128, 2, 128]) for i in range(1)])
                sto_r = Ring([k.sb("sto%d" % i, [128, 128]) for i in range(2)])
                cx = C_SSX + gi * 256
                load_w(Wg[:, :, 0:256], V(win.t[l, :, cx:cx + 256].rearrange("(c p) n -> p c n", p=128), [win]))
                cbb = C_SSX + 1024 + gi * 128
                load_w(Wg[:, :, 256:384], V(win.t[l, :, cbb:cbb + 128].rearrange("(c p) n -> p c n", p=128), [win]))
                ccc = C_SSX + 1536 + gi * 128
                load_w(Wg[:, :, 384:512], V(win.t[l, :, ccc:ccc + 128].rearrange("(c p) n -> p c n", p=128), [win]))
                cz = C_SSZ + gi * 256
                load_w(Wz.v(), V(win.t[l, :, cz:cz + 256].rearrange("(c p) n -> p c n", p=128), [win]))
                ch0 = [gi * 256, gi * 256 + 128, 1024 + gi * 128, 1536 + gi * 128]
                with nc.allow_non_contiguous_dma(reason="tiny conv params"):
                    for b in range(4):
                        k.dma(k.sp, cw[:, b, :], V(ssm_conv_w.t[l, :, ch0[b]:ch0[b] + 128].rearrange("j p -> p j"), [ssm_conv_w]))
                        k.dma(k.sp, cb[:, b:b + 1], V(ssm_conv_b.t[l, ch0[b]:ch0[b] + 128].rearrange("(p o) -> p o", o=1), [ssm_conv_b]))
                stc = sto_r.next()
                for b in range(4):
                    stc = sto_r.next()
                    k.dma(k.sp, stc[0:48, :], V(state_ssm_conv.t[l, :, :, ch0[b]:ch0[b] + 128].rearrange("s j n -> (s j) n"), [state_ssm_conv]))
                    pst = psr.next()
                    k.tr(pst[:, 0:48], stc[0:48, :], ident_f[0:48, 0:48])
                    k.cp(pre_s[:, b, :, 0:3], pst[:, 0:48].rr("p (s j) -> p s j", j=3))
                k.memset(hst.v(), 0.0)
                k.memset(hstb.v(), 0.0)
                prev_pre = None
                for tb in range(5):
                    t0, t1 = tb * 4, min(tb * 4 + 4, NT)
                    ntok = min(512, TOK - tb * 512)
                    a0 = tb * 512
                    if tb == 0:
                        k.memset(halo.v(), 0.0)
                    for b in range(4):
                        ps = psr.next()
                        for kc in range(8):
                            k.mm(ps[:, 0:ntok], Wg[:, kc, b * 128:(b + 1) * 128], hT(kc, t0, t1), start=(kc == 0), stop=(kc == 7))
                        if tb < 4:
                            pre = pre_r.next()
                            k.cp(pre[:, 0:3], halo[:, b, :], eng="pool")
                            k.cp(pre[:, 3:515], ps.v(), eng="act")
                            k.cp(halo[:, b, :], pre[:, 512:515], eng="pool")
                            src = lambda j, pre=pre: pre[:, j:j + 512]
                            acc = cacc[:, 0:512]
                            dst = cTb[:, b, 0:512]
                        else:
                            k.cp(pre_s[:, b, :, 3:7], ps[:, 0:64].rr("p (s j) -> p s j", j=4), eng="act")
                            src = lambda j: pre_s[:, b, :, j:j + 4]
                            acc = cacc[:, 0:64].rr("p (s j) -> p s j", j=4)
                            dst = cTb[:, b, 0:64].rr("p (s j) -> p s j", j=4)
                        k.ts(acc, src(0), cw[:, b, 0:1], cb[:, b:b + 1], op0=ALU.mult, op1=ALU.add)
                        for j in range(1, 4):
                            k.stt(acc, src(j), cw[:, b, j:j + 1], acc, ALU.mult, ALU.add)
                        k.actf(dst, acc, AF.Silu)
                        if tb == 3 or tb == 4:
                            pst = psr.next()
                            sto = sto_r.next()
                            if tb == 3:
                                k.tr(pst[0:3, 0:128], pre[:, 512:515], ident_f.v())
                                k.cp(sto[0:3, :], pst[0:3, 0:128])
                                k.dma(k.sp, p_ssm_conv[l, :, ch0[b]:ch0[b] + 128], sto[0:3, :])
                            else:
                                cst = cacc[:, 512:560]
                                k.cp(cst.rr("p (s j) -> p s j", j=3), pre_s[:, b, :, 4:7], eng="pool")
                                k.tr(pst[0:48, 0:128], cst, ident_f.v())
                                k.cp(sto[0:48, :], pst[0:48, 0:128])
                                k.dma(k.sp, V(s_ssm_conv.t[l, :, :, ch0[b]:ch0[b] + 128].rearrange("s j n -> (s j) n"), [s_ssm_conv]), sto[0:48, :])
                    for t in range(t0, t1):
                        n = tn(t)
                        v = 0 if t < 16 else 1
                        ta = t * 128
                        lo = (t - t0) * 128
                        hs4 = slice(gi * 4, gi * 4 + 4)
                        ac = sc_ac[0:n, t, hs4]
                        rd = rd_r.next()
                        k.tt(rd[0:n, :, 0:n], ident_f[0:n, 0:n].unsq(1).bc([n, 4, n]), ac.unsq(2).bc([n, 4, n]), ALU.mult, eng="pool")
                        Rp = psr.next()
                        k.mm(Rp[:, 0:4 * n], ones_f[0:n, :], rd[0:n, :, 0:n])
                        R3 = Rp[:, 0:4 * n].rr("p (h f) -> p h f", h=4)
                        u = u_r.next()
                        k.tt(u[0:n, :, 0:n], R3[0:n], ac.unsq(2).bc([n, 4, n]), ALU.subtract)
                        k.ts(u[0:n, :, 0:n], u[0:n, :, 0:n], 0.0, None, op0=ALU.min, eng="pool")
                        k.actf(u[0:n, :, 0:n], u[0:n, :, 0:n], AF.Exp)
                        k.tt(u[0:n, :, 0:n], u[0:n, :, 0:n], tri[0:n, v, 0:n].unsq(1).bc([n, 4, n]), ALU.mult, eng="pool")
                        eR = eR_r.next()
                        k.actf(eR[:, :, 0:n], R3, AF.Exp)
                        eal = eal_r.next()
                        if t < 16:
                            k.cp(eal.v(), eR[:, :, n - 1])
                        cbp = psr.next()
                        k.mm(cbp[0:n, 0:n], cTb[:, 2, lo:lo + n], cTb[:, 3, lo:lo + n])
                        WT = WT_r.next()
                        k.tt(WT[0:n, :, 0:n], cbp[0:n, 0:n].unsq(1).bc([n, 4, n]), u[0:n, :, 0:n], ALU.mult)
                        CE = CE_r.next()
                        k.tt(CE[:, :, 0:n], cTb[:, 3, lo:lo + n].unsq(1).bc([128, 4, n]), eR[:, :, 0:n], ALU.mult, eng="pool")
                        xp = psr.next()
                        xpb = xp.v().bitcast(BF16)
                        for b in range(2):
                            k.tr(xpb[0:n, b * 128:(b + 1) * 128], cTb[:, b, lo:lo + n], ident_b.v())
                        xtok = xt_r.next()
                        k.cp(xtok[0:n], xpb[0:n, 0:256].rr("p (h q) -> p h q", h=4), eng="act")
                        xdt = xd_r.next()
                        k.tt(xdt[0:n], xtok[0:n], sc_dt[0:n, t, hs4].unsq(2).bc([n, 4, 64]), ALU.mult, eng="pool")
                        xdec = xe_r.next()
                        k.tt(xdec[0:n], xdt[0:n], sc_ed[0:n, t, hs4].unsq(2).bc([n, 4, 64]), ALU.mult, eng="pool")
                        bp = psr.next()
                        bpb = bp.v().bitcast(BF16)
                        k.tr(bpb[0:n, 0:128], cTb[:, 2, lo:lo + n], ident_b.v())
                        btok = bt_r.next()
                        k.cp(btok[0:n, :], bpb[0:n, 0:128], eng="act")
                        yp = accr.next()
                        if t < 16:
                            for h in range(4):
                                k.mm(yp[0:n, h * 64:(h + 1) * 64], WT[0:n, h, 0:n], xdt[0:n, h, :], start=(h == 0), stop=False)
                                k.mm(yp[0:n, h * 64:(h + 1) * 64], CE[:, h, 0:n], hstb[:, h, :], start=False, stop=(h == 3))
                        else:
                            for h in range(4):
                                k.mm(yp[0:n, h * 64:(h + 1) * 64], WT[0:n, h, 0:n], xdt[0:n, h, :], start=(h == 0), stop=False)
                            for s in range(NSS):
                                stl = stl_r.next()
                                k.dma(k.sp, stl.v(), V(state_ssm.t[l, s, gi * 4:gi * 4 + 4].rearrange("(a h) p n -> (h p) a n", h=2), [state_ssm]))
                                pst = psr.next()
                                for a2 in range(2):
                                    k.tr(pst[:, a2 * 128:(a2 + 1) * 128], stl[:, a2, :], ident_f.v())
                                hs = hs_r.next()
                                hsb = hsb_r.next()
                                k.cp(hs.v(), pst[:, 0:256].rr("p (h q) -> p h q", h=4))
                                k.cp(hsb.v(), hs.v(), eng="act")
                                CEm = CEm_r.next()
                                k.tt(CEm.v(), CE[:, :, 0:64], segcol[:, s, :].unsq(1).bc([128, 4, 64]), ALU.mult, eng="pool")
                                for h in range(4):
                                    k.mm(yp[0:n, h * 64:(h + 1) * 64], CEm[:, h, :], hsb[:, h, :], start=False, stop=(h == 3 and s == NSS - 1))
                                Bm = Bm_r.next()
                                k.ts(Bm.v(), btok[0:64, :], rowmask[:, s:s + 1], None, op0=ALU.mult, eng="pool")
                                hp = psr.next()
                                k.mm(hp[:, 0:256], Bm.v(), xdec[0:64].rr("p h q -> p (h q)"))
                                eals = eal_r.next()
                                k.cp(eals.v(), eR[:, :, 4 * s + 3])
                                k.tt(hs.v(), hs.v(), eals.v().unsq(2).bc([128, 4, 64]), ALU.mult, eng="pool")
                                k.tt(hs.v(), hs.v(), hp[:, 0:256].rr("p (h q) -> p h q", h=4), ALU.add)
                                for a2 in range(2):
                                    pst2 = psr.next()
                                    k.tr(pst2[:, 0:128], hs[:, a2 * 2:a2 * 2 + 2, :].rr("p h q -> p (h q)"), ident_f.v())
                                    sto = sto_r.next()
                                    k.cp(sto.v(), pst2[:, 0:128], eng="act")
                                    k.dma(k.sp, V(s_ssm.t[l, s, gi * 4 + a2 * 2:gi * 4 + a2 * 2 + 2].rearrange("h p n -> (h p) n"), [s_ssm]), sto.v())
                        if t < 16:
                            hp = psr.next()
                            k.mm(hp[:, 0:256], btok[0:n, :], xdec[0:n].rr("p h q -> p (h q)"))
                            k.tt(hst.v(), hst.v(), eal.v().unsq(2).bc([128, 4, 64]), ALU.mult, eng="pool")
                            k.tt(hst.v(), hst.v(), hp[:, 0:256].rr("p (h q) -> p h q", h=4), ALU.add)
                            k.cp(hstb.v(), hst.v(), eng="act")
                            if t == 15:
                                for a2 in range(2):
                                    pst = psr.next()
                                    k.tr(pst[:, 0:128], hst[:, a2 * 2:a2 * 2 + 2, :].rr("p h q -> p (h q)"), ident_f.v())
                                    sto = sto_r.next()
                                    k.cp(sto.v(), pst[:, 0:128])
                                    k.dma(k.sp, V(p_ssm.t[l, gi * 4 + a2 * 2:gi * 4 + a2 * 2 + 2].rearrange("h p n -> (h p) n"), [p_ssm]), sto.v())
                        yy = yy_r.next()
                        k.tt(yy[0:n].rr("p (h q) -> p h q", h=4), xtok[0:n], Dsk[0:n, hs4].unsq(2).bc([n, 4, 64]), ALU.mult, eng="pool")
                        k.tt(yy[0:n], yy[0:n], yp[0:n, 0:256], ALU.add)
                        zp = psr.next()
                        for kc in range(8):
                            k.mm(zp[0:n, 0:256], hT(kc, t, t + 1), Wz[:, kc, :], start=(kc == 0), stop=(kc == 7))
                        zs = zs_r.next()
                        k.actf(zs[0:n], zp[0:n, 0:256], AF.Silu)
                        k.tt(yy[0:n], yy[0:n], zs[0:n], ALU.mult, eng="pool")
                        r = rstd_of(yy[0:n], n, 1.0 / 256)
                        ob = ob_r.next()
                        k.stt(ob[0:n], yy[0:n], r, gB[0:n, gi * 256:(gi + 1) * 256], ALU.mult, ALU.mult)
                        op_ = psr.next()
                        opb = op_.v().bitcast(BF16)
                        for b in range(2):
                            k.tr(opb[:, b * 128:b * 128 + n], ob[0:n, b * 128:(b + 1) * 128], ident_b[0:n, 0:n])
                        k.cp(obT[:, gi * 2:gi * 2 + 2, ta:ta + n], opb[:, 0:256].rr("p (c j) -> p c j", c=2)[:, :, 0:n], eng="act")

            for gp in range(2):
                with k.phase():
                    if 'noil' in FLAGS:
                        group_prog(2 * gp)
                        group_prog(2 * gp + 1)
                    else:
                        k.interleave([lambda g=2 * gp: group_prog(g), lambda g=2 * gp + 1: group_prog(g)])

        def dn_branch():
          with k.phase():
            win = W["w_in"]
            Wba = k.sb("Wba", [128, 8, 16], BF16)
            sc_b = k.sb("sc_b", [128, NT, 8])
            sc_nb = k.sb("sc_nb", [128, NT, 8])
            sc_gc = k.sb("sc_gc", [128, NT, 8])
            sc_eg = k.sb("sc_eg", [128, NT, 8])
            sc_ed = k.sb("sc_edd", [128, NT, 8])
            dtb = k.sb("dtbd", [128, 8])
            Aneg = k.sb("Anegd", [128, 8])
            gB = k.sb("gBd", [128, 128])
            tri = k.sb("trid", [128, 2, 128])
            stri = k.sb("strid", [128, 2, 128])
            segm = k.sb("segmd", [128, 2, 128])
            ones_f = k.sb("ones_fd", [128, 128])
            segcol = k.sb("segcold", [128, NSS, 64], BF16)
            rowmask = k.sb("rowmaskd", [64, NSS], BF16)
            k.dma(k.sp, tri.v(), c_tri.v())
            k.dma(k.sp, stri.v(), c_stri.v())
            k.dma(k.sp, segm.v(), c_segm.v())
            k.dma(k.pool, segcol.v(), c_segcol.v())
            k.dma(k.pool, rowmask.v(), c_rowmask.v())
            k.memset(ones_f.v(), 1.0)
            k.dma(k.sp, dtb.v(), V(dn_dt_bias.t[l, :].partition_broadcast(128), [dn_dt_bias]))
            k.dma(k.sp, Aneg.v(), V(dn_a_log.t[l, :].partition_broadcast(128), [dn_a_log]))
            k.actf(Aneg.v(), Aneg.v(), AF.Exp)
            k.ts(Aneg.v(), Aneg.v(), -1.0, None, op0=ALU.mult)
            k.dma(k.sp, gB.v(), V(dn_norm.t[l, :].partition_broadcast(128), [dn_norm]))
            load_w(Wba.v(), V(win.t[l, :, C_DNB:C_DNB + 16].rearrange("(c p) n -> p c n", p=128), [win]))
            tmp8 = Ring([k.sb("tmp8_%d" % i, [128, 8]) for i in range(4)])
            for t in range(NT):
                n = tn(t)
                v = 0 if t < 16 else 1
                ps = psr.next()
                for kc in range(8):
                    k.mm(ps[0:n, 0:16], hT(kc, t, t + 1), Wba[:, kc, :], start=(kc == 0), stop=(kc == 7))
                k.actf(sc_b[0:n, t, :], ps[0:n, 0:8], AF.Sigmoid)
                k.ts(sc_nb[0:n, t, :], sc_b[0:n, t, :], -1.0, None, op0=ALU.mult, eng="pool")
                x1 = tmp8.next()
                k.tt(x1[0:n, :], ps[0:n, 8:16], dtb[0:n, :], ALU.add)
                k.actf(x1[0:n, :], x1[0:n, :], AF.Exp)
                k.actf(x1[0:n, :], x1[0:n, :], AF.Ln, bias=1.0)
                g = tmp8.next()
                k.tt(g[0:n, :], x1[0:n, :], Aneg[0:n, :], ALU.mult)
                ps2 = psr.next()
                k.mm(ps2[0:n, 0:8], tri[0:n, v, 0:n], g[0:n, :])
                k.cp(sc_gc[0:n, t, :], ps2[0:n, 0:8])
                k.actf(sc_eg[0:n, t, :], ps2[0:n, 0:8], AF.Exp)
                ps3 = psr.next()
                k.mm(ps3[0:n, 0:8], segm[0:n, v, 0:n], g[0:n, :])
                e = tmp8.next()
                k.tt(e[0:n, :], ps3[0:n, 0:8], sc_gc[0:n, t, :], ALU.subtract)
                k.actf(sc_ed[0:n, t, :], e[0:n, :], AF.Exp)
            blk = k.sb("blk", [128, 4, 128])
            k.dma(k.sp, blk.v(), c_blk.v())


            def head_prog(h):
                cTb = k.sb("cTb", [128, 3, 512], BF16)
                pre_r = Ring([k.sb("pred%d" % i, [128, 515]) for i in range(1)])
                halo = k.sb("halod", [128, 3, 3])
                pre_s = k.sb("pre_sd", [128, 3, NSS, 7])
                Wh = k.sb("Wh", [128, 8, 384], BF16)
                Wz = k.sb("Wzd", [128, 8, 128], BF16)
                cw = k.sb("cwd", [128, 3, 4])
                cacc = k.sb("cacc", [128, 576])
                f32r = Ring([k.sb("f32_%d" % i, [128, 128]) for i in range(12)])
                eRd_r = Ring([k.sb("eRd%d" % i, [128, 128]) for i in range(2)])
                Nd_r = Ring([k.sb("Nd%d" % i, [128, 128]) for i in range(2)])
                Pd_r = Ring([k.sb("Pd%d" % i, [128, 128]) for i in range(2)])
                o1_r = Ring([k.sb("o1d%d" % i, [128, 128]) for i in range(2)])
                zsd_r = Ring([k.sb("zsd%d" % i, [128, 128]) for i in range(2)])
                b16s = Ring([k.sb("b16s_%d" % i, [128, 128], BF16) for i in range(4)])
                b16r = Ring([k.sb("b16_%d" % i, [128, 128], BF16) for i in range(16)])
                smr = Ring([k.sb("smd%d" % i, [128, 8]) for i in range(6)])
                S = k.sb("S", [128, 128])
                Sb = k.sb("Sb", [128, 128], BF16)
                Sb_all = k.sb("Sb_all", [128, NSS, 128], BF16)
                Ss_r = Ring([k.sb("Ss%d" % i, [128, 128]) for i in range(2)])
                sto_r = Ring([k.sb("stod%d" % i, [128, 128]) for i in range(2)])
                for b in range(3):
                    c0 = C_DNQKV + b * 1024 + h * 128
                    load_w(Wh[:, :, b * 128:(b + 1) * 128], V(win.t[l, :, c0:c0 + 128].rearrange("(c p) n -> p c n", p=128), [win]))
                cz = C_DNZ + h * 128
                load_w(Wz.v(), V(win.t[l, :, cz:cz + 128].rearrange("(c p) n -> p c n", p=128), [win]))
                ch0 = [b * 1024 + h * 128 for b in range(3)]
                with nc.allow_non_contiguous_dma(reason="tiny conv params"):
                    for b in range(3):
                        k.dma(k.sp, cw[:, b, :], V(dn_conv_w.t[l, :, ch0[b]:ch0[b] + 128].rearrange("j p -> p j"), [dn_conv_w]))
                for b in range(3):
                    stc = sto_r.next()
                    k.dma(k.sp, stc[0:48, :], V(state_dn_conv.t[l, :, :, ch0[b]:ch0[b] + 128].rearrange("s j n -> (s j) n"), [state_dn_conv]))
                    pst = psr.next()
                    k.tr(pst[:, 0:48], stc[0:48, :], ident_f[0:48, 0:48])
                    k.cp(pre_s[:, b, :, 0:3], pst[:, 0:48].rr("p (s j) -> p s j", j=3))
                k.dma(k.pool, Sb_all.v(), V(state_dn.t[l, :, h].rearrange("s k v -> k s v"), [state_dn]))
                k.memset(S.v(), 0.0)
                k.memset(Sb.v(), 0.0)
                for tb in range(5):
                    t0, t1 = tb * 4, min(tb * 4 + 4, NT)
                    ntok = min(512, TOK - tb * 512)
                    a0 = tb * 512
                    if tb == 0:
                        k.memset(halo.v(), 0.0)
                    for b in range(3):
                        ps = psr.next()
                        for kc in range(8):
                            k.mm(ps[:, 0:ntok], Wh[:, kc, b * 128:(b + 1) * 128], hT(kc, t0, t1), start=(kc == 0), stop=(kc == 7))
                        if tb < 4:
                            pre = pre_r.next()
                            k.cp(pre[:, 0:3], halo[:, b, :], eng="pool")
                            k.cp(pre[:, 3:515], ps.v(), eng="act")
                            k.cp(halo[:, b, :], pre[:, 512:515], eng="pool")
                            src = lambda j, pre=pre: pre[:, j:j + 512]
                            acc = cacc[:, 0:512]
                            dst = cTb[:, b, 0:512]
                        else:
                            k.cp(pre_s[:, b, :, 3:7], ps[:, 0:64].rr("p (s j) -> p s j", j=4), eng="act")
                            src = lambda j, b=b: pre_s[:, b, :, j:j + 4]
                            acc = cacc[:, 0:64].rr("p (s j) -> p s j", j=4)
                            dst = cTb[:, b, 0:64].rr("p (s j) -> p s j", j=4)
                        k.ts(acc, src(0), cw[:, b, 0:1], None, op0=ALU.mult)
                        for j in range(1, 4):
                            k.stt(acc, src(j), cw[:, b, j:j + 1], acc, ALU.mult, ALU.add)
                        k.actf(dst, acc, AF.Silu)
                        if tb == 3 or tb == 4:
                            pst = psr.next()
                            sto = sto_r.next()
                            if tb == 3:
                                k.tr(pst[0:3, 0:128], pre[:, 512:515], ident_f.v())
                                k.cp(sto[0:3, :], pst[0:3, 0:128])
                                k.dma(k.sp, p_dn_conv[l, :, ch0[b]:ch0[b] + 128], sto[0:3, :])
                            else:
                                cst = cacc[:, 512:560]
                                k.cp(cst.rr("p (s j) -> p s j", j=3), pre_s[:, b, :, 4:7], eng="pool")
                                k.tr(pst[0:48, 0:128], cst, ident_f.v())
                                k.cp(sto[0:48, :], pst[0:48, 0:128])
                                k.dma(k.sp, V(s_dn_conv.t[l, :, :, ch0[b]:ch0[b] + 128].rearrange("s j n -> (s j) n"), [s_dn_conv]), sto[0:48, :])
                    for t in range(t0, t1):
                        n = tn(t)
                        v = 0 if t < 16 else 1
                        ta = t * 128
                        samp = (t == 16)
                        beta = sc_b[0:n, t, h:h + 1]
                        nbeta = sc_nb[0:n, t, h:h + 1]
                        gc = sc_gc[0:n, t, h:h + 1]
                        tp = psr.next()
                        tpb = tp.v().bitcast(BF16)
                        for b in range(3):
                            k.tr(tpb[0:n, b * 128:(b + 1) * 128], cTb[:, b, (t - t0) * 128:(t - t0) * 128 + n], ident_b.v())
                        sm = smr.next()
                        jk = junk_r.next()
                        k.actf(jk[0:n, 0:128], tpb[0:n, 0:128], AF.Square, accum=sm[0:n, 0:1])
                        k.actf(jk[0:n, 128:256], tpb[0:n, 128:256], AF.Square, accum=sm[0:n, 1:2])
                        k.ts(sm[0:n, 2:4], sm[0:n, 0:2], EPS, None, op0=ALU.add)
                        k.actf(sm[0:n, 2:4], sm[0:n, 2:4], AF.Sqrt)
                        k.recip(sm[0:n, 4:6], sm[0:n, 2:4])
                        sc2 = smr.next()
                        k.ts(sc2[0:n, 0:1], sm[0:n, 4:5], SC, None, op0=ALU.mult, eng="pool")
                        k.tt(sc2[0:n, 1:2], sm[0:n, 5:6], sc_eg[0:n, t, h:h + 1], ALU.mult, eng="pool")
                        k.tt(sc2[0:n, 2:3], sm[0:n, 5:6], sc_ed[0:n, t, h:h + 1], ALU.mult, eng="pool")
                        qn, kn, ke, kd, vv = [b16r.next() for _ in range(5)]
                        k.actf(qn[0:n], tpb[0:n, 0:128], AF.Copy, scale=sc2[0:n, 0:1])
                        k.ts(kn[0:n], tpb[0:n, 128:256], sm[0:n, 5:6], None, op0=ALU.mult)
                        k.actf(ke[0:n], tpb[0:n, 128:256], AF.Copy, scale=sc2[0:n, 1:2])
                        k.ts(kd[0:n], tpb[0:n, 128:256], sc2[0:n, 2:3], None, op0=ALU.mult)
                        k.cp(vv[0:n], tpb[0:n, 256:384], eng="act")
                        tp2 = psr.next()
                        tp2b = tp2.v().bitcast(BF16)
                        k.tr(tp2b[:, 0:n], kn[0:n], ident_b[0:n, 0:n])
                        k.tr(tp2b[:, 128:128 + n], qn[0:n], ident_b[0:n, 0:n])
                        knT, qnT = b16r.next(), b16r.next()
                        k.cp(knT[:, 0:n], tp2b[:, 0:n])
                        k.cp(qnT[:, 0:n], tp2b[:, 128:128 + n], eng="act")
                        rd = f32r.next()
                        k.ts(rd[0:n, 0:n], ident_f[0:n, 0:n], gc, None, op0=ALU.mult, eng="pool")
                        Rp = psr.next()
                        k.mm(Rp[:, 0:n], ones_f[0:n, :], rd[0:n, 0:n])
                        DTm = f32r.next()
                        k.ts(DTm[0:n, 0:n], Rp[0:n, 0:n], gc, 0.0, op0=ALU.subtract, op1=ALU.min)
                        k.actf(DTm[0:n, 0:n], DTm[0:n, 0:n], AF.Exp)
                        DTs = f32r.next()
                        k.tt(DTs[0:n, 0:n], DTm[0:n, 0:n], stri[0:n, v, 0:n], ALU.mult, eng="pool")
                        k.tt(DTm[0:n, 0:n], DTm[0:n, 0:n], tri[0:n, v, 0:n], ALU.mult, eng="pool")
                        eR = eRd_r.next()
                        k.actf(eR[:, 0:n], Rp[:, 0:n], AF.Exp)
                        Gp = psr.next()
                        k.mm(Gp[0:n, 0:n], knT[:, 0:n], knT[:, 0:n])
                        P = Pd_r.next()
                        k.stt(P[0:n, 0:n], Gp[0:n, 0:n], nbeta, DTs[0:n, 0:n], ALU.mult, ALU.mult)
                        STp = psr.next()
                        k.mm(STp[0:n, 0:n], knT[:, 0:n], qnT[:, 0:n])
                        STm = b16r.next()
                        k.tt(STm[0:n, 0:n], STp[0:n, 0:n], DTm[0:n, 0:n], ALU.mult)
                        Np = psr.next()
                        k.tr(Np[0:n, 0:n], P[0:n, 0:n], ident_f[0:n, 0:n])
                        N = Nd_r.next()
                        k.cp(N[0:n, 0:n], Np[0:n, 0:n], eng="act")
                        Tt = f32r.next()
                        k.tt(Tt[0:n, 0:n], P[0:n, 0:n], ident_f[0:n, 0:n], ALU.add, eng="pool")
                        if samp:
                            p1 = psr.next()
                            k.mm(p1[0:n, 0:n], P[0:n, 0:n], N[0:n, 0:n])
                            N2 = f32r.next()
                            k.cp(N2[0:n, 0:n], p1[0:n, 0:n], eng="act")
                            p3 = psr.next()
                            k.mm(p3[0:n, 0:n], N2[0:n, 0:n], Tt[0:n, 0:n])
                            Tt2 = f32r.next()
                            k.tt(Tt2[0:n, 0:n], Tt[0:n, 0:n], p3[0:n, 0:n], ALU.add)
                            Tt = Tt2
                        else:
                            Pc, Nc = f32r.next(), f32r.next()
                            k.tt(Pc.v(), P.v(), blk[:, 0, :], ALU.mult, eng="pool")
                            k.tt(Nc.v(), N.v(), blk[:, 0, :], ALU.mult, eng="pool")
                            k.tt(Tt.v(), Pc.v(), ident_f.v(), ALU.add, eng="pool")
                            for lev in range(3):
                                p1 = psr.next()
                                k.mm(p1[:, 0:128], Pc.v(), Nc.v())
                                N2 = f32r.next()
                                k.cp(N2.v(), p1[:, 0:128], eng="act")
                                if lev < 2:
                                    p2 = psr.next()
                                    k.mm(p2[:, 0:128], Nc.v(), Pc.v())
                                    P2 = f32r.next()
                                    k.cp(P2.v(), p2[:, 0:128])
                                    Pc = P2
                                Nc = N2
                                p3 = psr.next()
                                k.mm(p3[:, 0:128], Nc.v(), Tt.v())
                                Tt2 = f32r.next()
                                k.tt(Tt2.v(), Tt.v(), p3[:, 0:128], ALU.add)
                                Tt = Tt2
                            for mi in range(1, 4):
                                tdp = psr.next()
                                k.tr(tdp[:, 0:128], Tt.v(), ident_f.v())
                                Td = f32r.next()
                                k.cp(Td.v(), tdp[:, 0:128], eng="act")
                                Noff = f32r.next()
                                k.tt(Noff.v(), N.v(), blk[:, mi, :], ALU.mult, eng="pool")
                                z1p = psr.next()
                                k.mm(z1p[:, 0:128], Noff.v(), Tt.v())
                                Z1 = f32r.next()
                                k.cp(Z1.v(), z1p[:, 0:128])
                                y1p = psr.next()
                                k.mm(y1p[:, 0:128], Td.v(), Z1.v())
                                Tt2 = f32r.next()
                                k.tt(Tt2.v(), Tt.v(), y1p[:, 0:128], ALU.add)
                                Tt = Tt2
                        TtT = b16r.next()
                        k.cp(TtT[0:n, 0:n], Tt[0:n, 0:n], eng="act")
                        acc = accr.next()
                        k.mm(acc[0:n, 0:128], TtT[0:n, 0:n], vv[0:n], start=True, stop=False)
                        wp = psr.next()
                        k.mm(wp[:, 0:n], ke[0:n], TtT[0:n, 0:n])
                        nwT = b16r.next()
                        k.actf(nwT[:, 0:n], wp[:, 0:n], AF.Copy, scale=-1.0)
                        o1p = accr.next()
                        if not samp:
                            k.mm(acc[0:n, 0:128], nwT[:, 0:n], Sb.v(), start=False, stop=True)
                            k.mm(o1p[0:n, 0:128], qnT[:, 0:n], Sb.v())
                        else:
                            for s in range(NSS):
                                wm = b16s.next()
                                k.tt(wm[:, 0:64], nwT[:, 0:64], segcol[:, s, :], ALU.mult, eng="pool")
                                k.mm(acc[0:n, 0:128], wm[:, 0:64], Sb_all[:, s, :], start=False, stop=(s == NSS - 1))
                                qm = b16s.next()
                                k.tt(qm[:, 0:64], qnT[:, 0:64], segcol[:, s, :], ALU.mult, eng="pool")
                                k.mm(o1p[0:n, 0:128], qm[:, 0:64], Sb_all[:, s, :], start=(s == 0), stop=(s == NSS - 1))
                        vnew = b16r.next()
                        k.ts(vnew[0:n], acc[0:n, 0:128], beta, None, op0=ALU.mult)
                        o1 = o1_r.next()
                        k.actf(o1[0:n], o1p[0:n, 0:128], AF.Copy, scale=sc_eg[0:n, t, h:h + 1])
                        o2p = psr.next()
                        k.mm(o2p[0:n, 0:128], STm[0:n, 0:n], vnew[0:n])
                        k.tt(o1[0:n], o1[0:n], o2p[0:n, 0:128], ALU.add)
                        if not samp:
                            Sp = psr.next()
                            k.mm(Sp[:, 0:128], kd[0:n], vnew[0:n])
                            k.stt(S.v(), S.v(), eR[:, n - 1:n], Sp[:, 0:128], ALU.mult, ALU.add)
                            k.cp(Sb.v(), S.v(), eng="act")
                            if t == 15:
                                k.dma(k.sp, p_dn[l, h], S.v())
                        else:
                            for s in range(NSS):
                                km = b16s.next()
                                k.ts(km[0:64], kd[0:64], rowmask[:, s:s + 1], None, op0=ALU.mult, eng="pool")
                                Sp = psr.next()
                                k.mm(Sp[:, 0:128], km[0:64], vnew[0:64])
                                Ss = Ss_r.next()
                                k.dma(k.sp, Ss.v(), V(state_dn.t[l, s, h], [state_dn]))
                                k.stt(Ss.v(), Ss.v(), eR[:, 4 * s + 3:4 * s + 4], Sp[:, 0:128], ALU.mult, ALU.add)
                                k.dma(k.sp, V(s_dn.t[l, s, h], [s_dn]), Ss.v())
                        zp = psr.next()
                        for kc in range(8):
                            k.mm(zp[0:n, 0:128], hT(kc, t, t + 1), Wz[:, kc, :], start=(kc == 0), stop=(kc == 7))
                        zs = zsd_r.next()
                        k.actf(zs[0:n], zp[0:n, 0:128], AF.Silu)
                        r = rstd_of(o1[0:n], n, 1.0 / 128)
                        k.stt(o1[0:n], o1[0:n], r, gB[0:n], ALU.mult, ALU.mult)
                        ob = b16r.next()
                        k.tt(ob[0:n], o1[0:n], zs[0:n], ALU.mult, eng="pool")
                        op_ = psr.next()
                        opb = op_.v().bitcast(BF16)
                        k.tr(opb[:, 0:n], ob[0:n], ident_b[0:n, 0:n])
                        k.cp(obT[:, h, ta:ta + n], opb[:, 0:n], eng="act")

            for hp in range(4):
                with k.phase():
                    if 'noil' in FLAGS:
                        head_prog(2 * hp)
                        head_prog(2 * hp + 1)
                    else:
                        k.interleave([lambda h=2 * hp: head_prog(h), lambda h=2 * hp + 1: head_prog(h)])

        first = True
        if 'nossm' not in FLAGS:
            ssd_branch()
            branch_proj("w_br_ssm", 8, 2, first)
            first = False
        if 'nodil' not in FLAGS:
            dil_branch()
            branch_proj("w_br_dil", 4, 1, first)
            first = False
        if 'nodn' not in FLAGS:
            dn_branch()
            branch_proj("w_br_dn", 8, 0, first)
            first = False
        if first:
            mbox[0] = k.sb("mergedT", [128, 8, TOK], BF16)
            k.memset(mbox[0].v(), 0.0)
        with k.phase():
            alloc_wblk(2)
            alloc_ysb(4)
            out_proj_post(l, "w_out", "norm_mix_post", lambda t, kc: mbox[0][:, kc, t * 128:t * 128 + tn(t)], 8, stream=(srcp, srcs))

    def cross_phase(l):
      with k.phase():
          alloc_wblk(3)
          alloc_ysb(3)
          Xm = [ysb_r.next() for i in range(2)]
          for i in range(2):
              k.dma(k.sp, Xm[i].v(), mem_p[i * 128:(i + 1) * 128, :])
          memT = k.sb("memT", [128, 8, 256], BF16)
          KmT = k.sb("KmT", [128, 4, 256], BF16)
          Vm = k.sb("Vm", [128, 2, 4, 129], BF16)
          k.memset(Vm.v(), 1.0)
          qT_r = Ring([k.sb("qT%d" % i, [128, 512], BF16) for i in range(2)])
          PT_r = Ring([k.sb("PT%d" % i, [128, 2, 512], BF16) for i in range(2)])
          om_r = Ring([k.sb("om%d" % i, [128, 512], BF16) for i in range(4)])
          stage_r = Ring([k.sb("stage%d" % i, [128, 2, 512]) for i in range(2)])
          KsT_r = Ring([k.sb("KsT%d" % i, [128, 8, 128], BF16) for i in range(2)])
          Vs_r = Ring([k.sb("Vs%d" % i, [128, 2, 4, 129], BF16) for i in range(2)])
          for b in Vs_r.bufs:
              k.memset(b.v(), 1.0)
          PTz_r = Ring([k.sb("PTz%d" % i, [128, 8, 64], BF16) for i in range(2)])
          for b in PTz_r.bufs:
              k.memset(b.v(), 0.0)
          qTs = k.sb("qTs", [128, 4, 64], BF16)

          def memkv(l):
              wm = W["w_mkv"].t
              for mt in range(2):
                  r = rstd_of(Xm[mt].v(), 128, 1.0 / D)
                  xn = xn_r.next()
                  k.actf(xn.v(), Xm[mt].v(), AF.Copy, scale=r)
                  ps = psr.next()
                  psb = ps.v().bitcast(BF16)
                  for kc in range(8):
                      k.tr(psb[:, kc * 128:(kc + 1) * 128], xn[:, kc * 128:(kc + 1) * 128], ident_b.v())
                  k.tt(memT[:, :, mt * 128:(mt + 1) * 128], psb.rr("p (c j) -> p c j", c=8),
                       gpre[:, 3, l, :].unsq(2).bc([128, 8, 128]), ALU.mult)
              for which in range(2):
                  wb = wblk.next()
                  load_w(wb.v(), V(wm[l, :, which * 512:(which + 1) * 512].rearrange("(c p) n -> p c n", p=128), [W["w_mkv"]]))
                  st = stage_r.next()
                  for mt in range(2):
                      ps = psr.next()
                      for kc in range(8):
                          k.mm(ps.v(), memT[:, kc, mt * 128:(mt + 1) * 128], wb[:, kc, :], start=(kc == 0), stop=(kc == 7))
                      k.cp(st[:, mt, :], ps.v(), eng="act")
                      if which == 1:
                          k.cp(Vm[:, mt, :, 0:128], ps.v().rr("p (h e) -> p h e", h=4))
                  dst = (p_mem_k if which == 0 else p_mem_v)
                  k.dma(k.sp, V(dst.t[l].rearrange("(m p) n -> p m n", p=128), [dst]), st.v())
                  if which == 0:
                      for hd in range(4):
                          ps = psr.next()
                          for kc in range(8):
                              k.mm(ps[:, 0:256], wb[:, kc, hd * 128:(hd + 1) * 128], memT[:, kc, :], start=(kc == 0), stop=(kc == 7))
                          k.cp(KmT[:, hd, :], ps[:, 0:256], eng="act")

          omT_all = k.sb("omT_all", [128, 4, TOK], BF16)

          def om_to_T(om, t, n):
              ps = psr.next()
              psb = ps.v().bitcast(BF16)
              for hd in range(4):
                  k.tr(psb[:, hd * 128:hd * 128 + n], om[0:n, hd * 128:(hd + 1) * 128], ident_b[0:n, 0:n])
              k.cp(omT_all[:, :, t * 128:t * 128 + n], psb.rr("p (c j) -> p c j", c=8)[:, 0:4, 0:n])

          def cross_attn(l):
              make_hT(1, l)
              wq = wblk.next()
              load_w(wq.v(), V(W["w_mq"].t[l].rearrange("(c p) n -> p c n", p=128), [W["w_mq"]]))
              sc = 128.0 ** -0.5
              for tb in range(5):
                  t0, t1 = tb * 4, min(tb * 4 + 4, NT)
                  ntok = min(512, TOK - tb * 512)
                  oms = [om_r.next() for _ in range(t0, t1)] if tb < 4 else []
                  for hd in range(4):
                      ps = psr.next()
                      for kc in range(8):
                          k.mm(ps[:, 0:ntok], wq[:, kc, hd * 128:(hd + 1) * 128], hT(kc, t0, t1), start=(kc == 0), stop=(kc == 7))
                      if tb == 4:
                          k.actf(qTs[:, hd, :], ps[:, 0:64], AF.Copy, scale=sc)
                          continue
                      qT = qT_r.next()
                      k.actf(qT.v(), ps.v(), AF.Copy, scale=sc)
                      PT = PT_r.next()
                      for mt in range(2):
                          ps2 = psr.next()
                          k.mm(ps2.v(), KmT[:, hd, mt * 128:(mt + 1) * 128], qT.v())
                          k.actf(PT[:, mt, :], ps2.v(), AF.Exp)
                      for ti in range(4):
                          ps3 = psr.next()
                          for mt in range(2):
                              k.mm(ps3[:, 0:129], PT[:, mt, ti * 128:(ti + 1) * 128], Vm[:, mt, hd, :], start=(mt == 0), stop=(mt == 1))
                          sm = small.next()
                          k.recip(sm[:, 0:1], ps3[:, 128:129])
                          k.ts(oms[ti][:, hd * 128:(hd + 1) * 128], ps3[:, 0:128], sm[:, 0:1], None, op0=ALU.mult)
                  if tb < 4:
                      for ti in range(4):
                          om_to_T(oms[ti], t0 + ti, 128)
              accs = PS[0:4]
              for s in range(NSS if 'nosamp' not in FLAGS else 0):
                  stK = stage_r.next()
                  k.dma(k.sp, stK.v(), V(cache_mem_k.t[l, s].rearrange("(m p) n -> p m n", p=128), [cache_mem_k]))
                  stV = stage_r.next()
                  k.dma(k.sp, stV.v(), V(cache_mem_v.t[l, s].rearrange("(m p) n -> p m n", p=128), [cache_mem_v]))
                  KsT = KsT_r.next()
                  for half in range(2):
                      ps = psr.next()
                      for j in range(4):
                          hd = half * 2 + j // 2
                          mt = j % 2
                          k.tr(ps[:, j * 128:(j + 1) * 128], stK[:, mt, hd * 128:(hd + 1) * 128], ident_f.v())
                      k.cp(KsT[:, half * 4:(half + 1) * 4, :], ps.v().rr("p (c j) -> p c j", c=4), eng="act")
                  Vs = Vs_r.next()
                  k.cp(Vs[:, :, :, 0:128], stV.v().rr("p m (h e) -> p m h e", h=4), eng="pool")
                  ps = psr.next()
                  for hd in range(4):
                      for mt in range(2):
                          c = hd * 2 + mt
                          k.mm(ps[:, c * 4:(c + 1) * 4], KsT[:, c, :], qTs[:, hd, s * 4:(s + 1) * 4])
                  PTz = PTz_r.next()
                  k.actf(PTz[:, :, s * 4:(s + 1) * 4], ps[:, 0:32].rr("p (c j) -> p c j", c=8), AF.Exp)
                  for hd in range(4):
                      for mt in range(2):
                          k.mm(accs[hd][0:64, 0:129], PTz[:, hd * 2 + mt, :], Vs[:, mt, hd, :],
                               start=(s == 0 and mt == 0), stop=(s == NSS - 1 and mt == 1))
                  k.memset(PTz[:, :, s * 4:(s + 1) * 4], 0.0)
              om = om_r.next()
              for hd in range(4):
                  sm = small.next()
                  k.recip(sm[0:64, 0:1], accs[hd][0:64, 128:129])
                  k.ts(om[0:64, hd * 128:(hd + 1) * 128], accs[hd][0:64, 0:128], sm[0:64, 0:1], None, op0=ALU.mult)
              om_to_T(om, 16, 64)
              out_proj_post(l, "w_mo", "norm_mem_post", lambda t, kc: omT_all[:, kc, t * 128:t * 128 + tn(t)], 4)
          memkv(l)
          if 'noca' not in FLAGS:
              cross_attn(l)

    def mlp(l):
      with k.phase():
          alloc_wblk(3)
          alloc_ysb(5)
          fT = k.sb("fT", [128, 32, 576], BF16)
          rtmp = Ring([k.sb("rtmp%d" % i, [128, 512]) for i in range(2)])
          make_hT(2, l)
          w1 = W["w_ff1"].t
          w2 = W["w_ff2"].t
          g = gpost.next()
          k.dma(k.sp, g.v(), V(W["norm_ffn_post"].t[l, :].partition_broadcast(128), [W["norm_ffn_post"]]))
          for tb in range(4):
              t0 = tb * 4
              t1 = t0 + 4 if tb < 3 else NT
              segs = [(0, 512, t0, t0 + 4)] if tb < 3 else [(0, 512, t0, t0 + 4), (512, 576, 16, 17)]
              for cg in range(8):
                  wb = wblk.next()
                  load_w(wb.v(), V(w1[l, :, cg * 512:(cg + 1) * 512].rearrange("(c p) n -> p c n", p=128), [W["w_ff1"]]))
                  for j in range(4):
                      fc = cg * 4 + j
                      for (ca, cb_, ta_, tb_) in segs:
                          w_ = cb_ - ca
                          ps = psr.next()
                          for kc in range(8):
                              k.mm(ps[:, 0:w_], wb[:, kc, j * 128:(j + 1) * 128], hT(kc, ta_, tb_), start=(kc == 0), stop=(kc == 7))
                          rt = rtmp.next()
                          k.actf(rt[:, 0:w_], ps[:, 0:w_], AF.Relu)
                          k.tt(fT[:, fc, ca:cb_], rt[:, 0:w_], rt[:, 0:w_], ALU.mult, eng="pool")
              ntile = t1 - t0
              ysbs = [ysb_r.next() for _ in range(ntile)]
              for half in range(2):
                  accs = PS[0:ntile]
                  for fg in range(4):
                      wb = wblk.next()
                      load_w(wb.v(), V(w2[l, fg * 1024:(fg + 1) * 1024, half * 512:(half + 1) * 512].rearrange("(c p) n -> p c n", p=128), [W["w_ff2"]]))
                      for ti in range(ntile):
                          n = tn(t0 + ti)
                          for j in range(8):
                              fc = fg * 8 + j
                              k.mm(accs[ti][0:n, :], fT[:, fc, ti * 128:ti * 128 + n], wb[:, j, :], start=(fc == 0), stop=(fc == 31))
                  for ti in range(ntile):
                      n = tn(t0 + ti)
                      k.cp(ysbs[ti][0:n, half * 512:(half + 1) * 512], accs[ti][0:n, :], eng="act")
              for ti in range(ntile):
                  t = t0 + ti
                  n = tn(t)
                  ysb = ysbs[ti]
                  r = rstd_of(ysb[0:n, :], n, 1.0 / D)
                  k.stt(ysb[0:n, :], ysb[0:n, :], r, g[0:n, :], ALU.mult, ALU.mult)
                  k.tt(X[t][0:n, :], X[t][0:n, :], ysb[0:n, :], ALU.add, eng="pool")

    for l in range(DEPTH if 'l1' not in FLAGS else 1):
        mixed = 'nomix' not in FLAGS
        if mixed:
            mixer(l)
        with k.phase():
            sp_, ss_ = (y_p, y_s) if (mixed or l > 0) else (x_p, x_s)
            for t in range(NT):
                X[t] = k.sb("X%d" % t, [128, D])
                k.dma(k.sp, X[t][0:tn(t), :], xrows(sp_, ss_, t))
            if 'nocross' not in FLAGS:
                cross_phase(l)
            if 'nomlp' not in FLAGS:
                mlp(l)
            for t in range(NT):
                k.dma(k.sp, xrows(y_p, y_s, t), X[t][0:tn(t), :])
    k.finish()
    es.close()
    return nc, k


_CACHE = {}


def core_inputs(inp, c, consts):
    f = lambda a: np.ascontiguousarray(np.asarray(a, dtype=np.float32))
    s0, s1 = c * NSS, (c + 1) * NSS
    m = {"x_p": f(inp["x_prompt"][c]), "x_s": f(inp["x_sample"][s0:s1]).reshape(NSS * TS, D),
         "mem_p": f(inp["mem_prompt"][c]),
         "cache_mem_k": f(inp["cache_mem_k"][:, s0:s1]).reshape(DEPTH, NSS, 256, 512),
         "cache_mem_v": f(inp["cache_mem_v"][:, s0:s1]).reshape(DEPTH, NSS, 256, 512)}
    for g in range(3):
        for kv in "kv":
            nm = "cache_win%d_%s" % (g + 1, kv)
            m[nm] = f(inp[nm][:, s0:s1]).reshape(DEPTH, NSS, DIL_W[g], 512)
    m["state_ssm_conv"] = f(inp["state_ssm_conv"][:, s0:s1])
    m["state_ssm"] = f(inp["state_ssm"][:, s0:s1])
    m["state_dn_conv"] = f(inp["state_dn_conv"][:, s0:s1])
    m["state_dn"] = f(inp["state_dn"][:, s0:s1])
    m.update(consts)
    return m


WNAMES = ["norm_mix_pre", "w_in", "w_br_dn", "w_br_dil", "w_br_ssm", "w_out", "norm_mix_post", "norm_mem_pre",
          "norm_mem_kv", "w_mq", "w_mkv", "w_mo", "norm_mem_post", "norm_ffn_pre", "w_ff1", "w_ff2", "norm_ffn_post",
          "ssm_conv_w", "ssm_conv_b", "ssm_a_log", "ssm_dt_bias", "ssm_d", "ssm_norm",
          "dn_conv_w", "dn_a_log", "dn_dt_bias", "dn_norm"]


def assemble(R, ncores):
    z = lambda *s: np.zeros(s, np.float32)

    def per_batch(name, shape):
        if name not in R[0]:
            return z(DEPTH, ncores, *shape)
        return np.stack([R[c][name].reshape((DEPTH,) + tuple(shape)) for c in range(ncores)], axis=1)

    def per_seq(name, shape):
        if name not in R[0]:
            return z(DEPTH, ncores * NSS, *shape)
        return np.concatenate([R[c][name].reshape((DEPTH, NSS) + tuple(shape)) for c in range(ncores)], axis=1)

    yp = np.stack([R[c]["y_p"] for c in range(ncores)])
    ys = np.concatenate([R[c]["y_s"].reshape(NSS, TS, D) for c in range(ncores)])
    outs = [yp, ys,
            per_batch("p_dn_conv", (3, 3072)), per_batch("p_dn", (8, 128, 128)),
            per_batch("p_ssm_conv", (3, 2048)), per_batch("p_ssm", (16, 64, 128)),
            per_batch("p_win1_k", (128, 4, 128)), per_batch("p_win1_v", (128, 4, 128)),
            per_batch("p_win2_k", (512, 4, 128)), per_batch("p_win2_v", (512, 4, 128)),
            per_batch("p_win3_k", (2048, 4, 128)), per_batch("p_win3_v", (2048, 4, 128)),
            per_batch("p_mem_k", (256, 4, 128)), per_batch("p_mem_v", (256, 4, 128)),
            per_seq("s_dn_conv", (3, 3072)), per_seq("s_dn", (8, 128, 128)),
            per_seq("s_ssm_conv", (3, 2048)), per_seq("s_ssm", (16, 64, 128)),
            per_seq("s_win1_k", (4, 4, 128)), per_seq("s_win1_v", (4, 4, 128)),
            per_seq("s_win2_k", (4, 4, 128)), per_seq("s_win2_v", (4, 4, 128)),
            per_seq("s_win3_k", (4, 4, 128)), per_seq("s_win3_v", (4, 4, 128))]
    return tuple(outs)


def kernel(**inp):
    if "nc" not in _CACHE:
        _CACHE["nc"] = build()
    nc, kb = _CACHE["nc"]
    f = lambda a: np.ascontiguousarray(np.asarray(a, dtype=np.float32))
    consts = host_consts()
    wts = {n: f(inp[n]) for n in WNAMES}
    in_maps = []
    for c in range(NCORES):
        m = core_inputs(inp, c, consts)
        m.update(wts)
        in_maps.append(m)
    res = run_bass_kernel_spmd(nc, in_maps, core_ids=list(range(NCORES)))
    return assemble(res.results, NCORES)
```
